# Optimizing a Trainium2 kernel written in Bass

```python
import math
import jax
import jax.numpy as jnp
from jax import lax
import numpy as np

D_MODEL = 1024
BATCH = 8
SEQ = 2048
DEPTH = 2
DEC_BATCH = 128
DEC_SEQ = 8
PAST_LEN = 8192
PAGE_SIZE = 128

SSM_GROUP = 16
SSM_GROUPS = D_MODEL // SSM_GROUP
SSM_STATE = 64
DT_MIN = 1e-3
DT_MAX = 1e-1
MLA_HEADS = 8
QK_NOPE = 128
QK_ROPE = 64
V_HEAD = 128
KV_LORA = 256
Q_LORA = 384
ROPE_BASE = 10000.0
Q_BLOCK = 128
MLA_SCALE = (QK_NOPE + QK_ROPE) ** -0.5
MEM_TOKENS = 256
MEM_HEADS = 4
MEM_HEAD_DIM = D_MODEL // MEM_HEADS
MEM_SCALE = MEM_HEAD_DIM ** -0.5
D_FF = 4 * D_MODEL
EPS = 1e-6

kernel_name = 'yoco_s5_mla_memory_decoder_step'


def rms_norm(x, g):
    xf = x.astype(jnp.float32)
    y = xf * lax.rsqrt(jnp.mean(xf * xf, axis=-1, keepdims=True) + EPS)
    return (y * g.astype(jnp.float32)).astype(x.dtype)


def rope(x, pos):
    half = x.shape[-1] // 2
    inv = ROPE_BASE ** (-jnp.arange(half, dtype=jnp.float32) / half)
    ang = pos.astype(jnp.float32)[:, None] * inv[None, :]
    shape = (ang.shape[0],) + (1,) * (x.ndim - 3) + (half,)
    cos = jnp.cos(ang).reshape(shape)
    sin = jnp.sin(ang).reshape(shape)
    xf = x.astype(jnp.float32)
    x1, x2 = xf[..., :half], xf[..., half:]
    return jnp.concatenate([x1 * cos - x2 * sin, x2 * cos + x1 * sin], axis=-1).astype(x.dtype)


def _complex_affine_combine(e1, e2):
    a1r, a1i, b1r, b1i = e1
    a2r, a2i, b2r, b2i = e2
    return (a2r * a1r - a2i * a1i,
            a2r * a1i + a2i * a1r,
            a2r * b1r - a2i * b1i + b2r,
            a2r * b1i + a2i * b1r + b2i)


def s5_mixer(u, a_re, a_im, log_dt, b_re, b_im, c_re, c_im, d_skip, w_glu, h0_re, h0_im):
    f32 = jnp.float32
    bsz, t, dm = u.shape
    ar = a_re.astype(f32)
    ai = a_im.astype(f32)
    dt = jnp.exp(log_dt.astype(f32))[:, None]
    mag = jnp.exp(ar * dt)
    abar_re = mag * jnp.cos(ai * dt)
    abar_im = mag * jnp.sin(ai * dt)
    den = ar * ar + ai * ai
    nr = abar_re - 1.0
    ni = abar_im
    coef_re = (nr * ar + ni * ai) / den
    coef_im = (ni * ar - nr * ai) / den
    br = b_re.astype(f32)
    bi = b_im.astype(f32)
    bbar_re = coef_re[..., None] * br - coef_im[..., None] * bi
    bbar_im = coef_re[..., None] * bi + coef_im[..., None] * br
    ug = u.astype(f32).reshape(bsz, t, SSM_GROUPS, SSM_GROUP)
    bu_re = jnp.einsum('gnk,btgk->btgn', bbar_re, ug)
    bu_im = jnp.einsum('gnk,btgk->btgn', bbar_im, ug)
    h0r = h0_re.astype(f32)
    h0i = h0_im.astype(f32)
    bu_re = bu_re.at[:, 0].add(abar_re * h0r - abar_im * h0i)
    bu_im = bu_im.at[:, 0].add(abar_re * h0i + abar_im * h0r)
    a_seq_re = jnp.broadcast_to(abar_re, bu_re.shape)
    a_seq_im = jnp.broadcast_to(abar_im, bu_im.shape)
    _, _, h_re, h_im = lax.associative_scan(
        _complex_affine_combine, (a_seq_re, a_seq_im, bu_re, bu_im), axis=1)
    cr = c_re.astype(f32)
    ci = c_im.astype(f32)
    y = jnp.einsum('gkn,btgn->btgk', cr, h_re) - jnp.einsum('gkn,btgn->btgk', ci, h_im)
    y = y.reshape(bsz, t, dm) + d_skip.astype(f32) * u.astype(f32)
    g = jax.nn.gelu(y, approximate=False).astype(u.dtype)
    z = g @ w_glu
    out = z[..., :dm] * jax.nn.sigmoid(z[..., dm:])
    return out, h_re[:, -1], h_im[:, -1]


def mem_kv(mem, g, w_k, w_v):
    b, m, _ = mem.shape
    mn = rms_norm(mem, g)
    k = (mn @ w_k).reshape(b, m, MEM_HEADS, MEM_HEAD_DIM)
    v = (mn @ w_v).reshape(b, m, MEM_HEADS, MEM_HEAD_DIM)
    return k, v


def mem_attend(xn, k, v, w_q, w_o):
    b, t, dm = xn.shape
    q = (xn @ w_q).reshape(b, t, MEM_HEADS, MEM_HEAD_DIM)
    s = jnp.einsum('bthd,bmhd->bhtm', q, k).astype(jnp.float32) * MEM_SCALE
    p = jax.nn.softmax(s, axis=-1).astype(v.dtype)
    o = jnp.einsum('bhtm,bmhd->bthd', p, v).reshape(b, t, dm)
    return o @ w_o


def shared_latent(h, kv_in_norm, w_dkv, kv_latent_norm, w_kr, pos):
    hn = rms_norm(h, kv_in_norm)
    ckv = rms_norm(hn @ w_dkv, kv_latent_norm)
    kr = rope(hn @ w_kr, pos)
    return ckv, kr


def latent_attend(q_lat, q_rope, q_pos, ckv, krope, k_pos):
    s = jnp.einsum('bthr,bsr->bhts', q_lat, ckv) + jnp.einsum('bthd,bsd->bhts', q_rope, krope)
    s = s.astype(jnp.float32) * MLA_SCALE
    s = jnp.where(k_pos[None, :] <= q_pos[:, None], s, -jnp.inf)
    p = jax.nn.softmax(s, axis=-1).astype(ckv.dtype)
    return jnp.einsum('bhts,bsr->bthr', p, ckv)


def mla_mixer(xn, q_pos, ckv, krope, k_pos, w_dq, q_norm, w_uq, w_uk, w_uv, w_o):
    b, t, _ = xn.shape
    cq = rms_norm(xn @ w_dq, q_norm)
    q = jnp.einsum('btr,rhd->bthd', cq, w_uq)
    q_rope = rope(q[..., QK_NOPE:], q_pos)
    q_lat = jnp.einsum('bthd,rhd->bthr', q[..., :QK_NOPE], w_uk)
    if t % Q_BLOCK == 0:
        nb = t // Q_BLOCK
        blk = lambda a: a.reshape((b, nb, Q_BLOCK) + a.shape[2:]).swapaxes(0, 1)
        o = lax.map(lambda args: latent_attend(args[0], args[1], args[2], ckv, krope, k_pos),
                    (blk(q_lat), blk(q_rope), q_pos.reshape(nb, Q_BLOCK)))
        o_lat = o.swapaxes(0, 1).reshape(b, t, MLA_HEADS, KV_LORA)
    else:
        o_lat = latent_attend(q_lat, q_rope, q_pos, ckv, krope, k_pos)
    o = jnp.einsum('bthr,rhv->bthv', o_lat, w_uv).reshape(b, t, MLA_HEADS * V_HEAD)
    return o @ w_o


def sq_relu_mlp(xn, w_up, w_down):
    hdn = jax.nn.relu(xn @ w_up)
    return (hdn * hdn) @ w_down


def setup_inputs(seed: int = 0) -> dict:
    key = jax.random.key(seed)
    ks = iter(jax.random.split(key, 64))
    f32 = jnp.float32
    n_a = DEPTH // 2
    n_b = DEPTH - n_a
    n_pages = PAST_LEN // PAGE_SIZE
    used = DEC_BATCH * n_pages
    n_phys = used + max(1, used // 4)
    d = D_MODEL
    G, N, K = SSM_GROUPS, SSM_STATE, SSM_GROUP

    def nrm(shape, scale):
        return jax.random.normal(next(ks), shape, f32) * scale

    def gain(shape):
        return 1.0 + nrm(shape, 0.02)

    inp = {}
    inp['x_prompt'] = nrm((BATCH, SEQ, d), 1.0)
    inp['x_sample'] = nrm((DEC_BATCH, DEC_SEQ, d), 1.0)
    inp['cache_ssm_re'] = nrm((n_a, DEC_BATCH, G, N), 0.5)
    inp['cache_ssm_im'] = nrm((n_a, DEC_BATCH, G, N), 0.5)
    inp['cache_kv_latent'] = nrm((n_phys, PAGE_SIZE, KV_LORA), 1.0)
    inp['cache_k_rope'] = nrm((n_phys, PAGE_SIZE, QK_ROPE), 1.0)
    inp['cache_mem_k'] = nrm((DEPTH, DEC_BATCH, MEM_TOKENS, MEM_HEADS, MEM_HEAD_DIM), 1.0)
    inp['cache_mem_v'] = nrm((DEPTH, DEC_BATCH, MEM_TOKENS, MEM_HEADS, MEM_HEAD_DIM), 1.0)
    inp['page_table'] = jax.random.permutation(next(ks), n_phys)[:used].reshape(DEC_BATCH, n_pages).astype(jnp.int32)
    inp['mem_prompt'] = nrm((BATCH, MEM_TOKENS, d), 1.0)
    inp['norm_mix_pre'] = gain((DEPTH, d))
    inp['norm_mix_post'] = gain((DEPTH, d))
    inp['norm_mem_pre'] = gain((DEPTH, d))
    inp['norm_mem_post'] = gain((DEPTH, d))
    inp['norm_mlp_pre'] = gain((DEPTH, d))
    inp['norm_mlp_post'] = gain((DEPTH, d))
    inp['mem_in_norm'] = gain((DEPTH, d))
    inp['w_mem_q'] = nrm((DEPTH, d, MEM_HEADS * MEM_HEAD_DIM), d ** -0.5)
    inp['w_mem_k'] = nrm((DEPTH, d, MEM_HEADS * MEM_HEAD_DIM), d ** -0.5)
    inp['w_mem_v'] = nrm((DEPTH, d, MEM_HEADS * MEM_HEAD_DIM), d ** -0.5)
    inp['w_mem_o'] = nrm((DEPTH, MEM_HEADS * MEM_HEAD_DIM, d), (MEM_HEADS * MEM_HEAD_DIM) ** -0.5)
    inp['w_mlp_up'] = nrm((DEPTH, d, D_FF), d ** -0.5)
    inp['w_mlp_down'] = nrm((DEPTH, D_FF, d), D_FF ** -0.5)
    n_idx = jnp.arange(N, dtype=f32)
    inp['ssm_a_re'] = -0.5 + nrm((n_a, G, N), 0.01)
    inp['ssm_a_im'] = math.pi * n_idx + nrm((n_a, G, N), 0.01)
    inp['ssm_log_dt'] = jax.random.uniform(next(ks), (n_a, G), f32, math.log(DT_MIN), math.log(DT_MAX))
    inp['ssm_b_re'] = nrm((n_a, G, N, K), (2 * K) ** -0.5)
    inp['ssm_b_im'] = nrm((n_a, G, N, K), (2 * K) ** -0.5)
    inp['ssm_c_re'] = nrm((n_a, G, K, N), (2 * N) ** -0.5)
    inp['ssm_c_im'] = nrm((n_a, G, K, N), (2 * N) ** -0.5)
    inp['ssm_d'] = nrm((n_a, d), 1.0)
    inp['w_glu'] = nrm((n_a, d, 2 * d), d ** -0.5)
    inp['kv_in_norm'] = gain((d,))
    inp['w_dkv'] = nrm((d, KV_LORA), d ** -0.5)
    inp['kv_latent_norm'] = gain((KV_LORA,))
    inp['w_kr'] = nrm((d, QK_ROPE), d ** -0.5)
    inp['w_uk'] = nrm((KV_LORA, MLA_HEADS, QK_NOPE), KV_LORA ** -0.5)
    inp['w_uv'] = nrm((KV_LORA, MLA_HEADS, V_HEAD), KV_LORA ** -0.5)
    inp['w_dq'] = nrm((n_b, d, Q_LORA), d ** -0.5)
    inp['q_norm'] = gain((n_b, Q_LORA))
    inp['w_uq'] = nrm((n_b, Q_LORA, MLA_HEADS, QK_NOPE + QK_ROPE), Q_LORA ** -0.5)
    inp['w_o'] = nrm((n_b, MLA_HEADS * V_HEAD, d), (MLA_HEADS * V_HEAD) ** -0.5)
    return inp


def reference(x_prompt, x_sample, cache_ssm_re, cache_ssm_im, cache_kv_latent, cache_k_rope,
              cache_mem_k, cache_mem_v, page_table, mem_prompt,
              norm_mix_pre, norm_mix_post, norm_mem_pre, norm_mem_post, norm_mlp_pre, norm_mlp_post,
              mem_in_norm, w_mem_q, w_mem_k, w_mem_v, w_mem_o, w_mlp_up, w_mlp_down,
              ssm_a_re, ssm_a_im, ssm_log_dt, ssm_b_re, ssm_b_im, ssm_c_re, ssm_c_im, ssm_d, w_glu,
              kv_in_norm, w_dkv, kv_latent_norm, w_kr, w_uk, w_uv,
              w_dq, q_norm, w_uq, w_o):
    n_a = DEPTH // 2
    bp, sp, _ = x_prompt.shape
    bs, ts, _ = x_sample.shape
    past_len = page_table.shape[1] * PAGE_SIZE
    pos_p = jnp.arange(sp, dtype=jnp.int32)
    pos_s = past_len + jnp.arange(ts, dtype=jnp.int32)
    k_pos_s = jnp.arange(past_len + ts, dtype=jnp.int32)

    hp, hs = x_prompt, x_sample
    ssm_p_re, ssm_p_im, ssm_s_re, ssm_s_im = [], [], [], []
    memk_p, memv_p = [], []
    for i in range(DEPTH):
        if i < n_a:
            ssm_w = (ssm_a_re[i], ssm_a_im[i], ssm_log_dt[i], ssm_b_re[i], ssm_b_im[i],
                     ssm_c_re[i], ssm_c_im[i], ssm_d[i], w_glu[i])
            h0 = jnp.zeros((bp, SSM_GROUPS, SSM_STATE), jnp.float32)
            yp, fpr, fpi = s5_mixer(rms_norm(hp, norm_mix_pre[i]), *ssm_w, h0, h0)
            ys, fsr, fsi = s5_mixer(rms_norm(hs, norm_mix_pre[i]), *ssm_w, cache_ssm_re[i], cache_ssm_im[i])
            ssm_p_re.append(fpr)
            ssm_p_im.append(fpi)
            ssm_s_re.append(fsr)
            ssm_s_im.append(fsi)
        else:
            if i == n_a:
                ckv_p, kr_p = shared_latent(hp, kv_in_norm, w_dkv, kv_latent_norm, w_kr, pos_p)
                ckv_s_new, kr_s_new = shared_latent(hs, kv_in_norm, w_dkv, kv_latent_norm, w_kr, pos_s)
                ckv_s = jnp.concatenate(
                    [cache_kv_latent[page_table].reshape(bs, past_len, KV_LORA), ckv_s_new], axis=1)
                kr_s = jnp.concatenate(
                    [cache_k_rope[page_table].reshape(bs, past_len, QK_ROPE), kr_s_new], axis=1)
            j = i - n_a
            yp = mla_mixer(rms_norm(hp, norm_mix_pre[i]), pos_p, ckv_p, kr_p, pos_p,
                           w_dq[j], q_norm[j], w_uq[j], w_uk, w_uv, w_o[j])
            ys = mla_mixer(rms_norm(hs, norm_mix_pre[i]), pos_s, ckv_s, kr_s, k_pos_s,
                           w_dq[j], q_norm[j], w_uq[j], w_uk, w_uv, w_o[j])
        hp = hp + rms_norm(yp, norm_mix_post[i])
        hs = hs + rms_norm(ys, norm_mix_post[i])
        mk, mv = mem_kv(mem_prompt, mem_in_norm[i], w_mem_k[i], w_mem_v[i])
        memk_p.append(mk)
        memv_p.append(mv)
        hp = hp + rms_norm(mem_attend(rms_norm(hp, norm_mem_pre[i]), mk, mv, w_mem_q[i], w_mem_o[i]), norm_mem_post[i])
        hs = hs + rms_norm(mem_attend(rms_norm(hs, norm_mem_pre[i]), cache_mem_k[i], cache_mem_v[i],
                                      w_mem_q[i], w_mem_o[i]), norm_mem_post[i])
        hp = hp + rms_norm(sq_relu_mlp(rms_norm(hp, norm_mlp_pre[i]), w_mlp_up[i], w_mlp_down[i]), norm_mlp_post[i])
        hs = hs + rms_norm(sq_relu_mlp(rms_norm(hs, norm_mlp_pre[i]), w_mlp_up[i], w_mlp_down[i]), norm_mlp_post[i])

    ssm_re_prompt = jnp.stack(ssm_p_re)
    ssm_im_prompt = jnp.stack(ssm_p_im)
    ssm_re_sample = jnp.stack(ssm_s_re)
    ssm_im_sample = jnp.stack(ssm_s_im)
    mem_k_prompt = jnp.stack(memk_p)
    mem_v_prompt = jnp.stack(memv_p)
    return (hp, hs, ssm_re_prompt, ssm_im_prompt, ssm_re_sample, ssm_im_sample,
            ckv_p, kr_p, ckv_s_new, kr_s_new, mem_k_prompt, mem_v_prompt)
```

```python
import math
from contextlib import ExitStack
import numpy as np
import concourse.bass as bass
import concourse.mybir as mybir
from concourse.bass_utils import run_bass_kernel_spmd

F32 = mybir.dt.float32
BF16 = mybir.dt.bfloat16
I32 = mybir.dt.int32
AF = mybir.ActivationFunctionType
ALU = mybir.AluOpType
AX = mybir.AxisListType

NCORES = 8
D = 1024
SEQ = 2048
NPT = 16
NG = 64
NS = 64
EPS = 1e-6
TWO_PI = 2.0 * math.pi
NEG = -30000.0
MEM_SCALE = 256 ** -0.5
MLA_SCALE = 192 ** -0.5
PAST = 8192
N_PHYS = 10240

GROUPS = [[("P", i) for i in range(0, 6)],
          [("P", i) for i in range(6, 12)],
          [("P", i) for i in range(12, 16)] + [("S", 0)]]
MAXT = 6
ARENA = 19456


class Prog:
    ENG = ("pe", "act", "dve", "pool", "sp")

    def __init__(self, nc, stack, n_dma_sems=48, same_engine_sync=True):
        self.nc = nc
        self.h = {"pe": nc.tensor, "act": nc.scalar, "dve": nc.vector,
                  "pool": nc.gpsimd, "sp": nc.sync}
        self.stream = {e: [] for e in self.ENG}
        self.sem = {e: stack.enter_context(nc.semaphore("s_" + e)) for e in self.ENG}
        self.cnt = {e: 0 for e in self.ENG}
        self.dsem = [stack.enter_context(nc.semaphore("d%d" % i)) for i in range(n_dma_sems)]
        self.dcnt = [0] * n_dma_sems
        self.dnext = 0
        self.waited = {e: {} for e in self.ENG}
        self.buf = {}
        self.same = same_engine_sync
        self.out_tokens = []
        self.nops = 0

    def _deps(self, eng, reads, writes):
        toks = []
        for k in reads:
            st = self.buf.get(k)
            if st and st[0] is not None:
                toks.append(st[0])
        for k in writes:
            st = self.buf.get(k)
            if st:
                if st[0] is not None:
                    toks.append(st[0])
                toks.extend(st[1])
        need = {}
        for (sem, val, src) in toks:
            if src == eng and (eng == "pe" or not self.same):
                continue
            key = id(sem)
            if self.waited[eng].get(key, 0) >= val:
                continue
            if key not in need or need[key][1] < val:
                need[key] = (sem, val)
        for key, (sem, val) in need.items():
            self.waited[eng][key] = val
        return list(need.values())

    def _commit(self, tok, reads, writes):
        for k in reads:
            st = self.buf.setdefault(k, [None, []])
            st[1].append(tok)
        for k in writes:
            self.buf[k] = [tok, []]

    def op(self, eng, fn, reads=(), writes=()):
        waits = self._deps(eng, reads, writes)
        sem = self.sem[eng]
        self.cnt[eng] += 1
        val = self.cnt[eng]
        self.nops += 1

        def emit(h, waits=waits, fn=fn, sem=sem):
            for (s, v) in waits:
                h.wait_ge(s, v)
            fn(h).then_inc(sem, 1)
        self.stream[eng].append(emit)
        tok = (sem, val, eng)
        self._commit(tok, reads, writes)
        return tok

    def dma(self, q, fn, reads=(), writes=(), is_output=False):
        waits = self._deps(q, reads, writes)
        i = self.dnext
        self.dnext = (self.dnext + 1) % len(self.dsem)
        sem = self.dsem[i]
        prev = self.dcnt[i]
        if prev > 0 and self.waited[q].get(id(sem), 0) < prev:
            waits.append((sem, prev))
            self.waited[q][id(sem)] = prev
        self.dcnt[i] += 16
        val = self.dcnt[i]
        self.nops += 1

        def emit(h, waits=waits, fn=fn, sem=sem):
            for (s, v) in waits:
                h.wait_ge(s, v)
            fn(h).then_inc(sem, 16)
        self.stream[q].append(emit)
        tok = (sem, val, "dma")
        self._commit(tok, reads, writes)
        if is_output:
            self.out_tokens.append(tok)
        return tok

    def barrier(self):
        targets = [(self.sem[e], self.cnt[e]) for e in self.ENG if self.cnt[e] > 0]
        targets += [(self.dsem[i], self.dcnt[i]) for i in range(len(self.dsem)) if self.dcnt[i] > 0]
        for e in self.ENG:
            ws = []
            for (s, v) in targets:
                if s is self.sem[e]:
                    continue
                if self.waited[e].get(id(s), 0) >= v:
                    continue
                self.waited[e][id(s)] = v
                ws.append((s, v))

            def emit(h, ws=ws):
                for (s, v) in ws:
                    h.wait_ge(s, v)
            self.stream[e].append(emit)
        self.buf = {}

    def finish(self):
        need = {}
        for (sem, val, _) in self.out_tokens:
            k = id(sem)
            if k not in need or need[k][1] < val:
                need[k] = (sem, val)
        waits = list(need.values())

        def emit(h, waits=waits):
            for (s, v) in waits:
                h.wait_ge(s, v)
        self.stream["sp"].append(emit)
        streams = self.stream
        with self.nc.Block() as block:
            @block.tensor
            def _(e):
                for f in streams["pe"]:
                    f(e)

            @block.scalar
            def _(e):
                for f in streams["act"]:
                    f(e)

            @block.vector
            def _(e):
                for f in streams["dve"]:
                    f(e)

            @block.gpsimd
            def _(e):
                for f in streams["pool"]:
                    f(e)

            @block.sync
            def _(e):
                for f in streams["sp"]:
                    f(e)


WEIGHT_SPECS = [
    ("norm_mix_pre", [2, D]), ("norm_mix_post", [2, D]), ("norm_mem_pre", [2, D]),
    ("norm_mem_post", [2, D]), ("norm_mlp_pre", [2, D]), ("norm_mlp_post", [2, D]),
    ("mem_in_norm", [2, D]), ("w_mem_q", [2, D, D]), ("w_mem_k", [2, D, D]),
    ("w_mem_v", [2, D, D]), ("w_mem_o", [2, D, D]), ("w_mlp_up", [2, D, 4 * D]),
    ("w_mlp_down", [2, 4 * D, D]), ("ssm_a_re", [1, 64, 64]), ("ssm_a_im", [1, 64, 64]),
    ("ssm_log_dt", [1, 64]), ("ssm_b_re", [1, 64, 64, 16]), ("ssm_b_im", [1, 64, 64, 16]),
    ("ssm_c_re", [1, 64, 16, 64]), ("ssm_c_im", [1, 64, 16, 64]), ("ssm_d", [1, D]),
    ("w_glu", [1, D, 2 * D]), ("kv_in_norm", [D]), ("w_dkv", [D, 256]),
    ("kv_latent_norm", [256]), ("w_kr", [D, 64]), ("w_uk", [256, 8, 128]),
    ("w_uv", [256, 8, 128]), ("w_dq", [1, D, 384]), ("q_norm", [1, 384]),
    ("w_uq", [1, 384, 8, 192]), ("w_o", [1, D, D]),
]

IN_SPECS = [
    ("xp", [SEQ, D], F32), ("xs", [128, D], F32), ("ssr", [16, 4096], F32), ("ssi", [16, 4096], F32),
    ("ckvc", [N_PHYS * 16, 2048], F32), ("krc", [N_PHYS * 16, 512], F32),
    ("memk", [2, 16, 256, D], F32), ("memv", [2, 16, 256, D], F32), ("pt", [16, 64], I32),
    ("memp", [256, D], F32),
]

OUT_SPECS = [
    ("y_p", [SEQ, D]), ("y_s", [128, D]), ("ssm_re_p", [1, 4096]), ("ssm_im_p", [1, 4096]),
    ("ssm_re_s", [16, 4096]), ("ssm_im_s", [16, 4096]), ("kvl_p", [SEQ, 256]), ("kr_p", [SEQ, 64]),
    ("kvl_s", [128, 256]), ("kr_s", [128, 64]), ("memk_p", [2, 256, D]), ("memv_p", [2, 256, D]),
]


def build(stage=99, dbg_shape=None):
    nc = bass.Bass("TRN2", target_bir_lowering=False)
    I = {}
    for name, shape, dt in IN_SPECS:
        I[name] = nc.dram_tensor(name, shape, dt, kind="ExternalInput").ap()
    for name, shape in WEIGHT_SPECS:
        I[name] = nc.dram_tensor(name, shape, F32, kind="ExternalInput").ap()
    O = {}
    for name, shape in OUT_SPECS:
        O[name] = nc.dram_tensor(name, shape, F32, kind="ExternalOutput").ap()
    if dbg_shape is not None:
        O["dbg"] = nc.dram_tensor("dbg", dbg_shape, F32, kind="ExternalOutput").ap()

    with ExitStack() as st:
        P = Prog(nc, st)
        K = Kern(nc, st, P, I, O, stage)
        K.run()
        P.finish()
    return nc


class Kern:
    def __init__(self, nc, st, P, I, O, stage):
        self.nc, self.st, self.P, self.I, self.O, self.stage = nc, st, P, I, O, stage
        self.uid = 0
        sb = self.sb
        self.h = sb("h", [128, MAXT, D], F32)
        self.xnT = sb("xnT", [128, 8, MAXT * 128], BF16)
        self.ident = sb("ident", [128, 128], BF16)
        self.identf = sb("identf", [128, 128], F32)
        self.gbc = [sb("gbc%d" % i, [128, D], F32) for i in range(2)]
        self.gbc_i = 0
        self.wslot = [sb("wslot%d" % i, [128, 8, 1024], BF16) for i in range(3)]
        self.ws_i = 0
        self.small = sb("small", [128, 64], F32)
        self.small_i = 0
        self.junk = sb("junk", [128, D], BF16)
        self.xn = sb("xn", [128, D], F32)
        self.iota_g = sb("iota_g", [128, MAXT * 128], F32)
        self.halfpi = sb("halfpi", [128, 1], F32)
        self.s5_carry = sb("s5_carry", [128, 32, 2], F32)
        self.arena = sb("arena", [128, ARENA], F32)
        self.KT = sb("KT", [128, 3, SEQ], BF16)
        self.Vp = sb("Vp", [128, NPT, 256], BF16)
        self.cm = sb("cm", [128, 128], F32)
        self.invf = sb("invf", [128, 2], F32)
        self.pp = [st.enter_context(nc.psum_tensor("pp%d" % i, [128, 1024], F32)) for i in range(4)]
        self.ps = [self.pp[i // 2][:, (i % 2) * 512:(i % 2 + 1) * 512] for i in range(8)]

    def sb(self, name, shape, dt):
        return self.st.enter_context(self.nc.sbuf_tensor(name, shape, dt))

    def key(self, base):
        self.uid += 1
        return "%s#%d" % (base, self.uid)

    def scal(self):
        i = self.small_i
        self.small_i = (self.small_i + 1) % 64
        return self.small[:, i:i + 1], "small%d" % i

    def load_gain(self, vec_ap):
        i = self.gbc_i
        self.gbc_i = (self.gbc_i + 1) % len(self.gbc)
        t = self.gbc[i]
        k = "gbc%d" % i
        src = vec_ap.rearrange("(o n) -> o n", o=1).to_broadcast([128, D])
        self.P.dma("sp", lambda h: h.dma_start(out=t[:], in_=src), writes=[k])
        return t, k

    def load_w(self, src_ap, ncols, nk=8):
        i = self.ws_i
        self.ws_i = (self.ws_i + 1) % len(self.wslot)
        t = self.wslot[i]
        k = "wslot%d" % i
        view = t[:].rearrange("p a b -> p (a b)")[:, 0:nk * ncols].rearrange("p (a b) -> p a b", b=ncols)
        src = src_ap.rearrange("(k p) n -> p k n", p=128)
        self.P.dma("pool", lambda h: h.dma_start(out=view, in_=src), writes=[k])
        return view, k

    def rms_stats(self, src, skey, n):
        P = self.P
        ssq, k1 = self.scal()
        rs, k2 = self.scal()
        P.op("act", lambda h: h.activation(out=self.junk[:, 0:n], in_=src, func=AF.Square, accum_out=ssq),
             reads=skey, writes=["junk", k1])
        P.op("dve", lambda h: h.tensor_scalar(out=rs, in0=ssq, scalar1=1.0 / n, scalar2=EPS, op0=ALU.mult, op1=ALU.add),
             reads=[k1], writes=[k2])
        P.op("act", lambda h: h.activation(out=rs, in_=rs, func=AF.Sqrt), reads=[k2], writes=[k2])
        P.op("dve", lambda h: h.reciprocal(out=rs, in_=rs), reads=[k2], writes=[k2])
        return rs, k2

    def norm_to_xnT(self, li, gain, gkey, col0):
        P = self.P
        hk = "h%d" % li
        rs, rk = self.rms_stats(self.h[:, li, :], [hk], D)
        P.op("dve", lambda h: h.scalar_tensor_tensor(out=self.xn[:], in0=self.h[:, li, :], scalar=rs, in1=gain[:],
                                                      op0=ALU.mult, op1=ALU.mult),
             reads=[hk, rk, gkey], writes=["xn"])
        self.transpose_to(self.xn, "xn", self.xnT, "xnT", col0)

    def transpose_to(self, src, skey, dstT, dkey, col0, nk=8, banks=(0, 1)):
        P = self.P
        for half in range((nk + 3) // 4):
            b = self.ps[banks[half % len(banks)]]
            bk = "ps%d" % banks[half % len(banks)]
            kk = range(half * 4, min(nk, half * 4 + 4))
            for k in kk:
                P.op("pe", lambda h, k=k, b=b: h.transpose(out=b[:, (k % 4) * 128:(k % 4 + 1) * 128],
                                                           in_=src[:, k * 128:(k + 1) * 128], identity=self.identf[:]),
                     reads=[skey, "identf"], writes=[bk])
            n = len(kk)
            eng = "act" if half % 2 == 0 else "dve"
            outv = dstT[:, half * 4:half * 4 + n, col0:col0 + 128]
            inv = b[:, 0:n * 128].rearrange("p (a b) -> p a b", b=128)
            if eng == "act":
                P.op("act", lambda h, outv=outv, inv=inv: h.copy(out=outv, in_=inv), reads=[bk],
                     writes=["%s.%d" % (dkey, k) for k in kk])
            else:
                P.op("dve", lambda h, outv=outv, inv=inv: h.tensor_copy(out=outv, in_=inv), reads=[bk],
                     writes=["%s.%d" % (dkey, k) for k in kk])

    def post_norm_add(self, li, src, skey, gain, gkey):
        P = self.P
        hk = "h%d" % li
        rs, rk = self.rms_stats(src, skey, D)
        P.op("dve", lambda h: h.scalar_tensor_tensor(out=self.xn[:], in0=src, scalar=rs, in1=gain[:],
                                                      op0=ALU.mult, op1=ALU.mult),
             reads=list(skey) + [rk, gkey], writes=["xn"])
        P.op("pool", lambda h: h.tensor_tensor(out=self.h[:, li, :], in0=self.h[:, li, :], in1=self.xn[:], op=ALU.add),
             reads=["xn", hk], writes=[hk])

    def setup_consts(self):
        P = self.P
        P.op("pool", lambda h: h.memset(self.identf[:], 0.0), writes=["identf"])
        P.op("pool", lambda h: h.affine_select(out=self.identf[:], in_=self.identf[:], pattern=[[-1, 128]],
                                               compare_op=ALU.not_equal, fill=1.0, base=0, channel_multiplier=1),
             reads=["identf"], writes=["identf"])
        P.op("dve", lambda h: h.tensor_copy(out=self.ident[:], in_=self.identf[:]), reads=["identf"], writes=["ident"])
        P.op("pool", lambda h: h.memset(self.halfpi[:], math.pi / 2), writes=["halfpi"])
        P.op("pool", lambda h: h.memset(self.cm[:], 0.0), writes=["cm"])
        P.op("pool", lambda h: h.affine_select(out=self.cm[:], in_=self.cm[:], pattern=[[-1, 128]], compare_op=ALU.is_ge, fill=NEG,
                                               base=0, channel_multiplier=1), reads=["cm"], writes=["cm"])
        iv = self.invf[:, 0:2].bitcast(I32)
        P.op("pool", lambda h: h.iota(iv[:, 0:1], pattern=[[0, 1]], base=0, channel_multiplier=1), writes=["invf"])
        P.op("dve", lambda h: h.tensor_single_scalar(out=iv[:, 1:2], in_=iv[:, 0:1], scalar=31, op=ALU.bitwise_and), reads=["invf"], writes=["invf"])
        P.op("dve", lambda h: h.tensor_copy(out=self.invf[:, 0:1], in_=iv[:, 1:2]), reads=["invf"], writes=["invf"])
        P.op("act", lambda h: h.activation(out=self.invf[:, 0:1], in_=self.invf[:, 0:1], func=AF.Exp, scale=-math.log(10000.0) / 32.0), reads=["invf"], writes=["invf"])
        P.op("dve", lambda h: h.tensor_scalar(out=self.invf[:, 0:1], in0=self.invf[:, 0:1], scalar1=1.0 / TWO_PI, scalar2=None, op0=ALU.mult), reads=["invf"], writes=["invf"])

    def run(self):
        P = self.P
        self.setup_consts()
        import os
        only = os.environ.get('KGROUPS')
        for gi, grp in enumerate(GROUPS):
            if only is not None and str(gi) not in only:
                continue
            self.grp = grp
            self.gi = gi
            self.ntile = len(grp)
            self.tok = 128 * len(grp)
            self.load_group()
            self.layer(0)
            if self.stage >= 4:
                self.layer(1)
            self.store_group()
        if "dbg" in self.O:
            pass

    def load_group(self):
        P = self.P
        ptiles = [idx for (kind, idx) in self.grp if kind == "P"]
        tp = 128 * len(ptiles)
        P.op("pool", lambda h: h.iota(self.iota_g[:, 0:tp], pattern=[[1, tp]], base=ptiles[0] * 128, channel_multiplier=0,
                                      allow_small_or_imprecise_dtypes=True), reads=["iota_g"], writes=["iota_g"])
        if len(ptiles) < len(self.grp):
            P.op("pool", lambda h: h.iota(self.iota_g[:, tp:tp + 128], pattern=[[0, 16], [1, 8]], base=0, channel_multiplier=0,
                                          allow_small_or_imprecise_dtypes=True), reads=["iota_g"], writes=["iota_g"])
        for li, (kind, idx) in enumerate(self.grp):
            src = self.I["xp"][idx * 128:(idx + 1) * 128, :] if kind == "P" else self.I["xs"][:, :]
            P.dma("sp", lambda h, li=li, src=src: h.dma_start(out=self.h[:, li, :], in_=src), writes=["h%d" % li])

    def store_group(self):
        P = self.P
        for li, (kind, idx) in enumerate(self.grp):
            dst = self.O["y_p"][idx * 128:(idx + 1) * 128, :] if kind == "P" else self.O["y_s"][:, :]
            P.dma("sp", lambda h, li=li, dst=dst: h.dma_start(out=dst, in_=self.h[:, li, :]), reads=["h%d" % li],
                  is_output=True)

    def layer(self, L):
        if L == 0:
            self.s5_mixer()
        else:
            self.mla_mixer()
        if self.stage >= 2:
            self.mem_attn(L)
        if self.stage >= 3:
            self.mlp(L)

    def s5_mixer(self):
        from_s5(self)

    def mem_attn(self, L):
        from_mem(self, L)

    def mlp(self, L):
        from_mlp(self, L)

    def mla_mixer(self):
        from_mla(self)


def from_s5(K):
    P, nc, I, O = K.P, K.nc, K.I, K.O
    A = K.arena
    off = [0]

    def carve(ncols, dt=F32, shape=None):
        a = off[0]
        off[0] += ncols
        v = A[:, a:a + ncols]
        if dt == BF16:
            v = v.bitcast(BF16)
        if shape is not None:
            names = "abc"[:len(shape)]
            kw = {names[i]: shape[i] for i in range(1, len(shape))}
            v = v.rearrange("p (%s) -> p %s" % (" ".join(names), " ".join(names)), **kw)
        return v

    BW = [carve(2048, BF16, [4, 8, 128]) for _ in range(2)]
    CW = [carve(2048, BF16, [4, 8, 128]) for _ in range(2)]
    sc = carve(32 * 16, F32, [16, 32])
    rows = carve(128 * 4, F32, [4, 128])
    msk = carve(8, F32)
    mski = carve(2, F32)
    dcol = carve(8)
    rmask = carve(128)
    rho_s = carve(128)
    ah0 = [carve(512, F32, [32, 16]) for _ in range(2)]
    fs = [carve(512, F32, [32, 16]) for _ in range(2)]
    fin = carve(64, F32, [32, 2])
    zt = carve(1024)
    r1 = off[0]
    Xre = carve(1024, F32, [32, 32])
    Xim = carve(1024, F32, [32, 32])
    T1 = carve(1024, F32, [32, 32])
    T2 = carve(1024, F32, [32, 32])
    raw = carve(1024, F32, [8, 128])
    Cl = carve(512, F32, [8, 64])
    endA = off[0]
    off[0] = r1
    h0stage = carve(4096)
    h0 = [carve(512, F32, [32, 16]) for _ in range(2)]
    endB = off[0]
    off[0] = r1
    CH = 256
    tabS = [carve(CH) for _ in range(2)]
    tabC = [carve(CH) for _ in range(2)]
    xi = [carve(CH).bitcast(I32) for _ in range(2)]
    rr = [carve(CH) for _ in range(2)]
    t1 = carve(CH)
    t2 = carve(CH)
    wre = [carve(CH) for _ in range(2)]
    wim = [carve(CH) for _ in range(2)]
    gre = [carve(CH) for _ in range(2)]
    gim = [carve(CH) for _ in range(2)]
    u1 = carve(CH)
    u2 = carve(CH)
    hre = [carve(CH // 2, BF16) for _ in range(2)]
    him = [carve(CH // 2, BF16) for _ in range(2)]
    yv = carve(CH)
    endC = off[0]
    carry = K.s5_carry
    assert max(endA, endB, endC) <= ARENA, (endA, endB, endC)

    SC_AR, SC_AI, SC_LDT, SC_DT, SC_RHO, SC_FR, SC_ABR, SC_ABI, SC_CRE, SC_CIM, SC_T0, SC_T1, SC_T2, SC_T3 = range(14)

    def s(k):
        return sc[:, k, :]

    P.dma("sp", lambda h: h.dma_start(out=rows[0:32, 0, :], in_=I["ssm_a_re"][0].rearrange("(j g) n -> j (g n)", g=2)), writes=["rows"])
    P.dma("sp", lambda h: h.dma_start(out=rows[0:32, 1, :], in_=I["ssm_a_im"][0].rearrange("(j g) n -> j (g n)", g=2)), writes=["rows"])
    P.dma("sp", lambda h: h.dma_start(out=rows[0:32, 3, 0:2], in_=I["ssm_log_dt"][0].rearrange("(j g) -> j g", g=2)), writes=["rows"])
    P.op("dve", lambda h: h.tensor_copy(out=rows[0:32, 2, :].rearrange("p (g n) -> p g n", g=2),
                                        in_=rows[0:32, 3, 0:2].unsqueeze(2).to_broadcast([32, 2, 64])),
         reads=["rows"], writes=["rows"])
    for k in range(3):
        P.op("pe", lambda h, k=k: h.transpose(out=K.ps[0][:, k * 32:(k + 1) * 32], in_=rows[0:32, k, :], identity=K.identf[0:32, 0:32]),
             reads=["rows", "identf"], writes=["ps0"])
    P.op("dve", lambda h: h.tensor_copy(out=sc[:, 0:3, :], in_=K.ps[0][:, 0:96].rearrange("p (a b) -> p a b", b=32)),
         reads=["ps0"], writes=["sc"])

    def ew(eng, fn):
        P.op(eng, fn, reads=["sc"], writes=["sc"])
    ew("act", lambda h: h.activation(out=s(SC_DT), in_=s(SC_LDT), func=AF.Exp))
    ew("dve", lambda h: h.tensor_tensor(out=s(SC_T0), in0=s(SC_AR), in1=s(SC_DT), op=ALU.mult))
    ew("act", lambda h: h.activation(out=s(SC_RHO), in_=s(SC_T0), func=AF.Exp))
    ew("dve", lambda h: h.scalar_tensor_tensor(out=s(SC_T1), in0=s(SC_AI), scalar=1.0 / TWO_PI, in1=s(SC_DT), op0=ALU.mult, op1=ALU.mult))
    ew("dve", lambda h: h.tensor_copy(out=s(SC_T2).bitcast(I32), in_=s(SC_T1)))
    ew("dve", lambda h: h.tensor_tensor(out=s(SC_FR), in0=s(SC_T1), in1=s(SC_T2).bitcast(I32), op=ALU.subtract))
    ew("act", lambda h: h.activation(out=s(SC_T0), in_=s(SC_FR), func=AF.Sin, scale=TWO_PI))
    ew("act", lambda h: h.activation(out=s(SC_T1), in_=s(SC_FR), func=AF.Abs))
    ew("act", lambda h: h.activation(out=s(SC_T1), in_=s(SC_T1), func=AF.Sin, scale=-TWO_PI, bias=K.halfpi[:]))
    ew("dve", lambda h: h.tensor_tensor(out=s(SC_ABI), in0=s(SC_RHO), in1=s(SC_T0), op=ALU.mult))
    ew("dve", lambda h: h.tensor_tensor(out=s(SC_ABR), in0=s(SC_RHO), in1=s(SC_T1), op=ALU.mult))
    ew("dve", lambda h: h.tensor_tensor(out=s(SC_T0), in0=s(SC_AR), in1=s(SC_AR), op=ALU.mult))
    ew("dve", lambda h: h.tensor_tensor(out=s(SC_T1), in0=s(SC_AI), in1=s(SC_AI), op=ALU.mult))
    ew("dve", lambda h: h.tensor_tensor(out=s(SC_T0), in0=s(SC_T0), in1=s(SC_T1), op=ALU.add))
    ew("dve", lambda h: h.reciprocal(out=s(SC_T0), in_=s(SC_T0)))
    ew("dve", lambda h: h.tensor_scalar(out=s(SC_T1), in0=s(SC_ABR), scalar1=-1.0, scalar2=None, op0=ALU.add))
    ew("dve", lambda h: h.tensor_tensor(out=s(SC_T2), in0=s(SC_T1), in1=s(SC_AR), op=ALU.mult))
    ew("dve", lambda h: h.tensor_tensor(out=s(SC_T3), in0=s(SC_ABI), in1=s(SC_AI), op=ALU.mult))
    ew("dve", lambda h: h.tensor_tensor(out=s(SC_T2), in0=s(SC_T2), in1=s(SC_T3), op=ALU.add))
    ew("dve", lambda h: h.tensor_tensor(out=s(SC_CRE), in0=s(SC_T2), in1=s(SC_T0), op=ALU.mult))
    ew("dve", lambda h: h.tensor_tensor(out=s(SC_T2), in0=s(SC_ABI), in1=s(SC_AR), op=ALU.mult))
    ew("dve", lambda h: h.tensor_tensor(out=s(SC_T3), in0=s(SC_T1), in1=s(SC_AI), op=ALU.mult))
    ew("dve", lambda h: h.tensor_tensor(out=s(SC_T2), in0=s(SC_T2), in1=s(SC_T3), op=ALU.subtract))
    ew("dve", lambda h: h.tensor_tensor(out=s(SC_CIM), in0=s(SC_T2), in1=s(SC_T0), op=ALU.mult))

    P.op("pool", lambda h: h.memset(Xre, 0.0), writes=["Xre"])
    P.op("pool", lambda h: h.memset(Xim, 0.0), writes=["Xim"])
    for g2 in range(2):
        for nm, X, xk in (("ssm_b_re", Xre, "Xre"), ("ssm_b_im", Xim, "Xim")):
            src = I[nm][0].rearrange("(j g) n k -> g n j k", g=2)[g2]
            P.dma("sp", lambda h, X=X, src=src, g2=g2: h.dma_start(out=X[g2 * 64:(g2 + 1) * 64, :, g2 * 16:(g2 + 1) * 16], in_=src),
                  reads=[xk], writes=[xk])
    cre_b = s(SC_CRE).unsqueeze(2).to_broadcast([128, 32, 32])
    cim_b = s(SC_CIM).unsqueeze(2).to_broadcast([128, 32, 32])
    P.op("pool", lambda h: h.memset(msk, 0.0), writes=["msk"])
    for jj in range(4):
        P.op("pool", lambda h, jj=jj: h.memset(msk[32 * jj:32 * jj + 32, jj:jj + 1], 1.0), reads=["msk"], writes=["msk"])
    for ri in range(2):
        if ri == 0:
            P.op("dve", lambda h: h.tensor_tensor(out=T1, in0=Xre, in1=cre_b, op=ALU.mult), reads=["Xre", "sc"], writes=["T1"])
            P.op("dve", lambda h: h.tensor_tensor(out=T2, in0=Xim, in1=cim_b, op=ALU.mult), reads=["Xim", "sc"], writes=["T2"])
            P.op("dve", lambda h: h.tensor_tensor(out=T1, in0=T1, in1=T2, op=ALU.subtract), reads=["T1", "T2"], writes=["T1"])
        else:
            P.op("dve", lambda h: h.tensor_tensor(out=T1, in0=Xim, in1=cre_b, op=ALU.mult), reads=["Xim", "sc"], writes=["T1"])
            P.op("dve", lambda h: h.tensor_tensor(out=T2, in0=Xre, in1=cim_b, op=ALU.mult), reads=["Xre", "sc"], writes=["T2"])
            P.op("dve", lambda h: h.tensor_tensor(out=T1, in0=T1, in1=T2, op=ALU.add), reads=["T1", "T2"], writes=["T1"])
        for half in range(2):
            bk = "ps%d" % half
            for qq in range(4):
                q = half * 4 + qq
                P.op("pe", lambda h, q=q, qq=qq, half=half: h.transpose(
                    out=K.ps[half][:, qq * 128:(qq + 1) * 128],
                    in_=T1[:, 4 * q:4 * q + 4, :].rearrange("p a b -> p (a b)"), identity=K.identf[:]),
                    reads=["T1", "identf"], writes=[bk])
            P.op("act", lambda h, half=half: h.copy(out=raw[:, half * 4:half * 4 + 4, :],
                                                     in_=K.ps[half][:, :].rearrange("p (a b) -> p a b", b=128)),
                 reads=[bk], writes=["raw"])
        for jj in range(4):
            P.op("dve", lambda h, jj=jj, ri=ri: h.tensor_scalar(out=BW[ri][:, jj, :, :], in0=raw, scalar1=msk[:, jj:jj + 1],
                                                                 scalar2=None, op0=ALU.mult),
                 reads=["raw", "msk"], writes=["BW%d" % ri])

    m1 = msk[:, 4:5]
    m0 = msk[:, 5:6]
    P.op("pool", lambda h: h.iota(mski.bitcast(I32)[:, 0:1], pattern=[[0, 1]], base=0, channel_multiplier=1), writes=["mski"])
    P.op("dve", lambda h: h.tensor_single_scalar(out=mski.bitcast(I32)[:, 1:2], in_=mski.bitcast(I32)[:, 0:1], scalar=16, op=ALU.bitwise_and),
         reads=["mski"], writes=["mski"])
    P.op("dve", lambda h: h.tensor_copy(out=m1, in_=mski.bitcast(I32)[:, 1:2]), reads=["mski", "msk"], writes=["msk"])
    P.op("dve", lambda h: h.tensor_scalar(out=m1, in0=m1, scalar1=1.0 / 16, scalar2=None, op0=ALU.mult), reads=["msk"], writes=["msk"])
    P.op("dve", lambda h: h.tensor_scalar(out=m0, in0=m1, scalar1=-1.0, scalar2=1.0, op0=ALU.mult, op1=ALU.add), reads=["msk"], writes=["msk"])
    XC = T1.rearrange("p a b -> p (a b)").rearrange("p (q c) -> p q c", c=128)
    for ri, nm in enumerate(("ssm_c_re", "ssm_c_im")):
        src = I[nm][0].rearrange("g k n -> (g k) n").rearrange("(q p) n -> p q n", p=128)
        P.dma("sp", lambda h, src=src: h.dma_start(out=Cl, in_=src), writes=["Cl"])
        sgn = 1.0 if ri == 0 else -1.0
        P.op("dve", lambda h, sgn=sgn: h.tensor_scalar(out=XC[:, :, 0:64], in0=Cl, scalar1=m0, scalar2=sgn, op0=ALU.mult, op1=ALU.mult),
             reads=["Cl", "msk"], writes=["T1"])
        P.op("dve", lambda h, sgn=sgn: h.tensor_scalar(out=XC[:, :, 64:128], in0=Cl, scalar1=m1, scalar2=sgn, op0=ALU.mult, op1=ALU.mult),
             reads=["Cl", "msk"], writes=["T1"])
        for half in range(2):
            bk = "ps%d" % half
            for qq in range(4):
                q = half * 4 + qq
                P.op("pe", lambda h, q=q, qq=qq, half=half: h.transpose(
                    out=K.ps[half][:, qq * 128:(qq + 1) * 128], in_=XC[:, q, :], identity=K.identf[:]),
                    reads=["T1", "identf"], writes=[bk])
            P.op("act", lambda h, half=half: h.copy(out=raw[:, half * 4:half * 4 + 4, :],
                                                     in_=K.ps[half][:, :].rearrange("p (a b) -> p a b", b=128)),
                 reads=[bk], writes=["raw"])
        P.op("pool", lambda h, ri=ri: h.memset(CW[ri].rearrange("p a b c -> p (a b c)"), 0.0), writes=["CW%d" % ri])
        for jj in range(4):
            P.op("dve", lambda h, jj=jj, ri=ri: h.tensor_copy(out=CW[ri][:, jj, :, 32 * jj:32 * jj + 32], in_=raw[:, :, 32 * jj:32 * jj + 32]),
                 reads=["raw", "CW%d" % ri], writes=["CW%d" % ri])

    P.dma("sp", lambda h: h.dma_start(out=rows[0:8, 0, :], in_=I["ssm_d"][0].rearrange("(q p) -> q p", p=128)), reads=["rows"], writes=["rows"])
    P.op("pe", lambda h: h.transpose(out=K.ps[2][:, 0:8], in_=rows[0:8, 0, :], identity=K.identf[0:8, 0:8]), reads=["rows", "identf"], writes=["ps2"])
    P.op("dve", lambda h: h.tensor_copy(out=dcol, in_=K.ps[2][:, 0:8]), reads=["ps2"], writes=["dcol"])

    P.barrier()
    g_pre, gk = K.load_gain(I["norm_mix_pre"][0])
    for li in range(K.ntile):
        K.norm_to_xnT(li, g_pre, gk, li * 128)

    wglu = [K.load_w(I["w_glu"][0][:, hf * 1024:(hf + 1) * 1024], 1024) for hf in range(2)]

    ptiles = [idx for (kind, idx) in K.grp if kind == "P"]
    has_s = any(kind == "S" for (kind, _) in K.grp)
    tp = 128 * len(ptiles)
    chunks = []
    c0 = 0
    while c0 < tp:
        n = min(256, tp - c0)
        chunks.append(("P", c0, n, ptiles[0] * 128 + c0))
        c0 += n
    if has_s:
        chunks.append(("S", tp, 128, 0))
        P.op("pool", lambda h: h.memset(rmask, 1.0), writes=["rmask"])
        P.op("pool", lambda h: h.memset(rmask.rearrange("p (b t) -> p b t", t=8)[:, :, 0:1], 0.0), reads=["rmask"], writes=["rmask"])
        for ri, nm in enumerate(("ssr", "ssi")):
            P.dma("sp", lambda h, nm=nm: h.dma_start(out=h0stage[0:16, :], in_=I[nm][:, :]), writes=["h0stage"])
            for j in range(32):
                P.op("pe", lambda h, j=j: h.transpose(out=K.ps[3][:, j * 16:(j + 1) * 16], in_=h0stage[0:16, j * 128:(j + 1) * 128],
                                                       identity=K.identf[0:16, 0:16]), reads=["h0stage", "identf"], writes=["ps3"])
            P.op("act", lambda h, ri=ri: h.copy(out=h0[ri], in_=K.ps[3][:, :].rearrange("p (j b) -> p j b", b=16)), reads=["ps3"], writes=["h0_%d" % ri])
        abr_b = s(SC_ABR).unsqueeze(2).to_broadcast([128, 32, 16])
        abi_b = s(SC_ABI).unsqueeze(2).to_broadcast([128, 32, 16])
        Ta = zt[:, 0:512].rearrange("p (j b) -> p j b", b=16)
        Tb = zt[:, 512:1024].rearrange("p (j b) -> p j b", b=16)
        P.op("dve", lambda h: h.tensor_tensor(out=Ta, in0=h0[0], in1=abr_b, op=ALU.mult), reads=["h0_0", "sc"], writes=["zt"])
        P.op("dve", lambda h: h.tensor_tensor(out=Tb, in0=h0[1], in1=abi_b, op=ALU.mult), reads=["h0_1", "sc"], writes=["zt"])
        P.op("dve", lambda h: h.tensor_tensor(out=ah0[0], in0=Ta, in1=Tb, op=ALU.subtract), reads=["zt"], writes=["ah0_0"])
        P.op("dve", lambda h: h.tensor_tensor(out=Ta, in0=h0[1], in1=abr_b, op=ALU.mult), reads=["h0_1", "sc"], writes=["zt"])
        P.op("dve", lambda h: h.tensor_tensor(out=Tb, in0=h0[0], in1=abi_b, op=ALU.mult), reads=["h0_0", "sc"], writes=["zt"])
        P.op("dve", lambda h: h.tensor_tensor(out=ah0[1], in0=Ta, in1=Tb, op=ALU.add), reads=["zt"], writes=["ah0_1"])
        P.barrier()
    if K.gi == 0:
        P.op("pool", lambda h: h.memset(carry.rearrange("p a b -> p (a b)"), 0.0), writes=["carry"])

    unit = 0
    for q in range(8):
        for (kind, c0, n, tglob) in chunks:
            ybank = 4 + (unit % 2)
            ybk = "ps%d" % ybank
            yps = K.ps[ybank]
            cols = slice(c0, c0 + n)
            for jj in range(4):
                j = 4 * q + jj
                bi = unit % 2
                bre, bim = K.ps[bi * 2], K.ps[bi * 2 + 1]
                brk, bik = "ps%d" % (bi * 2), "ps%d" % (bi * 2 + 1)
                ukey = "xnT.%d" % q
                P.op("pe", lambda h, bre=bre, jj=jj, q=q, cols=cols, n=n: h.matmul(bre[:, 0:n], lhsT=BW[0][:, jj, q, :], rhs=K.xnT[:, q, cols], start=True, stop=True),
                     reads=["BW0", ukey], writes=[brk])
                P.op("pe", lambda h, bim=bim, jj=jj, q=q, cols=cols, n=n: h.matmul(bim[:, 0:n], lhsT=BW[1][:, jj, q, :], rhs=K.xnT[:, q, cols], start=True, stop=True),
                     reads=["BW1", ukey], writes=[bik])
                tS, tC, xI, rR = tabS[bi], tabC[bi], xi[bi], rr[bi]
                tk = "tab%d" % bi
                fr = sc[:, SC_FR, j:j + 1]
                io = K.iota_g[:, cols]
                P.op("dve", lambda h, xI=xI, io=io, fr=fr, n=n: h.tensor_scalar(out=xI[:, 0:n], in0=io, scalar1=fr, scalar2=None, op0=ALU.mult),
                     reads=["iota_g", "sc"], writes=[tk + "x"])
                P.op("dve", lambda h, rR=rR, xI=xI, io=io, fr=fr, n=n: h.scalar_tensor_tensor(out=rR[:, 0:n], in0=io, scalar=fr, in1=xI[:, 0:n], op0=ALU.mult, op1=ALU.subtract),
                     reads=["iota_g", "sc", tk + "x"], writes=[tk + "r"])
                P.op("act", lambda h, tS=tS, rR=rR, n=n: h.activation(out=tS[:, 0:n], in_=rR[:, 0:n], func=AF.Sin, scale=TWO_PI),
                     reads=[tk + "r"], writes=[tk + "S"])
                P.op("act", lambda h, tC=tC, rR=rR, n=n: h.activation(out=tC[:, 0:n], in_=rR[:, 0:n], func=AF.Abs),
                     reads=[tk + "r"], writes=[tk + "C"])
                P.op("act", lambda h, tC=tC, n=n: h.activation(out=tC[:, 0:n], in_=tC[:, 0:n], func=AF.Sin, scale=-TWO_PI, bias=K.halfpi[:]),
                     reads=[tk + "C"], writes=[tk + "C"])
                if kind == "S":
                    for ri, bb, bk_ in ((0, bre, brk), (1, bim, bik)):
                        v = bb[:, 0:128].rearrange("p (b t) -> p b t", t=8)[:, :, 0]
                        P.op("dve", lambda h, v=v, ri=ri, j=j: h.tensor_tensor(out=v, in0=v, in1=ah0[ri][:, j, :], op=ALU.add),
                             reads=[bk_, "ah0_%d" % ri], writes=[bk_])
                wr, wi = wre[bi], wim[bi]
                wk = "w%d" % bi
                P.op("dve", lambda h, tC=tC, bre=bre, n=n: h.tensor_tensor(out=t1[:, 0:n], in0=bre[:, 0:n], in1=tC[:, 0:n], op=ALU.mult), reads=[brk, tk + "C"], writes=["t1"])
                P.op("dve", lambda h, tS=tS, bim=bim, n=n: h.tensor_tensor(out=t2[:, 0:n], in0=bim[:, 0:n], in1=tS[:, 0:n], op=ALU.mult), reads=[bik, tk + "S"], writes=["t2"])
                P.op("dve", lambda h, wr=wr, n=n: h.tensor_tensor(out=wr[:, 0:n], in0=t1[:, 0:n], in1=t2[:, 0:n], op=ALU.add), reads=["t1", "t2"], writes=[wk + "r"])
                P.op("dve", lambda h, tC=tC, bim=bim, n=n: h.tensor_tensor(out=t1[:, 0:n], in0=bim[:, 0:n], in1=tC[:, 0:n], op=ALU.mult), reads=[bik, tk + "C"], writes=["t1"])
                P.op("dve", lambda h, tS=tS, bre=bre, n=n: h.tensor_tensor(out=t2[:, 0:n], in0=bre[:, 0:n], in1=tS[:, 0:n], op=ALU.mult), reads=[brk, tk + "S"], writes=["t2"])
                P.op("dve", lambda h, wi=wi, n=n: h.tensor_tensor(out=wi[:, 0:n], in0=t1[:, 0:n], in1=t2[:, 0:n], op=ALU.subtract), reads=["t1", "t2"], writes=[wk + "i"])
                gr, gi_ = gre[bi], gim[bi]
                gk_ = "g%d" % bi
                if kind == "P":
                    d0 = sc[:, SC_RHO, j:j + 1].to_broadcast([128, n])
                    d0k = "sc"
                    ini = [carry[:, j, 0:1], carry[:, j, 1:2]]
                    inik = ["carry"]
                else:
                    P.op("pool", lambda h, j=j: h.tensor_scalar(out=rho_s, in0=rmask, scalar1=sc[:, SC_RHO, j:j + 1], scalar2=None, op0=ALU.mult),
                         reads=["rmask", "sc"], writes=["rho_s"])
                    d0 = rho_s[:, 0:n]
                    d0k = "rho_s"
                    ini = [0.0, 0.0]
                    inik = []
                P.op("dve", lambda h, gr=gr, wr=wr, d0=d0, ini=ini, n=n: h.tensor_tensor_scan(out=gr[:, 0:n], data0=d0, data1=wr[:, 0:n], initial=ini[0], op0=ALU.mult, op1=ALU.add),
                     reads=[wk + "r", d0k] + inik, writes=[gk_ + "r"])
                P.op("dve", lambda h, gi_=gi_, wi=wi, d0=d0, ini=ini, n=n: h.tensor_tensor_scan(out=gi_[:, 0:n], data0=d0, data1=wi[:, 0:n], initial=ini[1], op0=ALU.mult, op1=ALU.add),
                     reads=[wk + "i", d0k] + inik, writes=[gk_ + "i"])
                if kind == "P":
                    P.op("act", lambda h, gr=gr, j=j, n=n: h.copy(out=carry[:, j, 0:1], in_=gr[:, n - 1:n]), reads=[gk_ + "r", "carry"], writes=["carry"])
                    P.op("act", lambda h, gi_=gi_, j=j, n=n: h.copy(out=carry[:, j, 1:2], in_=gi_[:, n - 1:n]), reads=[gk_ + "i", "carry"], writes=["carry"])
                    if tglob + n == SEQ:
                        P.op("act", lambda h, tC=tC, tS=tS, j=j, n=n: h.copy(out=fin[:, j, 0:1], in_=tC[:, n - 1:n]), reads=[tk + "C"], writes=["fin"])
                        P.op("act", lambda h, tC=tC, tS=tS, j=j, n=n: h.copy(out=fin[:, j, 1:2], in_=tS[:, n - 1:n]), reads=[tk + "S", "fin"], writes=["fin"])
                else:
                    g7r = gr[:, 0:128].rearrange("p (b t) -> p b t", t=8)[:, :, 7]
                    g7i = gi_[:, 0:128].rearrange("p (b t) -> p b t", t=8)[:, :, 7]
                    c7 = tC[:, 7:8]
                    s7 = tS[:, 7:8]
                    o_r = fs[0][:, j, :]
                    o_i = fs[1][:, j, :]
                    P.op("pool", lambda h, g7r=g7r, c7=c7: h.tensor_scalar(out=u1[:, 0:16], in0=g7r, scalar1=c7, scalar2=None, op0=ALU.mult), reads=[gk_ + "r", tk + "C"], writes=["u1"])
                    P.op("pool", lambda h, g7i=g7i, s7=s7: h.tensor_scalar(out=u2[:, 0:16], in0=g7i, scalar1=s7, scalar2=None, op0=ALU.mult), reads=[gk_ + "i", tk + "S"], writes=["u2"])
                    P.op("pool", lambda h, o_r=o_r: h.tensor_tensor(out=o_r, in0=u1[:, 0:16], in1=u2[:, 0:16], op=ALU.subtract), reads=["u1", "u2"], writes=["fs0"])
                    P.op("pool", lambda h, g7i=g7i, c7=c7: h.tensor_scalar(out=u1[:, 0:16], in0=g7i, scalar1=c7, scalar2=None, op0=ALU.mult), reads=[gk_ + "i", tk + "C"], writes=["u1"])
                    P.op("pool", lambda h, g7r=g7r, s7=s7: h.tensor_scalar(out=u2[:, 0:16], in0=g7r, scalar1=s7, scalar2=None, op0=ALU.mult), reads=[gk_ + "r", tk + "S"], writes=["u2"])
                    P.op("pool", lambda h, o_i=o_i: h.tensor_tensor(out=o_i, in0=u1[:, 0:16], in1=u2[:, 0:16], op=ALU.add), reads=["u1", "u2"], writes=["fs1"])
                hr, hi = hre[bi], him[bi]
                hk_ = "hh%d" % bi
                P.op("pool", lambda h, tC=tC, gr=gr, n=n: h.tensor_tensor(out=u1[:, 0:n], in0=gr[:, 0:n], in1=tC[:, 0:n], op=ALU.mult), reads=[gk_ + "r", tk + "C"], writes=["u1"])
                P.op("pool", lambda h, tS=tS, gi_=gi_, n=n: h.tensor_tensor(out=u2[:, 0:n], in0=gi_[:, 0:n], in1=tS[:, 0:n], op=ALU.mult), reads=[gk_ + "i", tk + "S"], writes=["u2"])
                P.op("pool", lambda h, hr=hr, n=n: h.tensor_tensor(out=hr[:, 0:n], in0=u1[:, 0:n], in1=u2[:, 0:n], op=ALU.subtract), reads=["u1", "u2"], writes=[hk_ + "r"])
                P.op("pool", lambda h, tC=tC, gi_=gi_, n=n: h.tensor_tensor(out=u1[:, 0:n], in0=gi_[:, 0:n], in1=tC[:, 0:n], op=ALU.mult), reads=[gk_ + "i", tk + "C"], writes=["u1"])
                P.op("pool", lambda h, tS=tS, gr=gr, n=n: h.tensor_tensor(out=u2[:, 0:n], in0=gr[:, 0:n], in1=tS[:, 0:n], op=ALU.mult), reads=[gk_ + "r", tk + "S"], writes=["u2"])
                P.op("pool", lambda h, hi=hi, n=n: h.tensor_tensor(out=hi[:, 0:n], in0=u1[:, 0:n], in1=u2[:, 0:n], op=ALU.add), reads=["u1", "u2"], writes=[hk_ + "i"])
                P.op("pe", lambda h, yps=yps, hr=hr, jj=jj, q=q, n=n: h.matmul(yps[:, 0:n], lhsT=CW[0][:, jj, q, :], rhs=hr[:, 0:n], start=(jj == 0), stop=False),
                     reads=["CW0", hk_ + "r"], writes=[ybk])
                P.op("pe", lambda h, yps=yps, hi=hi, jj=jj, q=q, n=n: h.matmul(yps[:, 0:n], lhsT=CW[1][:, jj, q, :], rhs=hi[:, 0:n], start=False, stop=(jj == 3)),
                     reads=["CW1", hk_ + "i"], writes=[ybk])
                unit += 1
            P.op("dve", lambda h, yps=yps, q=q, cols=cols, n=n: h.scalar_tensor_tensor(out=yv[:, 0:n], in0=K.xnT[:, q, cols], scalar=dcol[:, q:q + 1], in1=yps[:, 0:n], op0=ALU.mult, op1=ALU.add),
                 reads=[ybk, "xnT.%d" % q, "dcol"], writes=["yv"])
            P.op("act", lambda h, q=q, cols=cols, n=n: h.activation(out=K.xnT[:, q, cols], in_=yv[:, 0:n], func=AF.Gelu),
                 reads=["yv"], writes=["xnT.%d" % q])

    P.barrier()
    if any(kind == "P" and idx == NPT - 1 for (kind, idx) in K.grp):
        cc, ss = fin[:, :, 0], fin[:, :, 1]
        gr_, gi2 = carry[:, :, 0], carry[:, :, 1]
        Fa = zt[:, 0:32]
        Fb = zt[:, 32:64]
        Fc = zt[:, 64:96]
        Fd = zt[:, 96:128]
        P.op("dve", lambda h: h.tensor_tensor(out=Fa, in0=cc, in1=gr_, op=ALU.mult), reads=["fin", "carry"], writes=["zt"])
        P.op("dve", lambda h: h.tensor_tensor(out=Fb, in0=ss, in1=gi2, op=ALU.mult), reads=["fin", "carry"], writes=["zt"])
        P.op("dve", lambda h: h.tensor_tensor(out=Fc, in0=Fa, in1=Fb, op=ALU.subtract), reads=["zt"], writes=["zt"])
        P.op("dve", lambda h: h.tensor_tensor(out=Fa, in0=cc, in1=gi2, op=ALU.mult), reads=["fin", "carry"], writes=["zt"])
        P.op("dve", lambda h: h.tensor_tensor(out=Fb, in0=ss, in1=gr_, op=ALU.mult), reads=["fin", "carry"], writes=["zt"])
        P.op("dve", lambda h: h.tensor_tensor(out=Fd, in0=Fa, in1=Fb, op=ALU.add), reads=["zt"], writes=["zt"])
        for ri, (T_, oname) in enumerate(((Fc, "ssm_re_p"), (Fd, "ssm_im_p"))):
            P.op("pe", lambda h, T_=T_: h.transpose(out=K.ps[6][0:32, 0:128], in_=T_, identity=K.identf[:]), reads=["zt", "identf"], writes=["ps6"])
            P.op("act", lambda h: h.copy(out=rows[0:32, 0, :], in_=K.ps[6][0:32, 0:128]), reads=["ps6"], writes=["rows"])
            P.dma("sp", lambda h, oname=oname: h.dma_start(out=O[oname].rearrange("o (j c) -> (o j) c", c=128), in_=rows[0:32, 0, :]), reads=["rows"], is_output=True)
    if has_s:
        for ri, oname in enumerate(("ssm_re_s", "ssm_im_s")):
            for j in range(32):
                bank = 6 + (j // 16) % 2
                P.op("pe", lambda h, j=j, ri=ri, bank=bank: h.transpose(out=K.ps[bank][0:16, (j % 4) * 128:(j % 4 + 1) * 128], in_=fs[ri][:, j, :], identity=K.identf[:]),
                     reads=["fs%d" % ri, "identf"], writes=["ps%d" % bank])
                if j % 4 == 3:
                    P.op("act", lambda h, j=j, bank=bank: h.copy(out=h0stage[0:16, (j - 3) * 128:(j + 1) * 128], in_=K.ps[bank][0:16, :]),
                         reads=["ps%d" % bank], writes=["h0stage"])
            P.dma("sp", lambda h, oname=oname: h.dma_start(out=O[oname][:, :], in_=h0stage[0:16, :]), reads=["h0stage"], is_output=True)

    g_post, gpk = K.load_gain(I["norm_mix_post"][0])
    for li in range(K.ntile):
        tc = slice(li * 128, (li + 1) * 128)
        for hf in range(2):
            wv, wk_ = wglu[hf]
            for nb in range(2):
                bank = hf * 2 + nb
                for k in range(8):
                    P.op("pe", lambda h, bank=bank, k=k, tc=tc, wv=wv, nb=nb: h.matmul(K.ps[bank][:, :], lhsT=K.xnT[:, k, tc], rhs=wv[:, k, nb * 512:(nb + 1) * 512], start=(k == 0), stop=(k == 7)),
                         reads=["xnT.%d" % k, wk_], writes=["ps%d" % bank])
        for nb in range(2):
            P.op("act", lambda h, nb=nb: h.activation(out=zt[:, nb * 512:(nb + 1) * 512], in_=K.ps[2 + nb][:, :], func=AF.Sigmoid), reads=["ps%d" % (2 + nb)], writes=["zt%d" % nb])
            P.op("dve", lambda h, nb=nb: h.tensor_tensor(out=zt[:, nb * 512:(nb + 1) * 512], in0=zt[:, nb * 512:(nb + 1) * 512], in1=K.ps[nb][:, :], op=ALU.mult),
                 reads=["zt%d" % nb, "ps%d" % nb], writes=["zt%d" % nb])
        K.post_norm_add(li, zt, ["zt0", "zt1"], g_post, gpk)
    P.barrier()


class Carver:
    def __init__(self, arena):
        self.A = arena
        self.off = 0
        self.hi = 0

    def __call__(self, ncols, dt=F32, shape=None):
        a = self.off
        self.off += ncols
        self.hi = max(self.hi, self.off)
        assert self.hi <= ARENA, self.hi
        v = self.A[:, a:a + ncols]
        if dt != F32:
            v = v.bitcast(dt)
        if shape is not None:
            names = "abc"[:len(shape)]
            kw = {names[i]: shape[i] for i in range(1, len(shape))}
            v = v.rearrange("p (%s) -> p %s" % (" ".join(names), " ".join(names)), **kw)
        return v


def evac(P, eng, out, in_, reads, writes):
    if eng == "act":
        P.op("act", lambda h: h.copy(out=out, in_=in_), reads=reads, writes=writes)
    else:
        P.op(eng, lambda h: h.tensor_copy(out=out, in_=in_), reads=reads, writes=writes)


def softmax_pt(K, S, P, spair, tpair, scale, p, Dm, pT, sm, tag):
    sp = K.pp[spair]
    tp_ = K.pp[tpair]
    sk = ["ps%d" % (2 * spair), "ps%d" % (2 * spair + 1)]
    tk = ["ps%d" % (2 * tpair), "ps%d" % (2 * tpair + 1)]
    mx, nb, l, rl = sm[:, 0:4], sm[:, 4:8], sm[:, 8:12], sm[:, 12:16]
    P.op("dve", lambda h: h.tensor_reduce(out=mx, in_=sp[:, :].rearrange("p (a b) -> p a b", b=256), axis=AX.X, op=ALU.max),
         reads=sk, writes=[tag + "sm"])
    P.op("dve", lambda h: h.tensor_scalar(out=nb, in0=mx, scalar1=-scale, scalar2=None, op0=ALU.mult), reads=[tag + "sm"], writes=[tag + "sm"])
    for hh in range(4):
        P.op("act", lambda h, hh=hh: h.activation(out=p[:, hh, :], in_=sp[:, hh * 256:(hh + 1) * 256], func=AF.Exp, scale=scale,
                                                   bias=nb[:, hh:hh + 1], accum_out=l[:, hh:hh + 1]),
             reads=sk + [tag + "sm"], writes=[tag + "p", tag + "l"])
    P.op("dve", lambda h: h.reciprocal(out=rl, in_=l), reads=[tag + "l", tag + "sm"], writes=[tag + "sm"])
    P.op("dve", lambda h: h.tensor_tensor(out=Dm, in0=K.ident[:].unsqueeze(1).to_broadcast([128, 4, 128]),
                                          in1=rl.unsqueeze(2).to_broadcast([128, 4, 128]), op=ALU.mult),
         reads=["ident", tag + "sm"], writes=[tag + "D"])
    for hh in range(4):
        for mt in range(2):
            c = hh * 2 + mt
            P.op("pe", lambda h, hh=hh, mt=mt, c=c: h.matmul(tp_[:, c * 128:(c + 1) * 128], lhsT=p[:, hh, mt * 128:(mt + 1) * 128], rhs=Dm[:, hh, :], start=True, stop=True),
                 reads=[tag + "p", tag + "D"], writes=[tk[c // 4]])
    evac(P, "act", pT[:, 0:4, :], tp_[:, 0:512].rearrange("p (a b) -> p a b", b=128), [tk[0]], [tag + "pT0"])
    evac(P, "dve", pT[:, 4:8, :], tp_[:, 512:1024].rearrange("p (a b) -> p a b", b=128), [tk[1]], [tag + "pT1"])


def from_mem(K, L):
    P, I, O = K.P, K.I, K.O
    C = Carver(K.arena)
    TOK = K.tok
    qT = C(TOK * 4, BF16, [8, TOK])
    mnT = C(1024, BF16, [8, 256])
    mkT = C(1024, BF16, [8, 256])
    mv = C(1024, BF16, [2, 1024])
    mst = C(1024)
    p = [C(512, BF16, [4, 256]) for _ in range(2)]
    Dm = [C(256, BF16, [4, 128]) for _ in range(2)]
    pT = [C(512, BF16, [8, 128]) for _ in range(2)]
    oT = [C(512, BF16, [8, 128]) for _ in range(2)]
    sm = [C(16) for _ in range(2)]
    has_s = any(kind == "S" for (kind, _) in K.grp)
    if has_s:
        Kst = [C(2048, F32, [2, 1024]) for _ in range(2)]
        KTb = C(1024, BF16, [8, 256])
        Vb = C(1024, BF16, [2, 1024])
        sTs = C(1024, F32, [8, 128])

    wk, wkk = K.load_w(I["w_mem_k"][L], 1024)
    wv, wvk = K.load_w(I["w_mem_v"][L], 1024)
    wq, wqk = K.load_w(I["w_mem_q"][L], 1024)

    g_in, gik = K.load_gain(I["mem_in_norm"][L])
    for mt in range(2):
        P.dma("sp", lambda h, mt=mt: h.dma_start(out=mst, in_=I["memp"][mt * 128:(mt + 1) * 128, :]), writes=["mst"])
        rs, rk = K.rms_stats(mst, ["mst"], D)
        P.op("dve", lambda h, rs=rs: h.scalar_tensor_tensor(out=K.xn[:], in0=mst, scalar=rs, in1=g_in[:], op0=ALU.mult, op1=ALU.mult),
             reads=["mst", rk, gik], writes=["xn"])
        K.transpose_to(K.xn, "xn", mnT, "mnT", mt * 128)
    mnk = ["mnT.%d" % k for k in range(8)]
    for mt in range(2):
        for which, (w_, wkey, oname) in enumerate(((wk, wkk, "memk_p"), (wv, wvk, "memv_p"))):
            if which == 0 and K.gi != 0:
                continue
            pair = 2 + which
            for nb in range(2):
                for k in range(8):
                    P.op("pe", lambda h, pair=pair, nb=nb, k=k, mt=mt, w_=w_: h.matmul(K.pp[pair][:, nb * 512:(nb + 1) * 512], lhsT=mnT[:, k, mt * 128:(mt + 1) * 128], rhs=w_[:, k, nb * 512:(nb + 1) * 512], start=(k == 0), stop=(k == 7)),
                         reads=mnk + [wkey], writes=["ps%d" % (2 * pair + nb)])
            pk = ["ps%d" % (2 * pair), "ps%d" % (2 * pair + 1)]
            if which == 1:
                evac(P, "act", mv[:, mt, :], K.pp[pair][:, :], pk, ["mv"])
            if K.gi == 0:
                evac(P, "dve", mst, K.pp[pair][:, :], pk, ["mst"])
                P.dma("sp", lambda h, oname=oname, mt=mt: h.dma_start(out=O[oname][L, mt * 128:(mt + 1) * 128, :], in_=mst), reads=["mst"], is_output=True)
    for c8 in range(8):
        bank = c8 % 2
        for k in range(8):
            P.op("pe", lambda h, bank=bank, c8=c8, k=k: h.matmul(K.ps[bank][:, 0:256], lhsT=wk[:, k, c8 * 128:(c8 + 1) * 128], rhs=mnT[:, k, :], start=(k == 0), stop=(k == 7)),
                 reads=mnk + [wkk], writes=["ps%d" % bank])
        evac(P, "act" if c8 % 2 == 0 else "dve", mkT[:, c8, :], K.ps[bank][:, 0:256], ["ps%d" % bank], ["mkT"])

    g_pre, gpk = K.load_gain(I["norm_mem_pre"][L])
    for li in range(K.ntile):
        K.norm_to_xnT(li, g_pre, gpk, li * 128)
    wo, wok = K.load_w(I["w_mem_o"][L], 1024)
    xk = ["xnT.%d" % k for k in range(8)]
    nblk = 2
    bs = TOK // nblk
    u = 0
    for blk in range(nblk):
        cs = slice(blk * bs, (blk + 1) * bs)
        for c8 in range(8):
            bank = 4 + (u % 4)
            for k in range(8):
                P.op("pe", lambda h, bank=bank, c8=c8, k=k, cs=cs: h.matmul(K.ps[bank][:, 0:bs], lhsT=wq[:, k, c8 * 128:(c8 + 1) * 128], rhs=K.xnT[:, k, cs], start=(k == 0), stop=(k == 7)),
                     reads=xk + [wqk], writes=["ps%d" % bank])
            evac(P, "act" if u % 2 == 0 else "dve", qT[:, c8, cs], K.ps[bank][:, 0:bs], ["ps%d" % bank], ["qT"])
            u += 1
    g_post, gpok = K.load_gain(I["norm_mem_post"][L])

    def tail(li, bi, tpair_pT):
        pass

    for li, (kind, idx) in enumerate(K.grp):
        bi = li % 2
        tag = "m%d" % bi
        tcols = slice(li * 128, (li + 1) * 128)
        if kind == "P":
            for hh in range(4):
                for dc in range(2):
                    c8 = hh * 2 + dc
                    P.op("pe", lambda h, hh=hh, dc=dc, c8=c8, tcols=tcols: h.matmul(K.pp[0][:, hh * 256:(hh + 1) * 256], lhsT=qT[:, c8, tcols], rhs=mkT[:, c8, :], start=(dc == 0), stop=(dc == 1)),
                         reads=["qT", "mkT"], writes=["ps%d" % (hh // 2)])
            softmax_pt(K, None, P, 0, 1, MEM_SCALE, p[bi], Dm[bi], pT[bi], sm[bi], tag)
            for hh in range(4):
                for dvc in range(2):
                    c = hh * 2 + dvc
                    for mt in range(2):
                        P.op("pe", lambda h, hh=hh, dvc=dvc, c=c, mt=mt, bi=bi: h.matmul(K.pp[2][:, c * 128:(c + 1) * 128], lhsT=mv[:, mt, hh * 256 + dvc * 128:hh * 256 + (dvc + 1) * 128], rhs=pT[bi][:, hh * 2 + mt, :], start=(mt == 0), stop=(mt == 1)),
                             reads=["mv", tag + "pT0", tag + "pT1"], writes=["ps%d" % (4 + c // 4)])
        else:
            scol = li * 128
            for b in range(16):
                sb_ = b % 2
                P.dma("sp", lambda h, b=b, sb_=sb_: h.dma_start(out=Kst[sb_], in_=I["memk"][L, b].rearrange("(a p) n -> p a n", p=128)), writes=["Kst%d" % sb_])
                for mt in range(2):
                    for c8 in range(8):
                        t_ = mt * 8 + c8
                        P.op("pe", lambda h, sb_=sb_, mt=mt, c8=c8: h.transpose(out=K.pp[2 + c8 // 4][:, (c8 % 4) * 256 + mt * 128:(c8 % 4) * 256 + (mt + 1) * 128], in_=Kst[sb_][:, mt, c8 * 128:(c8 + 1) * 128], identity=K.identf[:]),
                             reads=["Kst%d" % sb_, "identf"], writes=["ps%d" % (4 + (c8 // 4) * 2 + (c8 % 4) // 2)])
                evac(P, "act", KTb[:, 0:4, :], K.pp[2][:, :].rearrange("p (a b) -> p a b", b=256), ["ps4", "ps5"], ["KTb0"])
                evac(P, "dve", KTb[:, 4:8, :], K.pp[3][:, :].rearrange("p (a b) -> p a b", b=256), ["ps6", "ps7"], ["KTb1"])
                for hh in range(4):
                    for mt in range(2):
                        c = hh * 2 + mt
                        for dc in range(2):
                            P.op("pe", lambda h, hh=hh, mt=mt, c=c, dc=dc, b=b: h.matmul(K.pp[0][:, c * 128 + b * 8:c * 128 + b * 8 + 8], lhsT=KTb[:, hh * 2 + dc, mt * 128:(mt + 1) * 128], rhs=qT[:, hh * 2 + dc, scol + b * 8:scol + b * 8 + 8], start=(dc == 0), stop=(dc == 1)),
                                 reads=["KTb0", "KTb1", "qT"], writes=["ps%d" % (c // 4)])
            evac(P, "act", sTs[:, 0:4, :], K.pp[0][:, 0:512].rearrange("p (a b) -> p a b", b=128), ["ps0"], ["sTs"])
            evac(P, "dve", sTs[:, 4:8, :], K.pp[0][:, 512:1024].rearrange("p (a b) -> p a b", b=128), ["ps1"], ["sTs"])
            for hh in range(4):
                for mt in range(2):
                    P.op("pe", lambda h, hh=hh, mt=mt: h.transpose(out=K.pp[1][:, hh * 256 + mt * 128:hh * 256 + (mt + 1) * 128], in_=sTs[:, hh * 2 + mt, :], identity=K.identf[:]),
                         reads=["sTs", "identf"], writes=["ps%d" % (2 + hh // 2)])
            softmax_pt(K, None, P, 1, 0, MEM_SCALE, p[bi], Dm[bi], pT[bi], sm[bi], tag)
            for b in range(16):
                sb_ = b % 2
                P.dma("sp", lambda h, b=b, sb_=sb_: h.dma_start(out=Kst[sb_], in_=I["memv"][L, b].rearrange("(a p) n -> p a n", p=128)), writes=["Kst%d" % sb_])
                P.op("pool", lambda h, sb_=sb_: h.tensor_copy(out=Vb, in_=Kst[sb_]), reads=["Kst%d" % sb_], writes=["Vb"])
                for hh in range(4):
                    for dvc in range(2):
                        c = hh * 2 + dvc
                        for mt in range(2):
                            P.op("pe", lambda h, hh=hh, dvc=dvc, c=c, mt=mt, bi=bi, b=b: h.matmul(K.pp[2][:, c * 128 + b * 8:c * 128 + b * 8 + 8], lhsT=Vb[:, mt, hh * 256 + dvc * 128:hh * 256 + (dvc + 1) * 128], rhs=pT[bi][:, hh * 2 + mt, b * 8:b * 8 + 8], start=(mt == 0), stop=(mt == 1)),
                                 reads=["Vb", tag + "pT0", tag + "pT1"], writes=["ps%d" % (4 + c // 4)])
        evac(P, "act", oT[bi][:, 0:4, :], K.pp[2][:, 0:512].rearrange("p (a b) -> p a b", b=128), ["ps4"], [tag + "oT"])
        evac(P, "dve", oT[bi][:, 4:8, :], K.pp[2][:, 512:1024].rearrange("p (a b) -> p a b", b=128), ["ps5"], [tag + "oT"])
        for nb in range(2):
            for c8 in range(8):
                P.op("pe", lambda h, nb=nb, c8=c8, bi=bi: h.matmul(K.pp[3][:, nb * 512:(nb + 1) * 512], lhsT=oT[bi][:, c8, :], rhs=wo[:, c8, nb * 512:(nb + 1) * 512], start=(c8 == 0), stop=(c8 == 7)),
                     reads=[tag + "oT", wok], writes=["ps%d" % (6 + nb)])
        K.post_norm_add(li, K.pp[3][:, :], ["ps6", "ps7"], g_post, gpok)
    P.barrier()


def from_mlp(K, L):
    P, I, O = K.P, K.I, K.O
    C = Carver(K.arena)
    TOK = K.tok
    NT = K.ntile
    acc = C(NT * 1024, F32, [NT, 1024])
    hdnT = C(TOK * 4, BF16, [8, TOK])
    r = [C(512) for _ in range(2)]
    g_pre, gpk = K.load_gain(I["norm_mlp_pre"][L])
    for li in range(NT):
        K.norm_to_xnT(li, g_pre, gpk, li * 128)
    xk = ["xnT.%d" % k for k in range(8)]
    nblk = 2
    bs = TOK // nblk
    u = 0
    for c in range(4):
        wu, wuk = K.load_w(I["w_mlp_up"][L][:, c * 1024:(c + 1) * 1024], 1024)
        wd, wdk = K.load_w(I["w_mlp_down"][L][c * 1024:(c + 1) * 1024, :], 1024)
        for blk in range(nblk):
            cs = slice(blk * bs, (blk + 1) * bs)
            for f in range(8):
                bank = u % 4
                ri = u % 2
                for k in range(8):
                    P.op("pe", lambda h, bank=bank, f=f, k=k, cs=cs, wu=wu: h.matmul(K.ps[bank][:, 0:bs], lhsT=wu[:, k, f * 128:(f + 1) * 128], rhs=K.xnT[:, k, cs], start=(k == 0), stop=(k == 7)),
                         reads=xk + [wuk], writes=["ps%d" % bank])
                P.op("act", lambda h, bank=bank, ri=ri: h.activation(out=r[ri][:, 0:bs], in_=K.ps[bank][:, 0:bs], func=AF.Relu), reads=["ps%d" % bank], writes=["r%d" % ri])
                P.op("pool", lambda h, ri=ri, f=f, cs=cs: h.tensor_tensor(out=hdnT[:, f, cs], in0=r[ri][:, 0:bs], in1=r[ri][:, 0:bs], op=ALU.mult), reads=["r%d" % ri], writes=["hdnT.%d.%d" % (f, blk)])
                u += 1
        for li in range(NT):
            blk = (li * 128) // bs
            pair = 2 + li % 2
            for nb in range(2):
                for f in range(8):
                    P.op("pe", lambda h, pair=pair, nb=nb, f=f, li=li, wd=wd: h.matmul(K.pp[pair][:, nb * 512:(nb + 1) * 512], lhsT=hdnT[:, f, li * 128:(li + 1) * 128], rhs=wd[:, f, nb * 512:(nb + 1) * 512], start=(f == 0), stop=(f == 7)),
                         reads=["hdnT.%d.%d" % (f, b_) for b_ in range(nblk)] + [wdk], writes=["ps%d" % (2 * pair + nb)])
            pk = ["ps%d" % (2 * pair), "ps%d" % (2 * pair + 1)]
            if c == 0:
                evac(P, "act", acc[:, li, :], K.pp[pair][:, :], pk, ["acc%d" % li])
            else:
                P.op("dve", lambda h, li=li, pair=pair: h.tensor_tensor(out=acc[:, li, :], in0=acc[:, li, :], in1=K.pp[pair][:, :], op=ALU.add), reads=pk + ["acc%d" % li], writes=["acc%d" % li])
    g_post, gpok = K.load_gain(I["norm_mlp_post"][L])
    for li in range(NT):
        K.post_norm_add(li, acc[:, li, :], ["acc%d" % li], g_post, gpok)
    P.barrier()


def from_mla(K):
    P, I, O = K.P, K.I, K.O
    C = Carver(K.arena)
    TOK, NT, grp = K.tok, K.ntile, K.grp
    ptiles = [idx for (kind, idx) in grp if kind == "P"]
    has_s = len(ptiles) < len(grp)
    tp = 128 * len(ptiles)
    invf = K.invf[:, 0:1]
    cqT = C(3 * TOK // 2, BF16, [3, TOK])
    TQ = C(TOK)
    CS = C(NT * 128, F32, [NT, 128])
    glat = C(256)
    gq = C(384)
    wkr2 = C(512, BF16, [8, 128])
    off_wqr = C.off
    wqr = C(1536, BF16, [3, 8, 128])
    off_wukT = C.off
    wukT = C(1024, BF16, [8, 256])
    oT = C(4 * TOK, BF16, [8, TOK])
    ckvf = C(256)
    ab = C(128)
    kr2 = C(128)
    if has_s:
        qSl = C(1024, BF16, [2, 128, 8])
        qSr = C(512, BF16, [128, 8])
        KTn = C(192, BF16, [3, 128])
        Vn = C(128, BF16)
        olS = C(1024, BF16, [2, 128, 8])
    base = C.off
    xI = C(TOK).bitcast(I32)
    rr = C(TOK)
    stg = C(2048, F32, [2, 1024])
    cqf = C(384)
    C.off = base
    qn = [C(TOK // 2, BF16) for _ in range(2)]
    qlat = [C(TOK, BF16, [2, TOK]) for _ in range(2)]
    qs = [C(TOK // 2, BF16) for _ in range(2)]
    pb = [C(1024, BF16) for _ in range(2)]
    Dm = [C(64, BF16) for _ in range(2)]
    pT = [C(1024, BF16, [16, 128]) for _ in range(2)]
    olT = [C(128, BF16, [2, 128]) for _ in range(2)]
    sm = [C(16) for _ in range(2)]

    g_in, gk = K.load_gain(I["kv_in_norm"])
    for li in range(NT):
        K.norm_to_xnT(li, g_in, gk, li * 128)
    wdkv, wdkvk = K.load_w(I["w_dkv"], 256)
    wkr, wkrk = K.load_w(I["w_kr"], 64)
    P.op("act", lambda h: h.copy(out=wkr2[:, :, 0:64], in_=wkr), reads=[wkrk], writes=["wkr2"])
    P.op("act", lambda h: h.mul(out=wkr2[:, :, 64:96], in_=wkr[:, :, 32:64], mul=-1.0), reads=[wkrk, "wkr2"], writes=["wkr2"])
    P.op("act", lambda h: h.copy(out=wkr2[:, :, 96:128], in_=wkr[:, :, 0:32]), reads=[wkrk, "wkr2"], writes=["wkr2"])
    P.dma("sp", lambda h: h.dma_start(out=glat, in_=I["kv_latent_norm"].rearrange("(o n) -> o n", o=1).to_broadcast([128, 256])), writes=["glat"])
    P.dma("sp", lambda h: h.dma_start(out=gq, in_=I["q_norm"][0].rearrange("(o n) -> o n", o=1).to_broadcast([128, 384])), writes=["gq"])
    P.op("dve", lambda h: h.tensor_scalar(out=xI[:, 0:tp], in0=K.iota_g[:, 0:tp], scalar1=invf, scalar2=None, op0=ALU.mult), reads=["iota_g", "invf"], writes=["xI"])
    P.op("dve", lambda h: h.scalar_tensor_tensor(out=rr[:, 0:tp], in0=K.iota_g[:, 0:tp], scalar=invf, in1=xI[:, 0:tp], op0=ALU.mult, op1=ALU.subtract), reads=["iota_g", "invf", "xI"], writes=["rr"])
    if has_s:
        P.op("dve", lambda h: h.tensor_scalar(out=xI[:, tp:TOK], in0=K.iota_g[:, tp:TOK], scalar1=float(PAST), scalar2=invf, op0=ALU.add, op1=ALU.mult), reads=["iota_g", "invf", "xI"], writes=["xI"])
        P.op("dve", lambda h: h.tensor_scalar(out=rr[:, tp:TOK], in0=K.iota_g[:, tp:TOK], scalar1=float(PAST), scalar2=invf, op0=ALU.add, op1=ALU.mult), reads=["iota_g", "invf", "rr"], writes=["rr"])
        P.op("dve", lambda h: h.tensor_tensor(out=rr[:, tp:TOK], in0=rr[:, tp:TOK], in1=xI[:, tp:TOK], op=ALU.subtract), reads=["xI", "rr"], writes=["rr"])
    P.op("act", lambda h: h.activation(out=TQ[64:128, 0:TOK], in_=rr[64:128, 0:TOK], func=AF.Sin, scale=TWO_PI), reads=["rr"], writes=["TQs"])
    P.op("act", lambda h: h.activation(out=TQ[0:64, 0:TOK], in_=rr[0:64, 0:TOK], func=AF.Abs), reads=["rr"], writes=["TQc"])
    P.op("act", lambda h: h.activation(out=TQ[0:64, 0:TOK], in_=TQ[0:64, 0:TOK], func=AF.Sin, scale=-TWO_PI, bias=K.halfpi[0:64]), reads=["TQc"], writes=["TQc"])
    for li in range(NT):
        P.op("pe", lambda h, li=li: h.transpose(out=K.ps[3][:, 0:128], in_=TQ[:, li * 128:(li + 1) * 128], identity=K.identf[:]), reads=["TQs", "TQc", "identf"], writes=["ps3"])
        evac(P, "act", CS[:, li, :], K.ps[3][:, 0:128], ["ps3"], ["CS"])
    xk = ["xnT.%d" % k for k in range(8)]
    for li, (kind, idx) in enumerate(grp):
        tcols = slice(li * 128, (li + 1) * 128)
        for k in range(8):
            P.op("pe", lambda h, k=k, tcols=tcols: h.matmul(K.ps[0][:, 0:256], lhsT=K.xnT[:, k, tcols], rhs=wdkv[:, k, :], start=(k == 0), stop=(k == 7)), reads=xk + [wdkvk], writes=["ps0"])
        for k in range(8):
            P.op("pe", lambda h, k=k, tcols=tcols: h.matmul(K.ps[1][:, 0:128], lhsT=K.xnT[:, k, tcols], rhs=wkr2[:, k, :], start=(k == 0), stop=(k == 7)), reads=xk + ["wkr2"], writes=["ps1"])
        rs, rk = K.rms_stats(K.ps[0][:, 0:256], ["ps0"], 256)
        P.op("dve", lambda h, rs=rs: h.scalar_tensor_tensor(out=ckvf, in0=K.ps[0][:, 0:256], scalar=rs, in1=glat, op0=ALU.mult, op1=ALU.mult), reads=["ps0", rk, "glat"], writes=["ckvf"])
        if kind == "P":
            P.dma("sp", lambda h, idx=idx: h.dma_start(out=O["kvl_p"][idx * 128:(idx + 1) * 128, :], in_=ckvf), reads=["ckvf"], is_output=True)
            P.op("pool", lambda h, idx=idx: h.tensor_copy(out=K.Vp[:, idx, :], in_=ckvf), reads=["ckvf"], writes=["Vp"])
            K.transpose_to(ckvf, "ckvf", K.KT, "KT", idx * 128, nk=2, banks=(2,))
        else:
            P.dma("sp", lambda h: h.dma_start(out=O["kvl_s"][:, :], in_=ckvf), reads=["ckvf"], is_output=True)
            P.op("pool", lambda h: h.tensor_copy(out=Vn, in_=ckvf), reads=["ckvf"], writes=["Vn"])
            K.transpose_to(ckvf, "ckvf", KTn, "KTn", 0, nk=2, banks=(2,))
        P.op("dve", lambda h, li=li: h.tensor_tensor(out=ab, in0=K.ps[1][:, 0:128], in1=CS[:, li, :], op=ALU.mult), reads=["ps1", "CS"], writes=["ab"])
        P.op("pool", lambda h: h.tensor_tensor(out=kr2[:, 0:64], in0=ab[:, 0:64], in1=ab[:, 64:128], op=ALU.add), reads=["ab"], writes=["kr2"])
        P.op("pool", lambda h: h.tensor_copy(out=kr2[:, 64:128], in_=kr2[:, 0:64]), reads=["kr2"], writes=["kr2"])
        if kind == "P":
            P.dma("sp", lambda h, idx=idx: h.dma_start(out=O["kr_p"][idx * 128:(idx + 1) * 128, :], in_=kr2[:, 0:64]), reads=["kr2"], is_output=True)
            K.transpose_to(kr2, "kr2", K.KT[:, 2:3, :], "KTr", idx * 128, nk=1, banks=(3,))
        else:
            P.dma("sp", lambda h: h.dma_start(out=O["kr_s"][:, :], in_=kr2[:, 0:64]), reads=["kr2"], is_output=True)
            K.transpose_to(kr2, "kr2", KTn[:, 2:3, :], "KTnr", 0, nk=1, banks=(3,))

    g_pre, gpk = K.load_gain(I["norm_mix_pre"][1])
    for li in range(NT):
        K.norm_to_xnT(li, g_pre, gpk, li * 128)
    wdq, wdqk = K.load_w(I["w_dq"][0], 384)
    for li in range(NT):
        tcols = slice(li * 128, (li + 1) * 128)
        bank = 4 + li % 2
        for k in range(8):
            P.op("pe", lambda h, k=k, tcols=tcols, bank=bank: h.matmul(K.ps[bank][:, 0:384], lhsT=K.xnT[:, k, tcols], rhs=wdq[:, k, :], start=(k == 0), stop=(k == 7)), reads=xk + [wdqk], writes=["ps%d" % bank])
        rs, rk = K.rms_stats(K.ps[bank][:, 0:384], ["ps%d" % bank], 384)
        P.op("dve", lambda h, rs=rs, bank=bank: h.scalar_tensor_tensor(out=cqf, in0=K.ps[bank][:, 0:384], scalar=rs, in1=gq, op0=ALU.mult, op1=ALU.mult), reads=["ps%d" % bank, rk, "gq"], writes=["cqf"])
        K.transpose_to(cqf, "cqf", cqT, "cqT", li * 128, nk=3, banks=(6,))
    cqk = ["cqT.%d" % k for k in range(3)]
    wuq, wuqk = K.load_w(I["w_uq"][0].rearrange("r h d -> r (h d)"), 1536, nk=3)
    wuq4 = wuq.rearrange("p k (h d) -> p k h d", d=192)
    P.op("act", lambda h: h.copy(out=wqr[:, :, :, 0:64], in_=wuq4[:, :, :, 128:192]), reads=[wuqk], writes=["wqr"])
    P.op("act", lambda h: h.mul(out=wqr[:, :, :, 64:96], in_=wuq4[:, :, :, 160:192], mul=-1.0), reads=[wuqk, "wqr"], writes=["wqr"])
    P.op("act", lambda h: h.copy(out=wqr[:, :, :, 96:128], in_=wuq4[:, :, :, 128:160]), reads=[wuqk, "wqr"], writes=["wqr"])
    P.dma("sp", lambda h: h.dma_start(out=stg, in_=I["w_uk"].rearrange("(c p) h d -> p c (h d)", p=128)), writes=["stg"])
    for hh in range(8):
        for rc in range(2):
            t_ = hh * 2 + rc
            P.op("pe", lambda h, hh=hh, rc=rc, t_=t_: h.transpose(out=K.pp[t_ // 8][:, (t_ % 8) * 128:(t_ % 8 + 1) * 128], in_=stg[:, rc, hh * 128:(hh + 1) * 128], identity=K.identf[:]),
                 reads=["stg", "identf"], writes=["ps%d" % (2 * (t_ // 8) + (t_ % 8) // 4)])
    evac(P, "act", wukT[:, 0:4, :], K.pp[0][:, :].rearrange("p (a b) -> p a b", b=256), ["ps0", "ps1"], ["wukT"])
    evac(P, "dve", wukT[:, 4:8, :], K.pp[1][:, :].rearrange("p (a b) -> p a b", b=256), ["ps2", "ps3"], ["wukT"])
    wuv, wuvk = K.load_w(I["w_uv"].rearrange("r h v -> r (h v)"), 1024, nk=2)
    wo, wok = K.load_w(I["w_o"][0], 1024)
    P.barrier()

    nblk = 2
    bs = TOK // nblk
    cnt = 0
    for hh in range(8):
        hb = hh % 2
        qk = "q%d" % hb
        for blk in range(nblk):
            cs = slice(blk * bs, (blk + 1) * bs)
            for kc in range(3):
                P.op("pe", lambda h, hh=hh, kc=kc, cs=cs: h.matmul(K.ps[4][:, 0:bs], lhsT=wuq[:, kc, hh * 192:hh * 192 + 128], rhs=cqT[:, kc, cs], start=(kc == 0), stop=(kc == 2)), reads=cqk + [wuqk], writes=["ps4"])
            evac(P, "act", qn[hb][:, cs], K.ps[4][:, 0:bs], ["ps4"], [qk + "n"])
            for rc in range(2):
                P.op("pe", lambda h, hh=hh, rc=rc, cs=cs, hb=hb: h.matmul(K.ps[5 + rc][:, 0:bs], lhsT=wukT[:, hh, rc * 128:(rc + 1) * 128], rhs=qn[hb][:, cs], start=True, stop=True), reads=["wukT", qk + "n"], writes=["ps%d" % (5 + rc)])
                evac(P, "dve" if rc == 0 else "act", qlat[hb][:, rc, cs], K.ps[5 + rc][:, 0:bs], ["ps%d" % (5 + rc)], [qk + "l"])
            for kc in range(3):
                P.op("pe", lambda h, hh=hh, kc=kc, cs=cs: h.matmul(K.ps[7][:, 0:bs], lhsT=wqr[:, kc, hh, :], rhs=cqT[:, kc, cs], start=(kc == 0), stop=(kc == 2)), reads=cqk + ["wqr"], writes=["ps7"])
            P.op("dve", lambda h, cs=cs, hb=hb: h.tensor_tensor(out=qs[hb][:, cs], in0=K.ps[7][:, 0:bs], in1=TQ[:, cs], op=ALU.mult), reads=["ps7", "TQs", "TQc"], writes=[qk + "s"])
        if has_s:
            P.op("pool", lambda h, hh=hh, hb=hb: h.tensor_copy(out=qSl[:, :, :, hh], in_=qlat[hb][:, :, tp:TOK]), reads=[qk + "l"], writes=["qSl"])
            P.op("pool", lambda h, hh=hh, hb=hb: h.tensor_copy(out=qSr[:, :, hh], in_=qs[hb][:, tp:TOK]), reads=[qk + "s"], writes=["qSr"])
        for li, (kind, idx) in enumerate(grp):
            if kind != "P":
                continue
            bi = cnt % 2
            cnt += 1
            tag = "a%d" % bi
            tcols = slice(li * 128, (li + 1) * 128)
            nk = idx + 1
            nkeys = 128 * nk
            nkb = (nkeys + 511) // 512
            for kb in range(nkb):
                w = min(512, nkeys - kb * 512)
                kcols = slice(kb * 512, kb * 512 + w)
                P.op("pe", lambda h, kb=kb, w=w, kcols=kcols, tcols=tcols, hb=hb: h.matmul(K.ps[kb][:, 0:w], lhsT=qlat[hb][:, 0, tcols], rhs=K.KT[:, 0, kcols], start=True, stop=False), reads=[qk + "l", "KT.0"], writes=["ps%d" % kb])
                P.op("pe", lambda h, kb=kb, w=w, kcols=kcols, tcols=tcols, hb=hb: h.matmul(K.ps[kb][:, 0:w], lhsT=qlat[hb][:, 1, tcols], rhs=K.KT[:, 1, kcols], start=False, stop=False), reads=[qk + "l", "KT.1"], writes=["ps%d" % kb])
                P.op("pe", lambda h, kb=kb, w=w, kcols=kcols, tcols=tcols, hb=hb: h.matmul(K.ps[kb][:, 0:w], lhsT=qs[hb][:, tcols], rhs=K.KT[:, 2, kcols], start=False, stop=True), reads=[qk + "s", "KTr.0"], writes=["ps%d" % kb])
            db = (idx * 128) // 512
            do = (idx * 128) % 512
            P.op("dve", lambda h, db=db, do=do: h.tensor_tensor(out=K.ps[db][:, do:do + 128], in0=K.ps[db][:, do:do + 128], in1=K.cm[:], op=ALU.add), reads=["ps%d" % db, "cm"], writes=["ps%d" % db])
            n0 = min(nkeys, 1024)
            n1 = nkeys - n0
            smt = sm[bi]
            mx, nb_, l0, l1, rl = smt[:, 0:1], smt[:, 2:3], smt[:, 3:4], smt[:, 4:5], smt[:, 5:6]
            P.op("dve", lambda h, n0=n0, mx=mx: h.tensor_reduce(out=mx, in_=K.pp[0][:, 0:n0], axis=AX.X, op=ALU.max), reads=["ps0", "ps1"], writes=[tag + "sm"])
            if n1 > 0:
                m1 = smt[:, 1:2]
                P.op("dve", lambda h, n1=n1, m1=m1: h.tensor_reduce(out=m1, in_=K.pp[1][:, 0:n1], axis=AX.X, op=ALU.max), reads=["ps2", "ps3", tag + "sm"], writes=[tag + "sm"])
                P.op("dve", lambda h, mx=mx, m1=m1: h.tensor_tensor(out=mx, in0=mx, in1=m1, op=ALU.max), reads=[tag + "sm"], writes=[tag + "sm"])
            P.op("dve", lambda h, mx=mx, nb_=nb_: h.tensor_scalar(out=nb_, in0=mx, scalar1=-MLA_SCALE, scalar2=None, op0=ALU.mult), reads=[tag + "sm"], writes=[tag + "sm"])
            P.op("act", lambda h, n0=n0, bi=bi, nb_=nb_, l0=l0: h.activation(out=pb[bi][:, 0:n0], in_=K.pp[0][:, 0:n0], func=AF.Exp, scale=MLA_SCALE, bias=nb_, accum_out=l0), reads=["ps0", "ps1", tag + "sm"], writes=[tag + "p", tag + "l"])
            if n1 > 0:
                P.op("act", lambda h, n0=n0, n1=n1, bi=bi, nb_=nb_, l1=l1: h.activation(out=pb[bi][:, n0:n0 + n1], in_=K.pp[1][:, 0:n1], func=AF.Exp, scale=MLA_SCALE, bias=nb_, accum_out=l1), reads=["ps2", "ps3", tag + "sm", tag + "l"], writes=[tag + "p", tag + "l"])
                P.op("dve", lambda h, l0=l0, l1=l1: h.tensor_tensor(out=l0, in0=l0, in1=l1, op=ALU.add), reads=[tag + "l", tag + "sm"], writes=[tag + "l"])
            P.op("dve", lambda h, l0=l0, rl=rl: h.reciprocal(out=rl, in_=l0), reads=[tag + "l", tag + "sm"], writes=[tag + "sm"])
            P.op("dve", lambda h, bi=bi, rl=rl: h.tensor_scalar(out=Dm[bi], in0=K.ident[:], scalar1=rl, scalar2=None, op0=ALU.mult), reads=["ident", tag + "sm"], writes=[tag + "D"])
            for kt in range(nk):
                P.op("pe", lambda h, kt=kt, bi=bi: h.matmul(K.pp[2 + kt // 8][:, (kt % 8) * 128:(kt % 8 + 1) * 128], lhsT=pb[bi][:, kt * 128:(kt + 1) * 128], rhs=Dm[bi], start=True, stop=True),
                     reads=[tag + "p", tag + "D"], writes=["ps%d" % (4 + 2 * (kt // 8) + (kt % 8) // 4)])
            na = min(nk, 8)
            evac(P, "act", pT[bi][:, 0:na, :], K.pp[2][:, 0:na * 128].rearrange("p (a b) -> p a b", b=128), ["ps4", "ps5"], [tag + "pTa"])
            if nk > 8:
                evac(P, "dve", pT[bi][:, 8:nk, :], K.pp[3][:, 0:(nk - 8) * 128].rearrange("p (a b) -> p a b", b=128), ["ps6", "ps7"], [tag + "pTb"])
            for rc in range(2):
                for kt in range(nk):
                    P.op("pe", lambda h, rc=rc, kt=kt, bi=bi: h.matmul(K.ps[0][:, rc * 128:(rc + 1) * 128], lhsT=K.Vp[:, kt, rc * 128:(rc + 1) * 128], rhs=pT[bi][:, kt, :], start=(kt == 0), stop=(kt == nk - 1)),
                         reads=["Vp", tag + "pTa", tag + "pTb"], writes=["ps0"])
            evac(P, "act", olT[bi], K.ps[0][:, 0:256].rearrange("p (a b) -> p a b", b=128), ["ps0"], [tag + "ol"])
            for rc in range(2):
                P.op("pe", lambda h, rc=rc, bi=bi, hh=hh: h.matmul(K.ps[1][:, 0:128], lhsT=wuv[:, rc, hh * 128:(hh + 1) * 128], rhs=olT[bi][:, rc, :], start=(rc == 0), stop=(rc == 1)), reads=[wuvk, tag + "ol"], writes=["ps1"])
            evac(P, "dve", oT[:, hh, tcols], K.ps[1][:, 0:128], ["ps1"], ["oT.%d" % li])

    if has_s:
        mla_sample(K, C, locals())

    g_post, gpok = K.load_gain(I["norm_mix_post"][1])
    for li in range(NT):
        tcols = slice(li * 128, (li + 1) * 128)
        pair = 2 + li % 2
        for nb in range(2):
            for hh in range(8):
                P.op("pe", lambda h, nb=nb, hh=hh, tcols=tcols, pair=pair: h.matmul(K.pp[pair][:, nb * 512:(nb + 1) * 512], lhsT=oT[:, hh, tcols], rhs=wo[:, hh, nb * 512:(nb + 1) * 512], start=(hh == 0), stop=(hh == 7)),
                     reads=["oT.%d" % li, wok], writes=["ps%d" % (2 * pair + nb)])
        K.post_norm_add(li, K.pp[pair][:, :], ["ps%d" % (2 * pair), "ps%d" % (2 * pair + 1)], g_post, gpok)
    P.barrier()


def mla_sample(K, C, env):
    P, I, O = K.P, K.I, K.O
    qSl, qSr, KTn, Vn, olS, oT, wuv, wuvk = (env[k] for k in ("qSl", "qSr", "KTn", "Vn", "olS", "oT", "wuv", "wuvk"))
    tp, TOK = env["tp"], env["TOK"]
    P.barrier()
    C.off = env["base"]
    stgK = C(2048)
    stgR = C(512)
    krd = K.arena[:, env["off_wqr"]:env["off_wqr"] + 1024].rearrange("p (a b) -> p a b", b=128)
    Vb = K.arena[:, env["off_wukT"]:env["off_wukT"] + 1024].bitcast(BF16).rearrange("p (a b) -> p a b", b=256)
    KTb = C(1536, BF16, [3, 1024])
    pS = C(512, BF16)
    pTs = C(256, BF16, [8, 64])
    Oacc = C(256)
    On = C(256)
    mS = C(16)
    maskb = C(128)
    ptf = stgK[:, 0:1024]
    tmpx = stgK[:, 1024:2048]
    idx = C(128).bitcast(I32)
    Msel = C(8)
    pm = C(2)
    pti = tmpx.bitcast(I32)

    P.dma("sp", lambda h: h.dma_start(out=pti, in_=I["pt"].rearrange("b (o n) -> o (b n)", o=1).to_broadcast([128, 1024])), writes=["tmpx"])
    P.op("dve", lambda h: h.tensor_copy(out=ptf, in_=pti), reads=["tmpx"], writes=["ptf"])
    P.op("pool", lambda h: h.memset(Msel, 1.0), writes=["Msel"])
    P.op("pool", lambda h: h.affine_select(out=Msel, in_=Msel, pattern=[[-16, 8]], compare_op=ALU.is_ge, fill=0.0, base=0, channel_multiplier=1), reads=["Msel"], writes=["Msel"])
    P.op("pool", lambda h: h.affine_select(out=Msel, in_=Msel, pattern=[[16, 8]], compare_op=ALU.is_ge, fill=0.0, base=15, channel_multiplier=-1), reads=["Msel"], writes=["Msel"])
    P.op("dve", lambda h: h.tensor_tensor(out=tmpx.rearrange("p (a j) -> p a j", j=8), in0=ptf.rearrange("p (a j) -> p a j", j=8),
                                          in1=Msel.unsqueeze(1).to_broadcast([128, 128, 8]), op=ALU.mult), reads=["ptf", "Msel", "tmpx"], writes=["tmpx"])
    P.op("dve", lambda h: h.tensor_reduce(out=ptf[:, 0:128], in_=tmpx.rearrange("p (a j) -> p a j", j=8), axis=AX.X, op=ALU.add), reads=["tmpx", "ptf"], writes=["ptf"])
    pmi = pm.bitcast(I32)
    P.op("pool", lambda h: h.iota(pmi[:, 0:1], pattern=[[0, 1]], base=0, channel_multiplier=1), writes=["pm"])
    P.op("dve", lambda h: h.tensor_single_scalar(out=pmi[:, 1:2], in_=pmi[:, 0:1], scalar=15, op=ALU.bitwise_and), reads=["pm"], writes=["pm"])
    P.op("dve", lambda h: h.tensor_copy(out=pm[:, 0:1], in_=pmi[:, 1:2]), reads=["pm"], writes=["pm"])
    P.op("dve", lambda h: h.tensor_scalar(out=idx, in0=ptf[:, 0:128], scalar1=16.0, scalar2=pm[:, 0:1], op0=ALU.mult, op1=ALU.add), reads=["ptf", "pm"], writes=["idx"])

    P.barrier()
    ident64 = K.ident[0:64, 0:64]
    for b in range(16):
        qcols = slice(b * 8, (b + 1) * 8)
        lq = [qSl[:, rc, qcols, :].rearrange("p t h -> p (t h)") for rc in range(2)]
        rq = qSr[:, qcols, :].rearrange("p t h -> p (t h)")
        m_, l_, bm, mn, corr, nb_, ls = (mS[0:64, i:i + 1] for i in range(7))
        P.op("dve", lambda h, m_=m_: h.memset(m_, NEG), reads=["mS"], writes=["mS"])
        P.op("dve", lambda h, l_=l_: h.memset(l_, 0.0), reads=["mS"], writes=["mS"])
        P.op("pool", lambda h: h.memset(Oacc[0:64, :], 0.0), reads=["Oacc"], writes=["Oacc"])
        P.op("pool", lambda h: h.memset(maskb[0:64, :], 0.0), reads=["maskb"], writes=["maskb"])
        P.op("pool", lambda h, b=b: h.affine_select(out=maskb[0:64, :], in_=maskb[0:64, :], pattern=[[1, 128]], compare_op=ALU.is_ge, fill=NEG, base=-8 * b, channel_multiplier=0), reads=["maskb"], writes=["maskb"])
        P.op("pool", lambda h, b=b: h.affine_select(out=maskb[0:64, :], in_=maskb[0:64, :], pattern=[[-8, 128]], compare_op=ALU.is_ge, fill=NEG, base=64 * b, channel_multiplier=1), reads=["maskb"], writes=["maskb"])
        for d in range(9):
            if d < 8:
                nkeys = 1024
                col = b * 8 + d
                P.dma("pool", lambda h, col=col: h.indirect_dma_start(out=stgK, out_offset=None, in_=I["ckvc"], in_offset=bass.IndirectOffsetOnAxis(ap=idx[:, col:col + 1], axis=0)), reads=["idx"], writes=["stgK"])
                P.dma("pool", lambda h, col=col: h.indirect_dma_start(out=stgR, out_offset=None, in_=I["krc"], in_offset=bass.IndirectOffsetOnAxis(ap=idx[:, col:col + 1], axis=0)), reads=["idx"], writes=["stgR"])
                P.op("act", lambda h: h.copy(out=Vb, in_=stgK.rearrange("p (s r) -> p s r", r=256)), reads=["stgK"], writes=["Vb"])
                P.op("pool", lambda h: h.tensor_copy(out=krd[:, :, 0:64], in_=stgR.rearrange("p (s r) -> p s r", r=64)), reads=["stgR"], writes=["krd"])
                P.op("pool", lambda h: h.tensor_copy(out=krd[:, :, 64:128], in_=stgR.rearrange("p (s r) -> p s r", r=64)), reads=["stgR", "krd"], writes=["krd"])
                for half in range(2):
                    for sl in range(4):
                        s_ = half * 4 + sl
                        for c in range(3):
                            src = stgK[:, s_ * 256 + c * 128:s_ * 256 + (c + 1) * 128] if c < 2 else krd[:, s_, :]
                            P.op("pe", lambda h, src=src, c=c, sl=sl: h.transpose(out=K.ps[2 + c][:, sl * 128:(sl + 1) * 128], in_=src, identity=K.identf[:]),
                                 reads=["stgK" if c < 2 else "krd", "identf"], writes=["ps%d" % (2 + c)])
                    for c in range(3):
                        evac(P, ("act", "dve", "act")[c] if half == 0 else ("dve", "act", "dve")[c], KTb[:, c, half * 512:(half + 1) * 512], K.ps[2 + c][:, :], ["ps%d" % (2 + c)], ["KTb%d.%d" % (c, half)])
                    P.op("pe", lambda h, half=half, lq=lq: h.matmul(K.ps[half][0:64, :], lhsT=lq[0], rhs=KTb[:, 0, half * 512:(half + 1) * 512], start=True, stop=False), reads=["qSl", "KTb0.%d" % half], writes=["ps%d" % half])
                    P.op("pe", lambda h, half=half, lq=lq: h.matmul(K.ps[half][0:64, :], lhsT=lq[1], rhs=KTb[:, 1, half * 512:(half + 1) * 512], start=False, stop=False), reads=["qSl", "KTb1.%d" % half], writes=["ps%d" % half])
                    P.op("pe", lambda h, half=half, rq=rq: h.matmul(K.ps[half][0:64, :], lhsT=rq, rhs=KTb[:, 2, half * 512:(half + 1) * 512], start=False, stop=True), reads=["qSr", "KTb2.%d" % half], writes=["ps%d" % half])
                vsrc = lambda s_: Vb[:, s_, :]
                vkey = "Vb"
                nslot = 8
            else:
                nkeys = 128
                P.op("pe", lambda h, lq=lq: h.matmul(K.ps[0][0:64, 0:128], lhsT=lq[0], rhs=KTn[:, 0, :], start=True, stop=False), reads=["qSl", "KTn.0"], writes=["ps0"])
                P.op("pe", lambda h, lq=lq: h.matmul(K.ps[0][0:64, 0:128], lhsT=lq[1], rhs=KTn[:, 1, :], start=False, stop=False), reads=["qSl", "KTn.1"], writes=["ps0"])
                P.op("pe", lambda h, rq=rq: h.matmul(K.ps[0][0:64, 0:128], lhsT=rq, rhs=KTn[:, 2, :], start=False, stop=True), reads=["qSr", "KTnr.0"], writes=["ps0"])
                P.op("dve", lambda h: h.tensor_tensor(out=K.ps[0][0:64, 0:128], in0=K.ps[0][0:64, 0:128], in1=maskb[0:64, :], op=ALU.add), reads=["ps0", "maskb"], writes=["ps0"])
                vsrc = lambda s_: Vn
                vkey = "Vn"
                nslot = 1
            S = K.pp[0][0:64, 0:nkeys]
            sk = ["ps0", "ps1"] if nkeys > 512 else ["ps0"]
            P.op("dve", lambda h, S=S, bm=bm: h.tensor_reduce(out=bm, in_=S, axis=AX.X, op=ALU.max), reads=sk + ["mS"], writes=["mS"])
            P.op("dve", lambda h, mn=mn, m_=m_, bm=bm: h.tensor_tensor(out=mn, in0=m_, in1=bm, op=ALU.max), reads=["mS"], writes=["mS"])
            P.op("dve", lambda h, corr=corr, m_=m_, mn=mn: h.tensor_tensor(out=corr, in0=m_, in1=mn, op=ALU.subtract), reads=["mS"], writes=["mS"])
            P.op("act", lambda h, corr=corr: h.activation(out=corr, in_=corr, func=AF.Exp, scale=MLA_SCALE), reads=["mS"], writes=["mS"])
            P.op("dve", lambda h, nb_=nb_, mn=mn: h.tensor_scalar(out=nb_, in0=mn, scalar1=-MLA_SCALE, scalar2=None, op0=ALU.mult), reads=["mS"], writes=["mS"])
            P.op("act", lambda h, S=S, nb_=nb_, ls=ls, nkeys=nkeys: h.activation(out=pS[0:64, 0:nkeys], in_=S, func=AF.Exp, scale=MLA_SCALE, bias=nb_, accum_out=ls), reads=sk + ["mS"], writes=["pS", "mS"])
            P.op("dve", lambda h, l_=l_, corr=corr, ls=ls: h.scalar_tensor_tensor(out=l_, in0=l_, scalar=corr, in1=ls, op0=ALU.mult, op1=ALU.add), reads=["mS"], writes=["mS"])
            P.op("dve", lambda h, m_=m_, mn=mn: h.tensor_copy(out=m_, in_=mn), reads=["mS"], writes=["mS"])
            for s_ in range(nslot):
                P.op("pe", lambda h, s_=s_: h.matmul(K.ps[5][:, s_ * 64:(s_ + 1) * 64], lhsT=pS[0:64, s_ * 128:(s_ + 1) * 128], rhs=ident64, start=True, stop=True), reads=["pS", "ident"], writes=["ps5"])
            evac(P, "act", pTs[:, 0:nslot, :], K.ps[5][:, 0:nslot * 64].rearrange("p (a b) -> p a b", b=64), ["ps5"], ["pTs"])
            for s_ in range(nslot):
                P.op("pe", lambda h, s_=s_, vsrc=vsrc, nslot=nslot: h.matmul(K.ps[6][0:64, 0:256], lhsT=pTs[:, s_, :], rhs=vsrc(s_), start=(s_ == 0), stop=(s_ == nslot - 1)), reads=["pTs", vkey], writes=["ps6"])
            P.op("dve", lambda h, corr=corr: h.scalar_tensor_tensor(out=Oacc[0:64, :], in0=Oacc[0:64, :], scalar=corr, in1=K.ps[6][0:64, 0:256], op0=ALU.mult, op1=ALU.add), reads=["Oacc", "ps6", "mS"], writes=["Oacc"])
        rl = mS[0:64, 8:9]
        P.op("dve", lambda h, rl=rl, l_=l_: h.reciprocal(out=rl, in_=l_), reads=["mS"], writes=["mS"])
        P.op("dve", lambda h, rl=rl: h.tensor_scalar(out=On[0:64, :], in0=Oacc[0:64, :], scalar1=rl, scalar2=None, op0=ALU.mult), reads=["Oacc", "mS"], writes=["On"])
        for rc in range(2):
            P.op("pe", lambda h, rc=rc: h.transpose(out=K.ps[7][:, rc * 64:(rc + 1) * 64], in_=On[0:64, rc * 128:(rc + 1) * 128], identity=K.identf[0:64, 0:64]), reads=["On", "identf"], writes=["ps7"])
        evac(P, "act", olS[:, :, qcols, :].rearrange("p c t h -> p c (t h)"), K.ps[7][:, 0:128].rearrange("p (c x) -> p c x", x=64), ["ps7"], ["olS"])
    for hh in range(8):
        for rc in range(2):
            P.op("pe", lambda h, hh=hh, rc=rc: h.matmul(K.ps[1][:, 0:128], lhsT=wuv[:, rc, hh * 128:(hh + 1) * 128], rhs=olS[:, rc, :, hh], start=(rc == 0), stop=(rc == 1)), reads=[wuvk, "olS"], writes=["ps1"])
        evac(P, "dve", oT[:, hh, tp:TOK], K.ps[1][:, 0:128], ["ps1"], ["oT.%d" % (K.ntile - 1)])


def _prep_inputs(inputs, c):
    m = {}
    m["xp"] = np.ascontiguousarray(inputs["x_prompt"][c])
    m["xs"] = np.ascontiguousarray(inputs["x_sample"][16 * c:16 * c + 16].reshape(128, D))
    m["ssr"] = np.ascontiguousarray(inputs["cache_ssm_re"][0, 16 * c:16 * c + 16].reshape(16, 4096))
    m["ssi"] = np.ascontiguousarray(inputs["cache_ssm_im"][0, 16 * c:16 * c + 16].reshape(16, 4096))
    m["ckvc"] = inputs["cache_kv_latent"].reshape(N_PHYS * 16, 2048)
    m["krc"] = inputs["cache_k_rope"].reshape(N_PHYS * 16, 512)
    m["memk"] = np.ascontiguousarray(inputs["cache_mem_k"][:, 16 * c:16 * c + 16].reshape(2, 16, 256, D))
    m["memv"] = np.ascontiguousarray(inputs["cache_mem_v"][:, 16 * c:16 * c + 16].reshape(2, 16, 256, D))
    m["pt"] = np.ascontiguousarray(inputs["page_table"][16 * c:16 * c + 16]).astype(np.int32)
    m["memp"] = np.ascontiguousarray(inputs["mem_prompt"][c])
    for name, shape in WEIGHT_SPECS:
        m[name] = np.ascontiguousarray(inputs[name]).reshape(shape)
    return m


_NC_CACHE = {}


def kernel(**inputs):
    inputs = {k: np.asarray(v) for k, v in inputs.items()}
    if "nc" not in _NC_CACHE:
        _NC_CACHE["nc"] = build()
    nc = _NC_CACHE["nc"]
    in_maps = [_prep_inputs(inputs, c) for c in range(NCORES)]
    res = run_bass_kernel_spmd(nc, in_maps, core_ids=list(range(NCORES)))
    R = res.results
    f = np.float32
    y_p = np.stack([R[c]["y_p"] for c in range(NCORES)]).astype(f)
    y_s = np.concatenate([R[c]["y_s"].reshape(16, 8, D) for c in range(NCORES)]).astype(f)
    ssm_re_p = np.stack([R[c]["ssm_re_p"].reshape(64, 64) for c in range(NCORES)])[None].astype(f)
    ssm_im_p = np.stack([R[c]["ssm_im_p"].reshape(64, 64) for c in range(NCORES)])[None].astype(f)
    ssm_re_s = np.concatenate([R[c]["ssm_re_s"].reshape(16, 64, 64) for c in range(NCORES)])[None].astype(f)
    ssm_im_s = np.concatenate([R[c]["ssm_im_s"].reshape(16, 64, 64) for c in range(NCORES)])[None].astype(f)
    kvl_p = np.stack([R[c]["kvl_p"] for c in range(NCORES)]).astype(f)
    kr_p = np.stack([R[c]["kr_p"] for c in range(NCORES)]).astype(f)
    kvl_s = np.concatenate([R[c]["kvl_s"].reshape(16, 8, 256) for c in range(NCORES)]).astype(f)
    kr_s = np.concatenate([R[c]["kr_s"].reshape(16, 8, 64) for c in range(NCORES)]).astype(f)
    memk_p = np.stack([R[c]["memk_p"].reshape(2, 256, 4, 256) for c in range(NCORES)], axis=1).astype(f)
    memv_p = np.stack([R[c]["memv_p"].reshape(2, 256, 4, 256) for c in range(NCORES)], axis=1).astype(f)
    return (y_p, y_s, ssm_re_p, ssm_im_p, ssm_re_s, ssm_im_s, kvl_p, kr_p, kvl_s, kr_s, memk_p, memv_p)
```

```python
import math
from contextlib import ExitStack
import numpy as np
import concourse.bass as bass
import concourse.mybir as mybir
from concourse.bass_utils import run_bass_kernel_spmd

F32 = mybir.dt.float32
BF16 = mybir.dt.bfloat16
I32 = mybir.dt.int32
AF = mybir.ActivationFunctionType
ALU = mybir.AluOpType
AX = mybir.AxisListType

NCORES = 8
D = 1024
SEQ = 2048
NPT = 16
NG = 64
NS = 64
EPS = 1e-6
TWO_PI = 2.0 * math.pi
NEG = -30000.0
MEM_SCALE = 256 ** -0.5
MLA_SCALE = 192 ** -0.5
PAST = 8192
N_PHYS = 10240

GROUPS = [[("P", i) for i in range(0, 6)],
          [("P", i) for i in range(6, 12)],
          [("P", i) for i in range(12, 16)] + [("S", 0)]]
MAXT = 6
ARENA = 19712


class Prog:
    ENG = ("pe", "act", "dve", "pool", "sp")

    def __init__(self, nc, stack, n_dma_sems=48, same_engine_sync=True):
        self.nc = nc
        self.h = {"pe": nc.tensor, "act": nc.scalar, "dve": nc.vector,
                  "pool": nc.gpsimd, "sp": nc.sync}
        self.stream = {e: [] for e in self.ENG}
        self.sem = {e: stack.enter_context(nc.semaphore("s_" + e)) for e in self.ENG}
        self.cnt = {e: 0 for e in self.ENG}
        self.dsem = [stack.enter_context(nc.semaphore("d%d" % i)) for i in range(n_dma_sems)]
        self.dcnt = [0] * n_dma_sems
        self.dnext = 0
        self.waited = {e: {} for e in self.ENG}
        self.buf = {}
        self.same = same_engine_sync
        self.out_tokens = []
        self.nops = 0

    def _deps(self, eng, reads, writes):
        toks = []
        for k in reads:
            st = self.buf.get(k)
            if st and st[0] is not None:
                toks.append(st[0])
        for k in writes:
            st = self.buf.get(k)
            if st:
                if st[0] is not None:
                    toks.append(st[0])
                toks.extend(st[1])
        need = {}
        for (sem, val, src) in toks:
            if src == eng and (eng == "pe" or not self.same):
                continue
            key = id(sem)
            if self.waited[eng].get(key, 0) >= val:
                continue
            if key not in need or need[key][1] < val:
                need[key] = (sem, val)
        for key, (sem, val) in need.items():
            self.waited[eng][key] = val
        return list(need.values())

    def _commit(self, tok, reads, writes):
        for k in reads:
            st = self.buf.setdefault(k, [None, []])
            st[1].append(tok)
        for k in writes:
            self.buf[k] = [tok, []]

    def op(self, eng, fn, reads=(), writes=()):
        waits = self._deps(eng, reads, writes)
        sem = self.sem[eng]
        self.cnt[eng] += 1
        val = self.cnt[eng]
        self.nops += 1

        def emit(h, waits=waits, fn=fn, sem=sem):
            for (s, v) in waits:
                h.wait_ge(s, v)
            fn(h).then_inc(sem, 1)
        self.stream[eng].append(emit)
        tok = (sem, val, eng)
        self._commit(tok, reads, writes)
        return tok

    def dma(self, q, fn, reads=(), writes=(), is_output=False):
        waits = self._deps(q, reads, writes)
        i = self.dnext
        self.dnext = (self.dnext + 1) % len(self.dsem)
        sem = self.dsem[i]
        prev = self.dcnt[i]
        if prev > 0 and self.waited[q].get(id(sem), 0) < prev:
            waits.append((sem, prev))
            self.waited[q][id(sem)] = prev
        self.dcnt[i] += 16
        val = self.dcnt[i]
        self.nops += 1

        def emit(h, waits=waits, fn=fn, sem=sem):
            for (s, v) in waits:
                h.wait_ge(s, v)
            fn(h).then_inc(sem, 16)
        self.stream[q].append(emit)
        tok = (sem, val, "dma")
        self._commit(tok, reads, writes)
        if is_output:
            self.out_tokens.append(tok)
        return tok

    def barrier(self):
        targets = [(self.sem[e], self.cnt[e]) for e in self.ENG if self.cnt[e] > 0]
        targets += [(self.dsem[i], self.dcnt[i]) for i in range(len(self.dsem)) if self.dcnt[i] > 0]
        for e in self.ENG:
            ws = []
            for (s, v) in targets:
                if s is self.sem[e]:
                    continue
                if self.waited[e].get(id(s), 0) >= v:
                    continue
                self.waited[e][id(s)] = v
                ws.append((s, v))

            def emit(h, ws=ws):
                for (s, v) in ws:
                    h.wait_ge(s, v)
            self.stream[e].append(emit)
        self.buf = {}

    def finish(self):
        need = {}
        for (sem, val, _) in self.out_tokens:
            k = id(sem)
            if k not in need or need[k][1] < val:
                need[k] = (sem, val)
        waits = list(need.values())

        def emit(h, waits=waits):
            for (s, v) in waits:
                h.wait_ge(s, v)
        self.stream["sp"].append(emit)
        streams = self.stream
        with self.nc.Block() as block:
            @block.tensor
            def _(e):
                for f in streams["pe"]:
                    f(e)

            @block.scalar
            def _(e):
                for f in streams["act"]:
                    f(e)

            @block.vector
            def _(e):
                for f in streams["dve"]:
                    f(e)

            @block.gpsimd
            def _(e):
                for f in streams["pool"]:
                    f(e)

            @block.sync
            def _(e):
                for f in streams["sp"]:
                    f(e)


WEIGHT_SPECS = [
    ("norm_mix_pre", [2, D]), ("norm_mix_post", [2, D]), ("norm_mem_pre", [2, D]),
    ("norm_mem_post", [2, D]), ("norm_mlp_pre", [2, D]), ("norm_mlp_post", [2, D]),
    ("mem_in_norm", [2, D]), ("w_mem_q", [2, D, D]), ("w_mem_k", [2, D, D]),
    ("w_mem_v", [2, D, D]), ("w_mem_o", [2, D, D]), ("w_mlp_up", [2, D, 4 * D]),
    ("w_mlp_down", [2, 4 * D, D]), ("ssm_a_re", [1, 64, 64]), ("ssm_a_im", [1, 64, 64]),
    ("ssm_log_dt", [1, 64]), ("ssm_b_re", [1, 64, 64, 16]), ("ssm_b_im", [1, 64, 64, 16]),
    ("ssm_c_re", [1, 64, 16, 64]), ("ssm_c_im", [1, 64, 16, 64]), ("ssm_d", [1, D]),
    ("w_glu", [1, D, 2 * D]), ("kv_in_norm", [D]), ("w_dkv", [D, 256]),
    ("kv_latent_norm", [256]), ("w_kr", [D, 64]), ("w_uk", [256, 8, 128]),
    ("w_uv", [256, 8, 128]), ("w_dq", [1, D, 384]), ("q_norm", [1, 384]),
    ("w_uq", [1, 384, 8, 192]), ("w_o", [1, D, D]),
]

IN_SPECS = [
    ("xp", [SEQ, D], F32), ("xs", [128, D], F32), ("ssr", [16, 4096], F32), ("ssi", [16, 4096], F32),
    ("ckvc", [N_PHYS * 16, 2048], F32), ("krc", [N_PHYS * 16, 512], F32),
    ("memk", [2, 16, 256, D], F32), ("memv", [2, 16, 256, D], F32), ("pt", [16, 64], I32),
    ("memp", [256, D], F32),
]

OUT_SPECS = [
    ("y_p", [SEQ, D]), ("y_s", [128, D]), ("ssm_re_p", [1, 4096]), ("ssm_im_p", [1, 4096]),
    ("ssm_re_s", [16, 4096]), ("ssm_im_s", [16, 4096]), ("kvl_p", [SEQ, 256]), ("kr_p", [SEQ, 64]),
    ("kvl_s", [128, 256]), ("kr_s", [128, 64]), ("memk_p", [2, 256, D]), ("memv_p", [2, 256, D]),
]


def build(stage=99, dbg_shape=None):
    nc = bass.Bass("TRN2", target_bir_lowering=False)
    I = {}
    for name, shape, dt in IN_SPECS:
        I[name] = nc.dram_tensor(name, shape, dt, kind="ExternalInput").ap()
    for name, shape in WEIGHT_SPECS:
        I[name] = nc.dram_tensor(name, shape, F32, kind="ExternalInput").ap()
    O = {}
    for name, shape in OUT_SPECS:
        O[name] = nc.dram_tensor(name, shape, F32, kind="ExternalOutput").ap()
    if dbg_shape is not None:
        O["dbg"] = nc.dram_tensor("dbg", dbg_shape, F32, kind="ExternalOutput").ap()

    with ExitStack() as st:
        import os
        P = Prog(nc, st, same_engine_sync=(os.environ.get('KSAME', '1') == '1'))
        K = Kern(nc, st, P, I, O, stage)
        K.run()
        P.finish()
    return nc


class Kern:
    def __init__(self, nc, st, P, I, O, stage):
        self.nc, self.st, self.P, self.I, self.O, self.stage = nc, st, P, I, O, stage
        self.uid = 0
        sb = self.sb
        self.h = sb("h", [128, MAXT, D], F32)
        self.xnT = sb("xnT", [128, 8, MAXT * 128], BF16)
        self.ident = sb("ident", [128, 128], BF16)
        self.identf = sb("identf", [128, 128], F32)
        self.gbc = [sb("gbc%d" % i, [128, D], F32) for i in range(2)]
        self.gbc_i = 0
        self.wslot = [sb("wslot%d" % i, [128, 8, 1024], BF16) for i in range(3)]
        self.ws_i = 0
        self.small = sb("small", [128, 64], F32)
        self.small_i = 0
        self.junk = sb("junk", [128, D], BF16)
        self.xn = sb("xn", [128, D], F32)
        self.iota_g = sb("iota_g", [128, MAXT * 128], F32)
        self.halfpi = sb("halfpi", [128, 1], F32)
        self.s5_carry = sb("s5_carry", [128, 32, 2], F32)
        self.arena = sb("arena", [128, ARENA], F32)
        self.KT = sb("KT", [128, 3, SEQ], BF16)
        self.Vp = sb("Vp", [128, NPT, 256], BF16)
        self.cm = sb("cm", [128, 128], F32)
        self.invf = sb("invf", [128, 2], F32)
        self.pp = [st.enter_context(nc.psum_tensor("pp%d" % i, [128, 1024], F32)) for i in range(4)]
        self.ps = [self.pp[i // 2][:, (i % 2) * 512:(i % 2 + 1) * 512] for i in range(8)]

    def sb(self, name, shape, dt):
        return self.st.enter_context(self.nc.sbuf_tensor(name, shape, dt))

    def key(self, base):
        self.uid += 1
        return "%s#%d" % (base, self.uid)

    def scal(self):
        i = self.small_i
        self.small_i = (self.small_i + 1) % 64
        return self.small[:, i:i + 1], "small%d" % i

    def load_gain(self, vec_ap):
        i = self.gbc_i
        self.gbc_i = (self.gbc_i + 1) % len(self.gbc)
        t = self.gbc[i]
        k = "gbc%d" % i
        src = vec_ap.rearrange("(o n) -> o n", o=1).to_broadcast([128, D])
        self.P.dma("sp", lambda h: h.dma_start(out=t[:], in_=src), writes=[k])
        return t, k

    def load_w(self, src_ap, ncols, nk=8):
        i = self.ws_i
        self.ws_i = (self.ws_i + 1) % len(self.wslot)
        t = self.wslot[i]
        k = "wslot%d" % i
        view = t[:].rearrange("p a b -> p (a b)")[:, 0:nk * ncols].rearrange("p (a b) -> p a b", b=ncols)
        src = src_ap.rearrange("(k p) n -> p k n", p=128)
        self.P.dma("pool", lambda h: h.dma_start(out=view, in_=src), writes=[k])
        return view, k

    def rms_stats(self, src, skey, n):
        P = self.P
        ssq, k1 = self.scal()
        rs, k2 = self.scal()
        P.op("act", lambda h: h.activation(out=self.junk[:, 0:n], in_=src, func=AF.Square, accum_out=ssq),
             reads=skey, writes=["junk", k1])
        P.op("dve", lambda h: h.tensor_scalar(out=rs, in0=ssq, scalar1=1.0 / n, scalar2=EPS, op0=ALU.mult, op1=ALU.add),
             reads=[k1], writes=[k2])
        P.op("act", lambda h: h.activation(out=rs, in_=rs, func=AF.Sqrt), reads=[k2], writes=[k2])
        P.op("dve", lambda h: h.reciprocal(out=rs, in_=rs), reads=[k2], writes=[k2])
        return rs, k2

    def norm_to_xnT(self, li, gain, gkey, col0):
        P = self.P
        hk = "h%d" % li
        rs, rk = self.rms_stats(self.h[:, li, :], [hk], D)
        P.op("dve", lambda h: h.scalar_tensor_tensor(out=self.xn[:], in0=self.h[:, li, :], scalar=rs, in1=gain[:],
                                                      op0=ALU.mult, op1=ALU.mult),
             reads=[hk, rk, gkey], writes=["xn"])
        self.transpose_to(self.xn, "xn", self.xnT, "xnT", col0)

    def transpose_to(self, src, skey, dstT, dkey, col0, nk=8, banks=(0, 1)):
        P = self.P
        for half in range((nk + 3) // 4):
            b = self.ps[banks[half % len(banks)]]
            bk = "ps%d" % banks[half % len(banks)]
            kk = range(half * 4, min(nk, half * 4 + 4))
            for k in kk:
                P.op("pe", lambda h, k=k, b=b: h.transpose(out=b[:, (k % 4) * 128:(k % 4 + 1) * 128],
                                                           in_=src[:, k * 128:(k + 1) * 128], identity=self.identf[:]),
                     reads=[skey, "identf"], writes=[bk])
            n = len(kk)
            eng = "act" if half % 2 == 0 else "dve"
            outv = dstT[:, half * 4:half * 4 + n, col0:col0 + 128]
            inv = b[:, 0:n * 128].rearrange("p (a b) -> p a b", b=128)
            if eng == "act":
                P.op("act", lambda h, outv=outv, inv=inv: h.copy(out=outv, in_=inv), reads=[bk],
                     writes=["%s.%d" % (dkey, k) for k in kk])
            else:
                P.op("dve", lambda h, outv=outv, inv=inv: h.tensor_copy(out=outv, in_=inv), reads=[bk],
                     writes=["%s.%d" % (dkey, k) for k in kk])

    def post_norm_add(self, li, src, skey, gain, gkey):
        P = self.P
        hk = "h%d" % li
        rs, rk = self.rms_stats(src, skey, D)
        P.op("dve", lambda h: h.scalar_tensor_tensor(out=self.xn[:], in0=src, scalar=rs, in1=gain[:],
                                                      op0=ALU.mult, op1=ALU.mult),
             reads=list(skey) + [rk, gkey], writes=["xn"])
        P.op("pool", lambda h: h.tensor_tensor(out=self.h[:, li, :], in0=self.h[:, li, :], in1=self.xn[:], op=ALU.add),
             reads=["xn", hk], writes=[hk])

    def setup_consts(self):
        P = self.P
        P.op("pool", lambda h: h.memset(self.identf[:], 0.0), writes=["identf"])
        P.op("pool", lambda h: h.affine_select(out=self.identf[:], in_=self.identf[:], pattern=[[-1, 128]],
                                               compare_op=ALU.not_equal, fill=1.0, base=0, channel_multiplier=1),
             reads=["identf"], writes=["identf"])
        P.op("dve", lambda h: h.tensor_copy(out=self.ident[:], in_=self.identf[:]), reads=["identf"], writes=["ident"])
        P.op("pool", lambda h: h.memset(self.halfpi[:], math.pi / 2), writes=["halfpi"])
        P.op("pool", lambda h: h.memset(self.cm[:], 0.0), writes=["cm"])
        P.op("pool", lambda h: h.affine_select(out=self.cm[:], in_=self.cm[:], pattern=[[-1, 128]], compare_op=ALU.is_ge, fill=NEG,
                                               base=0, channel_multiplier=1), reads=["cm"], writes=["cm"])
        iv = self.invf[:, 0:2].bitcast(I32)
        P.op("pool", lambda h: h.iota(iv[:, 0:1], pattern=[[0, 1]], base=0, channel_multiplier=1), writes=["invf"])
        P.op("dve", lambda h: h.tensor_single_scalar(out=iv[:, 1:2], in_=iv[:, 0:1], scalar=31, op=ALU.bitwise_and), reads=["invf"], writes=["invf"])
        P.op("dve", lambda h: h.tensor_copy(out=self.invf[:, 0:1], in_=iv[:, 1:2]), reads=["invf"], writes=["invf"])
        P.op("act", lambda h: h.activation(out=self.invf[:, 0:1], in_=self.invf[:, 0:1], func=AF.Exp, scale=-math.log(10000.0) / 32.0), reads=["invf"], writes=["invf"])
        P.op("dve", lambda h: h.tensor_scalar(out=self.invf[:, 0:1], in0=self.invf[:, 0:1], scalar1=1.0 / TWO_PI, scalar2=None, op0=ALU.mult), reads=["invf"], writes=["invf"])

    def run(self):
        P = self.P
        self.setup_consts()
        import os
        only = os.environ.get('KGROUPS')
        for gi, grp in enumerate(GROUPS):
            if only is not None and str(gi) not in only:
                continue
            self.grp = grp
            self.gi = gi
            self.ntile = len(grp)
            self.tok = 128 * len(grp)
            self.load_group()
            self.layer(0)
            if self.stage >= 4:
                self.layer(1)
            self.store_group()
        if "dbg" in self.O:
            pass

    def load_group(self):
        P = self.P
        ptiles = [idx for (kind, idx) in self.grp if kind == "P"]
        tp = 128 * len(ptiles)
        P.op("pool", lambda h: h.iota(self.iota_g[:, 0:tp], pattern=[[1, tp]], base=ptiles[0] * 128, channel_multiplier=0,
                                      allow_small_or_imprecise_dtypes=True), reads=["iota_g"], writes=["iota_g"])
        if len(ptiles) < len(self.grp):
            P.op("pool", lambda h: h.iota(self.iota_g[:, tp:tp + 128], pattern=[[0, 16], [1, 8]], base=0, channel_multiplier=0,
                                          allow_small_or_imprecise_dtypes=True), reads=["iota_g"], writes=["iota_g"])
        for li, (kind, idx) in enumerate(self.grp):
            src = self.I["xp"][idx * 128:(idx + 1) * 128, :] if kind == "P" else self.I["xs"][:, :]
            P.dma("sp", lambda h, li=li, src=src: h.dma_start(out=self.h[:, li, :], in_=src), writes=["h%d" % li])

    def store_group(self):
        P = self.P
        for li, (kind, idx) in enumerate(self.grp):
            dst = self.O["y_p"][idx * 128:(idx + 1) * 128, :] if kind == "P" else self.O["y_s"][:, :]
            P.dma("sp", lambda h, li=li, dst=dst: h.dma_start(out=dst, in_=self.h[:, li, :]), reads=["h%d" % li],
                  is_output=True)

    def layer(self, L):
        if L == 0:
            self.s5_mixer()
        else:
            self.mla_mixer()
        if self.stage >= 2:
            self.mem_attn(L)
        if self.stage >= 3:
            self.mlp(L)

    def s5_mixer(self):
        from_s5(self)

    def mem_attn(self, L):
        from_mem(self, L)

    def mlp(self, L):
        from_mlp(self, L)

    def mla_mixer(self):
        from_mla(self)


def from_s5(K):
    P, nc, I, O = K.P, K.nc, K.I, K.O
    A = K.arena
    off = [0]

    def carve(ncols, dt=F32, shape=None):
        a = off[0]
        off[0] += ncols
        v = A[:, a:a + ncols]
        if dt == BF16:
            v = v.bitcast(BF16)
        if shape is not None:
            names = "abc"[:len(shape)]
            kw = {names[i]: shape[i] for i in range(1, len(shape))}
            v = v.rearrange("p (%s) -> p %s" % (" ".join(names), " ".join(names)), **kw)
        return v

    BW = [carve(2048, BF16, [4, 8, 128]) for _ in range(2)]
    CW = [carve(2048, BF16, [4, 8, 128]) for _ in range(2)]
    sc = carve(32 * 16, F32, [16, 32])
    rows = carve(128 * 4, F32, [4, 128])
    msk = carve(8, F32)
    mski = carve(2, F32)
    dcol = carve(8)
    rmask = carve(128)
    rho_s = carve(128)
    ah0 = [carve(512, F32, [32, 16]) for _ in range(2)]
    fs = [carve(512, F32, [32, 16]) for _ in range(2)]
    fin = carve(64, F32, [32, 2])
    zt = carve(1024)
    r1 = off[0]
    Xre = carve(1024, F32, [32, 32])
    Xim = carve(1024, F32, [32, 32])
    T1 = carve(1024, F32, [32, 32])
    T2 = carve(1024, F32, [32, 32])
    raw = carve(1024, F32, [8, 128])
    Cl = carve(512, F32, [8, 64])
    endA = off[0]
    off[0] = r1
    h0stage = carve(4096)
    h0 = [carve(512, F32, [32, 16]) for _ in range(2)]
    endB = off[0]
    off[0] = r1
    CH = 256
    tabS = [carve(CH) for _ in range(2)]
    tabC = [carve(CH) for _ in range(2)]
    xi = [carve(CH).bitcast(I32) for _ in range(2)]
    rr = [carve(CH) for _ in range(2)]
    t1 = carve(CH)
    t2 = carve(CH)
    wre = [carve(CH) for _ in range(2)]
    wim = [carve(CH) for _ in range(2)]
    gre = [carve(CH) for _ in range(2)]
    gim = [carve(CH) for _ in range(2)]
    u1 = carve(CH)
    u2 = carve(CH)
    hre = [carve(CH // 2, BF16) for _ in range(2)]
    him = [carve(CH // 2, BF16) for _ in range(2)]
    yv = carve(MAXT * 128)
    v1 = carve(CH)
    v2 = carve(CH)
    endC = off[0]
    carry = K.s5_carry
    assert max(endA, endB, endC) <= ARENA, (endA, endB, endC)

    SC_AR, SC_AI, SC_LDT, SC_DT, SC_RHO, SC_FR, SC_ABR, SC_ABI, SC_CRE, SC_CIM, SC_T0, SC_T1, SC_T2, SC_T3 = range(14)

    def s(k):
        return sc[:, k, :]

    P.dma("sp", lambda h: h.dma_start(out=rows[0:32, 0, :], in_=I["ssm_a_re"][0].rearrange("(j g) n -> j (g n)", g=2)), writes=["rows"])
    P.dma("sp", lambda h: h.dma_start(out=rows[0:32, 1, :], in_=I["ssm_a_im"][0].rearrange("(j g) n -> j (g n)", g=2)), writes=["rows"])
    P.dma("sp", lambda h: h.dma_start(out=rows[0:32, 3, 0:2], in_=I["ssm_log_dt"][0].rearrange("(j g) -> j g", g=2)), writes=["rows"])
    P.op("dve", lambda h: h.tensor_copy(out=rows[0:32, 2, :].rearrange("p (g n) -> p g n", g=2),
                                        in_=rows[0:32, 3, 0:2].unsqueeze(2).to_broadcast([32, 2, 64])),
         reads=["rows"], writes=["rows"])
    for k in range(3):
        P.op("pe", lambda h, k=k: h.transpose(out=K.ps[0][:, k * 32:(k + 1) * 32], in_=rows[0:32, k, :], identity=K.identf[0:32, 0:32]),
             reads=["rows", "identf"], writes=["ps0"])
    P.op("dve", lambda h: h.tensor_copy(out=sc[:, 0:3, :], in_=K.ps[0][:, 0:96].rearrange("p (a b) -> p a b", b=32)),
         reads=["ps0"], writes=["sc"])

    def ew(eng, fn):
        P.op(eng, fn, reads=["sc"], writes=["sc"])
    ew("act", lambda h: h.activation(out=s(SC_DT), in_=s(SC_LDT), func=AF.Exp))
    ew("dve", lambda h: h.tensor_tensor(out=s(SC_T0), in0=s(SC_AR), in1=s(SC_DT), op=ALU.mult))
    ew("act", lambda h: h.activation(out=s(SC_RHO), in_=s(SC_T0), func=AF.Exp))
    ew("dve", lambda h: h.scalar_tensor_tensor(out=s(SC_T1), in0=s(SC_AI), scalar=1.0 / TWO_PI, in1=s(SC_DT), op0=ALU.mult, op1=ALU.mult))
    ew("dve", lambda h: h.tensor_copy(out=s(SC_T2).bitcast(I32), in_=s(SC_T1)))
    ew("dve", lambda h: h.tensor_tensor(out=s(SC_FR), in0=s(SC_T1), in1=s(SC_T2).bitcast(I32), op=ALU.subtract))
    ew("act", lambda h: h.activation(out=s(SC_T0), in_=s(SC_FR), func=AF.Sin, scale=TWO_PI))
    ew("act", lambda h: h.activation(out=s(SC_T1), in_=s(SC_FR), func=AF.Abs))
    ew("act", lambda h: h.activation(out=s(SC_T1), in_=s(SC_T1), func=AF.Sin, scale=-TWO_PI, bias=K.halfpi[:]))
    ew("dve", lambda h: h.tensor_tensor(out=s(SC_ABI), in0=s(SC_RHO), in1=s(SC_T0), op=ALU.mult))
    ew("dve", lambda h: h.tensor_tensor(out=s(SC_ABR), in0=s(SC_RHO), in1=s(SC_T1), op=ALU.mult))
    ew("dve", lambda h: h.tensor_tensor(out=s(SC_T0), in0=s(SC_AR), in1=s(SC_AR), op=ALU.mult))
    ew("dve", lambda h: h.tensor_tensor(out=s(SC_T1), in0=s(SC_AI), in1=s(SC_AI), op=ALU.mult))
    ew("dve", lambda h: h.tensor_tensor(out=s(SC_T0), in0=s(SC_T0), in1=s(SC_T1), op=ALU.add))
    ew("dve", lambda h: h.reciprocal(out=s(SC_T0), in_=s(SC_T0)))
    ew("dve", lambda h: h.tensor_scalar(out=s(SC_T1), in0=s(SC_ABR), scalar1=-1.0, scalar2=None, op0=ALU.add))
    ew("dve", lambda h: h.tensor_tensor(out=s(SC_T2), in0=s(SC_T1), in1=s(SC_AR), op=ALU.mult))
    ew("dve", lambda h: h.tensor_tensor(out=s(SC_T3), in0=s(SC_ABI), in1=s(SC_AI), op=ALU.mult))
    ew("dve", lambda h: h.tensor_tensor(out=s(SC_T2), in0=s(SC_T2), in1=s(SC_T3), op=ALU.add))
    ew("dve", lambda h: h.tensor_tensor(out=s(SC_CRE), in0=s(SC_T2), in1=s(SC_T0), op=ALU.mult))
    ew("dve", lambda h: h.tensor_tensor(out=s(SC_T2), in0=s(SC_ABI), in1=s(SC_AR), op=ALU.mult))
    ew("dve", lambda h: h.tensor_tensor(out=s(SC_T3), in0=s(SC_T1), in1=s(SC_AI), op=ALU.mult))
    ew("dve", lambda h: h.tensor_tensor(out=s(SC_T2), in0=s(SC_T2), in1=s(SC_T3), op=ALU.subtract))
    ew("dve", lambda h: h.tensor_tensor(out=s(SC_CIM), in0=s(SC_T2), in1=s(SC_T0), op=ALU.mult))

    P.op("pool", lambda h: h.memset(Xre, 0.0), writes=["Xre"])
    P.op("pool", lambda h: h.memset(Xim, 0.0), writes=["Xim"])
    for g2 in range(2):
        for nm, X, xk in (("ssm_b_re", Xre, "Xre"), ("ssm_b_im", Xim, "Xim")):
            src = I[nm][0].rearrange("(j g) n k -> g n j k", g=2)[g2]
            P.dma("sp", lambda h, X=X, src=src, g2=g2: h.dma_start(out=X[g2 * 64:(g2 + 1) * 64, :, g2 * 16:(g2 + 1) * 16], in_=src),
                  reads=[xk], writes=[xk])
    cre_b = s(SC_CRE).unsqueeze(2).to_broadcast([128, 32, 32])
    cim_b = s(SC_CIM).unsqueeze(2).to_broadcast([128, 32, 32])
    P.op("pool", lambda h: h.memset(msk, 0.0), writes=["msk"])
    for jj in range(4):
        P.op("pool", lambda h, jj=jj: h.memset(msk[32 * jj:32 * jj + 32, jj:jj + 1], 1.0), reads=["msk"], writes=["msk"])
    for ri in range(2):
        if ri == 0:
            P.op("dve", lambda h: h.tensor_tensor(out=T1, in0=Xre, in1=cre_b, op=ALU.mult), reads=["Xre", "sc"], writes=["T1"])
            P.op("dve", lambda h: h.tensor_tensor(out=T2, in0=Xim, in1=cim_b, op=ALU.mult), reads=["Xim", "sc"], writes=["T2"])
            P.op("dve", lambda h: h.tensor_tensor(out=T1, in0=T1, in1=T2, op=ALU.subtract), reads=["T1", "T2"], writes=["T1"])
        else:
            P.op("dve", lambda h: h.tensor_tensor(out=T1, in0=Xim, in1=cre_b, op=ALU.mult), reads=["Xim", "sc"], writes=["T1"])
            P.op("dve", lambda h: h.tensor_tensor(out=T2, in0=Xre, in1=cim_b, op=ALU.mult), reads=["Xre", "sc"], writes=["T2"])
            P.op("dve", lambda h: h.tensor_tensor(out=T1, in0=T1, in1=T2, op=ALU.add), reads=["T1", "T2"], writes=["T1"])
        for half in range(2):
            bk = "ps%d" % half
            for qq in range(4):
                q = half * 4 + qq
                P.op("pe", lambda h, q=q, qq=qq, half=half: h.transpose(
                    out=K.ps[half][:, qq * 128:(qq + 1) * 128],
                    in_=T1[:, 4 * q:4 * q + 4, :].rearrange("p a b -> p (a b)"), identity=K.identf[:]),
                    reads=["T1", "identf"], writes=[bk])
            P.op("act", lambda h, half=half: h.copy(out=raw[:, half * 4:half * 4 + 4, :],
                                                     in_=K.ps[half][:, :].rearrange("p (a b) -> p a b", b=128)),
                 reads=[bk], writes=["raw"])
        for jj in range(4):
            P.op("dve", lambda h, jj=jj, ri=ri: h.tensor_scalar(out=BW[ri][:, jj, :, :], in0=raw, scalar1=msk[:, jj:jj + 1],
                                                                 scalar2=None, op0=ALU.mult),
                 reads=["raw", "msk"], writes=["BW%d" % ri])

    m1 = msk[:, 4:5]
    m0 = msk[:, 5:6]
    P.op("pool", lambda h: h.iota(mski.bitcast(I32)[:, 0:1], pattern=[[0, 1]], base=0, channel_multiplier=1), writes=["mski"])
    P.op("dve", lambda h: h.tensor_single_scalar(out=mski.bitcast(I32)[:, 1:2], in_=mski.bitcast(I32)[:, 0:1], scalar=16, op=ALU.bitwise_and),
         reads=["mski"], writes=["mski"])
    P.op("dve", lambda h: h.tensor_copy(out=m1, in_=mski.bitcast(I32)[:, 1:2]), reads=["mski", "msk"], writes=["msk"])
    P.op("dve", lambda h: h.tensor_scalar(out=m1, in0=m1, scalar1=1.0 / 16, scalar2=None, op0=ALU.mult), reads=["msk"], writes=["msk"])
    P.op("dve", lambda h: h.tensor_scalar(out=m0, in0=m1, scalar1=-1.0, scalar2=1.0, op0=ALU.mult, op1=ALU.add), reads=["msk"], writes=["msk"])
    XC = T1.rearrange("p a b -> p (a b)").rearrange("p (q c) -> p q c", c=128)
    for ri, nm in enumerate(("ssm_c_re", "ssm_c_im")):
        src = I[nm][0].rearrange("g k n -> (g k) n").rearrange("(q p) n -> p q n", p=128)
        P.dma("sp", lambda h, src=src: h.dma_start(out=Cl, in_=src), writes=["Cl"])
        sgn = 1.0 if ri == 0 else -1.0
        P.op("dve", lambda h, sgn=sgn: h.tensor_scalar(out=XC[:, :, 0:64], in0=Cl, scalar1=m0, scalar2=sgn, op0=ALU.mult, op1=ALU.mult),
             reads=["Cl", "msk"], writes=["T1"])
        P.op("dve", lambda h, sgn=sgn: h.tensor_scalar(out=XC[:, :, 64:128], in0=Cl, scalar1=m1, scalar2=sgn, op0=ALU.mult, op1=ALU.mult),
             reads=["Cl", "msk"], writes=["T1"])
        for half in range(2):
            bk = "ps%d" % half
            for qq in range(4):
                q = half * 4 + qq
                P.op("pe", lambda h, q=q, qq=qq, half=half: h.transpose(
                    out=K.ps[half][:, qq * 128:(qq + 1) * 128], in_=XC[:, q, :], identity=K.identf[:]),
                    reads=["T1", "identf"], writes=[bk])
            P.op("act", lambda h, half=half: h.copy(out=raw[:, half * 4:half * 4 + 4, :],
                                                     in_=K.ps[half][:, :].rearrange("p (a b) -> p a b", b=128)),
                 reads=[bk], writes=["raw"])
        P.op("pool", lambda h, ri=ri: h.memset(CW[ri].rearrange("p a b c -> p (a b c)"), 0.0), writes=["CW%d" % ri])
        for jj in range(4):
            P.op("dve", lambda h, jj=jj, ri=ri: h.tensor_copy(out=CW[ri][:, jj, :, 32 * jj:32 * jj + 32], in_=raw[:, :, 32 * jj:32 * jj + 32]),
                 reads=["raw", "CW%d" % ri], writes=["CW%d" % ri])

    P.dma("sp", lambda h: h.dma_start(out=rows[0:8, 0, :], in_=I["ssm_d"][0].rearrange("(q p) -> q p", p=128)), reads=["rows"], writes=["rows"])
    P.op("pe", lambda h: h.transpose(out=K.ps[2][:, 0:8], in_=rows[0:8, 0, :], identity=K.identf[0:8, 0:8]), reads=["rows", "identf"], writes=["ps2"])
    P.op("dve", lambda h: h.tensor_copy(out=dcol, in_=K.ps[2][:, 0:8]), reads=["ps2"], writes=["dcol"])

    P.barrier()
    g_pre, gk = K.load_gain(I["norm_mix_pre"][0])
    for li in range(K.ntile):
        K.norm_to_xnT(li, g_pre, gk, li * 128)

    wglu = [K.load_w(I["w_glu"][0][:, hf * 1024:(hf + 1) * 1024], 1024) for hf in range(2)]

    ptiles = [idx for (kind, idx) in K.grp if kind == "P"]
    has_s = any(kind == "S" for (kind, _) in K.grp)
    tp = 128 * len(ptiles)
    chunks = []
    c0 = 0
    while c0 < tp:
        n = min(256, tp - c0)
        chunks.append(("P", c0, n, ptiles[0] * 128 + c0))
        c0 += n
    if has_s:
        chunks.append(("S", tp, 128, 0))
        P.op("pool", lambda h: h.memset(rmask, 1.0), writes=["rmask"])
        P.op("pool", lambda h: h.memset(rmask.rearrange("p (b t) -> p b t", t=8)[:, :, 0:1], 0.0), reads=["rmask"], writes=["rmask"])
        for ri, nm in enumerate(("ssr", "ssi")):
            P.dma("sp", lambda h, nm=nm: h.dma_start(out=h0stage[0:16, :], in_=I[nm][:, :]), writes=["h0stage"])
            for j in range(32):
                P.op("pe", lambda h, j=j: h.transpose(out=K.ps[3][:, j * 16:(j + 1) * 16], in_=h0stage[0:16, j * 128:(j + 1) * 128],
                                                       identity=K.identf[0:16, 0:16]), reads=["h0stage", "identf"], writes=["ps3"])
            P.op("act", lambda h, ri=ri: h.copy(out=h0[ri], in_=K.ps[3][:, :].rearrange("p (j b) -> p j b", b=16)), reads=["ps3"], writes=["h0_%d" % ri])
        abr_b = s(SC_ABR).unsqueeze(2).to_broadcast([128, 32, 16])
        abi_b = s(SC_ABI).unsqueeze(2).to_broadcast([128, 32, 16])
        Ta = zt[:, 0:512].rearrange("p (j b) -> p j b", b=16)
        Tb = zt[:, 512:1024].rearrange("p (j b) -> p j b", b=16)
        P.op("dve", lambda h: h.tensor_tensor(out=Ta, in0=h0[0], in1=abr_b, op=ALU.mult), reads=["h0_0", "sc"], writes=["zt"])
        P.op("dve", lambda h: h.tensor_tensor(out=Tb, in0=h0[1], in1=abi_b, op=ALU.mult), reads=["h0_1", "sc"], writes=["zt"])
        P.op("dve", lambda h: h.tensor_tensor(out=ah0[0], in0=Ta, in1=Tb, op=ALU.subtract), reads=["zt"], writes=["ah0_0"])
        P.op("dve", lambda h: h.tensor_tensor(out=Ta, in0=h0[1], in1=abr_b, op=ALU.mult), reads=["h0_1", "sc"], writes=["zt"])
        P.op("dve", lambda h: h.tensor_tensor(out=Tb, in0=h0[0], in1=abi_b, op=ALU.mult), reads=["h0_0", "sc"], writes=["zt"])
        P.op("dve", lambda h: h.tensor_tensor(out=ah0[1], in0=Ta, in1=Tb, op=ALU.add), reads=["zt"], writes=["ah0_1"])
        P.barrier()
    if K.gi == 0:
        P.op("pool", lambda h: h.memset(carry.rearrange("p a b -> p (a b)"), 0.0), writes=["carry%d" % j_ for j_ in range(32)])

    units = []
    for q in range(8):
        for ci, (kind, c0, n, tglob) in enumerate(chunks):
            for jj in range(4):
                units.append(dict(q=q, ci=ci, kind=kind, c0=c0, n=n, tglob=tglob, jj=jj, u=len(units), last_chunk=(ci == len(chunks) - 1)))

    def stage1(U):
        q, kind, c0, n, jj, u = U["q"], U["kind"], U["c0"], U["n"], U["jj"], U["u"]
        j = 4 * q + jj
        bi = u % 2
        cols = slice(c0, c0 + n)
        bre, bim = K.ps[bi * 2], K.ps[bi * 2 + 1]
        brk, bik = "ps%d" % (bi * 2), "ps%d" % (bi * 2 + 1)
        ukey = "xnT.%d" % q
        P.op("pe", lambda h: h.matmul(bre[:, 0:n], lhsT=BW[0][:, jj, q, :], rhs=K.xnT[:, q, cols], start=True, stop=True), reads=["BW0", ukey], writes=[brk])
        P.op("pe", lambda h: h.matmul(bim[:, 0:n], lhsT=BW[1][:, jj, q, :], rhs=K.xnT[:, q, cols], start=True, stop=True), reads=["BW1", ukey], writes=[bik])
        tS, tC, xI, rR = tabS[bi], tabC[bi], xi[bi], rr[bi]
        tk = "tab%d" % bi
        fr = sc[:, SC_FR, j:j + 1]
        io = K.iota_g[:, cols]
        P.op("dve", lambda h: h.tensor_scalar(out=xI[:, 0:n], in0=io, scalar1=fr, scalar2=None, op0=ALU.mult), reads=["iota_g", "sc"], writes=[tk + "x"])
        P.op("dve", lambda h: h.scalar_tensor_tensor(out=rR[:, 0:n], in0=io, scalar=fr, in1=xI[:, 0:n], op0=ALU.mult, op1=ALU.subtract), reads=["iota_g", "sc", tk + "x"], writes=[tk + "r"])
        P.op("act", lambda h: h.activation(out=tS[:, 0:n], in_=rR[:, 0:n], func=AF.Sin, scale=TWO_PI), reads=[tk + "r"], writes=[tk + "S"])
        P.op("dve", lambda h: h.scalar_tensor_tensor(out=tC[:, 0:n], in0=rR[:, 0:n], scalar=-1.0, in1=rR[:, 0:n], op0=ALU.mult, op1=ALU.max), reads=[tk + "r"], writes=[tk + "C"])
        P.op("act", lambda h: h.activation(out=tC[:, 0:n], in_=tC[:, 0:n], func=AF.Sin, scale=-TWO_PI, bias=K.halfpi[:]), reads=[tk + "C"], writes=[tk + "C"])
        if kind == "S":
            for ri, bb, bk_ in ((0, bre, brk), (1, bim, bik)):
                v = bb[:, 0:128].rearrange("p (b t) -> p b t", t=8)[:, :, 0]
                P.op("dve", lambda h, v=v, ri=ri: h.tensor_tensor(out=v, in0=v, in1=ah0[ri][:, j, :], op=ALU.add), reads=[bk_, "ah0_%d" % ri], writes=[bk_])

    def stage2(U):
        q, kind, c0, n, jj, u, tglob = U["q"], U["kind"], U["c0"], U["n"], U["jj"], U["u"], U["tglob"]
        j = 4 * q + jj
        bi = u % 2
        cols = slice(c0, c0 + n)
        bre, bim = K.ps[bi * 2], K.ps[bi * 2 + 1]
        brk, bik = "ps%d" % (bi * 2), "ps%d" % (bi * 2 + 1)
        tS, tC = tabS[bi], tabC[bi]
        tk = "tab%d" % bi
        ybank = 4 + ((u // 4) % 2)
        ybk = "ps%d" % ybank
        yps = K.ps[ybank]
        wr, wi = wre[bi], wim[bi]
        wk = "w%d" % bi
        P.op("dve", lambda h: h.tensor_tensor(out=t1[:, 0:n], in0=bre[:, 0:n], in1=tC[:, 0:n], op=ALU.mult), reads=[brk, tk + "C"], writes=["t1"])
        P.op("dve", lambda h: h.tensor_tensor(out=t2[:, 0:n], in0=bim[:, 0:n], in1=tS[:, 0:n], op=ALU.mult), reads=[bik, tk + "S"], writes=["t2"])
        P.op("dve", lambda h: h.tensor_tensor(out=wr[:, 0:n], in0=t1[:, 0:n], in1=t2[:, 0:n], op=ALU.add), reads=["t1", "t2"], writes=[wk + "r"])
        P.op("dve", lambda h: h.tensor_tensor(out=t1[:, 0:n], in0=bim[:, 0:n], in1=tC[:, 0:n], op=ALU.mult), reads=[bik, tk + "C"], writes=["t1"])
        P.op("dve", lambda h: h.tensor_tensor(out=t2[:, 0:n], in0=bre[:, 0:n], in1=tS[:, 0:n], op=ALU.mult), reads=[brk, tk + "S"], writes=["t2"])
        P.op("dve", lambda h: h.tensor_tensor(out=wi[:, 0:n], in0=t1[:, 0:n], in1=t2[:, 0:n], op=ALU.subtract), reads=["t1", "t2"], writes=[wk + "i"])
        gr, gi_ = gre[bi], gim[bi]
        gk_ = "g%d" % bi
        if kind == "P":
            d0 = sc[:, SC_RHO, j:j + 1].to_broadcast([128, n])
            d0k = "sc"
            ini = [carry[:, j, 0:1], carry[:, j, 1:2]]
            inik = ["carry%d" % j]
        else:
            P.op("pool", lambda h: h.tensor_scalar(out=rho_s, in0=rmask, scalar1=sc[:, SC_RHO, j:j + 1], scalar2=None, op0=ALU.mult), reads=["rmask", "sc"], writes=["rho_s"])
            d0 = rho_s[:, 0:n]
            d0k = "rho_s"
            ini = [0.0, 0.0]
            inik = []
        P.op("dve", lambda h: h.tensor_tensor_scan(out=gr[:, 0:n], data0=d0, data1=wr[:, 0:n], initial=ini[0], op0=ALU.mult, op1=ALU.add), reads=[wk + "r", d0k] + inik, writes=[gk_ + "r"])
        P.op("dve", lambda h: h.tensor_tensor_scan(out=gi_[:, 0:n], data0=d0, data1=wi[:, 0:n], initial=ini[1], op0=ALU.mult, op1=ALU.add), reads=[wk + "i", d0k] + inik, writes=[gk_ + "i"])
        if kind == "P":
            P.op("pool", lambda h: h.tensor_copy(out=carry[:, j, 0:1], in_=gr[:, n - 1:n]), reads=[gk_ + "r"], writes=["carry%d" % j])
            P.op("pool", lambda h: h.tensor_copy(out=carry[:, j, 1:2], in_=gi_[:, n - 1:n]), reads=[gk_ + "i", "carry%d" % j], writes=["carry%d" % j])
            if tglob + n == SEQ:
                P.op("pool", lambda h: h.tensor_copy(out=fin[:, j, 0:1], in_=tC[:, n - 1:n]), reads=[tk + "C"], writes=["fin"])
                P.op("pool", lambda h: h.tensor_copy(out=fin[:, j, 1:2], in_=tS[:, n - 1:n]), reads=[tk + "S", "fin"], writes=["fin"])
        else:
            g7r = gr[:, 0:128].rearrange("p (b t) -> p b t", t=8)[:, :, 7]
            g7i = gi_[:, 0:128].rearrange("p (b t) -> p b t", t=8)[:, :, 7]
            c7 = tC[:, 7:8]
            s7 = tS[:, 7:8]
            o_r = fs[0][:, j, :]
            o_i = fs[1][:, j, :]
            P.op("pool", lambda h: h.tensor_scalar(out=u1[:, 0:16], in0=g7r, scalar1=c7, scalar2=None, op0=ALU.mult), reads=[gk_ + "r", tk + "C"], writes=["u1"])
            P.op("pool", lambda h: h.tensor_scalar(out=u2[:, 0:16], in0=g7i, scalar1=s7, scalar2=None, op0=ALU.mult), reads=[gk_ + "i", tk + "S"], writes=["u2"])
            P.op("pool", lambda h: h.tensor_tensor(out=o_r, in0=u1[:, 0:16], in1=u2[:, 0:16], op=ALU.subtract), reads=["u1", "u2"], writes=["fs0"])
            P.op("pool", lambda h: h.tensor_scalar(out=u1[:, 0:16], in0=g7i, scalar1=c7, scalar2=None, op0=ALU.mult), reads=[gk_ + "i", tk + "C"], writes=["u1"])
            P.op("pool", lambda h: h.tensor_scalar(out=u2[:, 0:16], in0=g7r, scalar1=s7, scalar2=None, op0=ALU.mult), reads=[gk_ + "r", tk + "S"], writes=["u2"])
            P.op("pool", lambda h: h.tensor_tensor(out=o_i, in0=u1[:, 0:16], in1=u2[:, 0:16], op=ALU.add), reads=["u1", "u2"], writes=["fs1"])
        hr, hi = hre[bi], him[bi]
        hk_ = "hh%d" % bi
        P.op("pool", lambda h: h.tensor_tensor(out=u1[:, 0:n], in0=gr[:, 0:n], in1=tC[:, 0:n], op=ALU.mult), reads=[gk_ + "r", tk + "C"], writes=["u1"])
        P.op("pool", lambda h: h.tensor_tensor(out=u2[:, 0:n], in0=gi_[:, 0:n], in1=tS[:, 0:n], op=ALU.mult), reads=[gk_ + "i", tk + "S"], writes=["u2"])
        P.op("pool", lambda h: h.tensor_tensor(out=hr[:, 0:n], in0=u1[:, 0:n], in1=u2[:, 0:n], op=ALU.subtract), reads=["u1", "u2"], writes=[hk_ + "r"])
        P.op("dve", lambda h: h.tensor_tensor(out=v1[:, 0:n], in0=gi_[:, 0:n], in1=tC[:, 0:n], op=ALU.mult), reads=[gk_ + "i", tk + "C"], writes=["v1"])
        P.op("dve", lambda h: h.tensor_tensor(out=v2[:, 0:n], in0=gr[:, 0:n], in1=tS[:, 0:n], op=ALU.mult), reads=[gk_ + "r", tk + "S"], writes=["v2"])
        P.op("pool", lambda h: h.tensor_tensor(out=hi[:, 0:n], in0=v1[:, 0:n], in1=v2[:, 0:n], op=ALU.add), reads=["v1", "v2"], writes=[hk_ + "i"])
        P.op("pe", lambda h: h.matmul(yps[:, 0:n], lhsT=CW[0][:, jj, q, :], rhs=hr[:, 0:n], start=(jj == 0), stop=False), reads=["CW0", hk_ + "r"], writes=[ybk])
        P.op("pe", lambda h: h.matmul(yps[:, 0:n], lhsT=CW[1][:, jj, q, :], rhs=hi[:, 0:n], start=False, stop=(jj == 3)), reads=["CW1", hk_ + "i"], writes=[ybk])
        if jj == 3:
            P.op("dve", lambda h: h.scalar_tensor_tensor(out=yv[:, cols], in0=K.xnT[:, q, cols], scalar=dcol[:, q:q + 1], in1=yps[:, 0:n], op0=ALU.mult, op1=ALU.add),
                 reads=[ybk, "xnT.%d" % q, "dcol"], writes=["yv"])
            if U["last_chunk"]:
                tok = K.tok
                P.op("act", lambda h: h.activation(out=K.xnT[:, q, 0:tok], in_=yv[:, 0:tok], func=AF.Gelu), reads=["yv"], writes=["xnT.%d" % q])

    stage1(units[0])
    for i, U in enumerate(units):
        if i + 1 < len(units):
            stage1(units[i + 1])
        stage2(U)

    P.barrier()
    if any(kind == "P" and idx == NPT - 1 for (kind, idx) in K.grp):
        cc, ss = fin[:, :, 0], fin[:, :, 1]
        gr_, gi2 = carry[:, :, 0], carry[:, :, 1]
        Fa = zt[:, 0:32]
        Fb = zt[:, 32:64]
        Fc = zt[:, 64:96]
        Fd = zt[:, 96:128]
        P.op("dve", lambda h: h.tensor_tensor(out=Fa, in0=cc, in1=gr_, op=ALU.mult), reads=["fin"] + ["carry%d" % j_ for j_ in range(32)], writes=["zt"])
        P.op("dve", lambda h: h.tensor_tensor(out=Fb, in0=ss, in1=gi2, op=ALU.mult), reads=["fin"] + ["carry%d" % j_ for j_ in range(32)], writes=["zt"])
        P.op("dve", lambda h: h.tensor_tensor(out=Fc, in0=Fa, in1=Fb, op=ALU.subtract), reads=["zt"], writes=["zt"])
        P.op("dve", lambda h: h.tensor_tensor(out=Fa, in0=cc, in1=gi2, op=ALU.mult), reads=["fin"] + ["carry%d" % j_ for j_ in range(32)], writes=["zt"])
        P.op("dve", lambda h: h.tensor_tensor(out=Fb, in0=ss, in1=gr_, op=ALU.mult), reads=["fin"] + ["carry%d" % j_ for j_ in range(32)], writes=["zt"])
        P.op("dve", lambda h: h.tensor_tensor(out=Fd, in0=Fa, in1=Fb, op=ALU.add), reads=["zt"], writes=["zt"])
        for ri, (T_, oname) in enumerate(((Fc, "ssm_re_p"), (Fd, "ssm_im_p"))):
            P.op("pe", lambda h, T_=T_: h.transpose(out=K.ps[6][0:32, 0:128], in_=T_, identity=K.identf[:]), reads=["zt", "identf"], writes=["ps6"])
            P.op("act", lambda h: h.copy(out=rows[0:32, 0, :], in_=K.ps[6][0:32, 0:128]), reads=["ps6"], writes=["rows"])
            P.dma("sp", lambda h, oname=oname: h.dma_start(out=O[oname].rearrange("o (j c) -> (o j) c", c=128), in_=rows[0:32, 0, :]), reads=["rows"], is_output=True)
    if has_s:
        for ri, oname in enumerate(("ssm_re_s", "ssm_im_s")):
            for j in range(32):
                bank = 6 + (j // 16) % 2
                P.op("pe", lambda h, j=j, ri=ri, bank=bank: h.transpose(out=K.ps[bank][0:16, (j % 4) * 128:(j % 4 + 1) * 128], in_=fs[ri][:, j, :], identity=K.identf[:]),
                     reads=["fs%d" % ri, "identf"], writes=["ps%d" % bank])
                if j % 4 == 3:
                    P.op("act", lambda h, j=j, bank=bank: h.copy(out=h0stage[0:16, (j - 3) * 128:(j + 1) * 128], in_=K.ps[bank][0:16, :]),
                         reads=["ps%d" % bank], writes=["h0stage"])
            P.dma("sp", lambda h, oname=oname: h.dma_start(out=O[oname][:, :], in_=h0stage[0:16, :]), reads=["h0stage"], is_output=True)

    g_post, gpk = K.load_gain(I["norm_mix_post"][0])
    for li in range(K.ntile):
        tc = slice(li * 128, (li + 1) * 128)
        for hf in range(2):
            wv, wk_ = wglu[hf]
            for nb in range(2):
                bank = hf * 2 + nb
                for k in range(8):
                    P.op("pe", lambda h, bank=bank, k=k, tc=tc, wv=wv, nb=nb: h.matmul(K.ps[bank][:, :], lhsT=K.xnT[:, k, tc], rhs=wv[:, k, nb * 512:(nb + 1) * 512], start=(k == 0), stop=(k == 7)),
                         reads=["xnT.%d" % k, wk_], writes=["ps%d" % bank])
        for nb in range(2):
            P.op("act", lambda h, nb=nb: h.activation(out=zt[:, nb * 512:(nb + 1) * 512], in_=K.ps[2 + nb][:, :], func=AF.Sigmoid), reads=["ps%d" % (2 + nb)], writes=["zt%d" % nb])
            P.op("dve", lambda h, nb=nb: h.tensor_tensor(out=zt[:, nb * 512:(nb + 1) * 512], in0=zt[:, nb * 512:(nb + 1) * 512], in1=K.ps[nb][:, :], op=ALU.mult),
                 reads=["zt%d" % nb, "ps%d" % nb], writes=["zt%d" % nb])
        K.post_norm_add(li, zt, ["zt0", "zt1"], g_post, gpk)
    P.barrier()


class Carver:
    def __init__(self, arena):
        self.A = arena
        self.off = 0
        self.hi = 0

    def __call__(self, ncols, dt=F32, shape=None):
        a = self.off
        self.off += ncols
        self.hi = max(self.hi, self.off)
        assert self.hi <= ARENA, self.hi
        v = self.A[:, a:a + ncols]
        if dt != F32:
            v = v.bitcast(dt)
        if shape is not None:
            names = "abc"[:len(shape)]
            kw = {names[i]: shape[i] for i in range(1, len(shape))}
            v = v.rearrange("p (%s) -> p %s" % (" ".join(names), " ".join(names)), **kw)
        return v


def evac(P, eng, out, in_, reads, writes):
    if eng == "act":
        P.op("act", lambda h: h.copy(out=out, in_=in_), reads=reads, writes=writes)
    else:
        P.op(eng, lambda h: h.tensor_copy(out=out, in_=in_), reads=reads, writes=writes)


def softmax_pt(K, S, P, spair, tpair, scale, p, Dm, pT, sm, tag):
    sp = K.pp[spair]
    tp_ = K.pp[tpair]
    sk = ["ps%d" % (2 * spair), "ps%d" % (2 * spair + 1)]
    tk = ["ps%d" % (2 * tpair), "ps%d" % (2 * tpair + 1)]
    mx, nb, l, rl = sm[:, 0:4], sm[:, 4:8], sm[:, 8:12], sm[:, 12:16]
    P.op("dve", lambda h: h.tensor_reduce(out=mx, in_=sp[:, :].rearrange("p (a b) -> p a b", b=256), axis=AX.X, op=ALU.max),
         reads=sk, writes=[tag + "sm"])
    P.op("dve", lambda h: h.tensor_scalar(out=nb, in0=mx, scalar1=-scale, scalar2=None, op0=ALU.mult), reads=[tag + "sm"], writes=[tag + "sm"])
    for hh in range(4):
        P.op("act", lambda h, hh=hh: h.activation(out=p[:, hh, :], in_=sp[:, hh * 256:(hh + 1) * 256], func=AF.Exp, scale=scale,
                                                   bias=nb[:, hh:hh + 1], accum_out=l[:, hh:hh + 1]),
             reads=sk + [tag + "sm"], writes=[tag + "p", tag + "l"])
    P.op("dve", lambda h: h.reciprocal(out=rl, in_=l), reads=[tag + "l", tag + "sm"], writes=[tag + "sm"])
    P.op("dve", lambda h: h.tensor_tensor(out=Dm, in0=K.ident[:].unsqueeze(1).to_broadcast([128, 4, 128]),
                                          in1=rl.unsqueeze(2).to_broadcast([128, 4, 128]), op=ALU.mult),
         reads=["ident", tag + "sm"], writes=[tag + "D"])
    for hh in range(4):
        for mt in range(2):
            c = hh * 2 + mt
            P.op("pe", lambda h, hh=hh, mt=mt, c=c: h.matmul(tp_[:, c * 128:(c + 1) * 128], lhsT=p[:, hh, mt * 128:(mt + 1) * 128], rhs=Dm[:, hh, :], start=True, stop=True),
                 reads=[tag + "p", tag + "D"], writes=[tk[c // 4]])
    evac(P, "act", pT[:, 0:4, :], tp_[:, 0:512].rearrange("p (a b) -> p a b", b=128), [tk[0]], [tag + "pT0"])
    evac(P, "dve", pT[:, 4:8, :], tp_[:, 512:1024].rearrange("p (a b) -> p a b", b=128), [tk[1]], [tag + "pT1"])


def from_mem(K, L):
    P, I, O = K.P, K.I, K.O
    C = Carver(K.arena)
    TOK = K.tok
    qT = C(TOK * 4, BF16, [8, TOK])
    mnT = C(1024, BF16, [8, 256])
    mkT = C(1024, BF16, [8, 256])
    mv = C(1024, BF16, [2, 1024])
    mst = C(1024)
    p = [C(512, BF16, [4, 256]) for _ in range(2)]
    Dm = [C(256, BF16, [4, 128]) for _ in range(2)]
    pT = [C(512, BF16, [8, 128]) for _ in range(2)]
    oT = [C(512, BF16, [8, 128]) for _ in range(2)]
    sm = [C(16) for _ in range(2)]
    has_s = any(kind == "S" for (kind, _) in K.grp)
    if has_s:
        Kst = [C(2048, F32, [2, 1024]) for _ in range(2)]
        KTb = C(1024, BF16, [8, 256])
        Vb = C(1024, BF16, [2, 1024])
        sTs = C(1024, F32, [8, 128])

    wk, wkk = K.load_w(I["w_mem_k"][L], 1024)
    wv, wvk = K.load_w(I["w_mem_v"][L], 1024)
    wq, wqk = K.load_w(I["w_mem_q"][L], 1024)

    g_in, gik = K.load_gain(I["mem_in_norm"][L])
    for mt in range(2):
        P.dma("sp", lambda h, mt=mt: h.dma_start(out=mst, in_=I["memp"][mt * 128:(mt + 1) * 128, :]), writes=["mst"])
        rs, rk = K.rms_stats(mst, ["mst"], D)
        P.op("dve", lambda h, rs=rs: h.scalar_tensor_tensor(out=K.xn[:], in0=mst, scalar=rs, in1=g_in[:], op0=ALU.mult, op1=ALU.mult),
             reads=["mst", rk, gik], writes=["xn"])
        K.transpose_to(K.xn, "xn", mnT, "mnT", mt * 128)
    mnk = ["mnT.%d" % k for k in range(8)]
    for mt in range(2):
        for which, (w_, wkey, oname) in enumerate(((wk, wkk, "memk_p"), (wv, wvk, "memv_p"))):
            if which == 0 and K.gi != 0:
                continue
            pair = 2 + which
            for nb in range(2):
                for k in range(8):
                    P.op("pe", lambda h, pair=pair, nb=nb, k=k, mt=mt, w_=w_: h.matmul(K.pp[pair][:, nb * 512:(nb + 1) * 512], lhsT=mnT[:, k, mt * 128:(mt + 1) * 128], rhs=w_[:, k, nb * 512:(nb + 1) * 512], start=(k == 0), stop=(k == 7)),
                         reads=mnk + [wkey], writes=["ps%d" % (2 * pair + nb)])
            pk = ["ps%d" % (2 * pair), "ps%d" % (2 * pair + 1)]
            if which == 1:
                evac(P, "act", mv[:, mt, :], K.pp[pair][:, :], pk, ["mv"])
            if K.gi == 0:
                evac(P, "dve", mst, K.pp[pair][:, :], pk, ["mst"])
                P.dma("sp", lambda h, oname=oname, mt=mt: h.dma_start(out=O[oname][L, mt * 128:(mt + 1) * 128, :], in_=mst), reads=["mst"], is_output=True)
    for c8 in range(8):
        bank = c8 % 2
        for k in range(8):
            P.op("pe", lambda h, bank=bank, c8=c8, k=k: h.matmul(K.ps[bank][:, 0:256], lhsT=wk[:, k, c8 * 128:(c8 + 1) * 128], rhs=mnT[:, k, :], start=(k == 0), stop=(k == 7)),
                 reads=mnk + [wkk], writes=["ps%d" % bank])
        evac(P, "act" if c8 % 2 == 0 else "dve", mkT[:, c8, :], K.ps[bank][:, 0:256], ["ps%d" % bank], ["mkT"])

    g_pre, gpk = K.load_gain(I["norm_mem_pre"][L])
    for li in range(K.ntile):
        K.norm_to_xnT(li, g_pre, gpk, li * 128)
    wo, wok = K.load_w(I["w_mem_o"][L], 1024)
    xk = ["xnT.%d" % k for k in range(8)]
    nblk = 2
    bs = TOK // nblk
    u = 0
    for blk in range(nblk):
        cs = slice(blk * bs, (blk + 1) * bs)
        for c8 in range(8):
            bank = 4 + (u % 4)
            for k in range(8):
                P.op("pe", lambda h, bank=bank, c8=c8, k=k, cs=cs: h.matmul(K.ps[bank][:, 0:bs], lhsT=wq[:, k, c8 * 128:(c8 + 1) * 128], rhs=K.xnT[:, k, cs], start=(k == 0), stop=(k == 7)),
                     reads=xk + [wqk], writes=["ps%d" % bank])
            evac(P, "act" if u % 2 == 0 else "dve", qT[:, c8, cs], K.ps[bank][:, 0:bs], ["ps%d" % bank], ["qT"])
            u += 1
    g_post, gpok = K.load_gain(I["norm_mem_post"][L])

    def tail(li, bi, tpair_pT):
        pass

    for li, (kind, idx) in enumerate(K.grp):
        bi = li % 2
        tag = "m%d" % bi
        tcols = slice(li * 128, (li + 1) * 128)
        if kind == "P":
            for hh in range(4):
                for dc in range(2):
                    c8 = hh * 2 + dc
                    P.op("pe", lambda h, hh=hh, dc=dc, c8=c8, tcols=tcols: h.matmul(K.pp[0][:, hh * 256:(hh + 1) * 256], lhsT=qT[:, c8, tcols], rhs=mkT[:, c8, :], start=(dc == 0), stop=(dc == 1)),
                         reads=["qT", "mkT"], writes=["ps%d" % (hh // 2)])
            softmax_pt(K, None, P, 0, 1, MEM_SCALE, p[bi], Dm[bi], pT[bi], sm[bi], tag)
            for hh in range(4):
                for dvc in range(2):
                    c = hh * 2 + dvc
                    for mt in range(2):
                        P.op("pe", lambda h, hh=hh, dvc=dvc, c=c, mt=mt, bi=bi: h.matmul(K.pp[2][:, c * 128:(c + 1) * 128], lhsT=mv[:, mt, hh * 256 + dvc * 128:hh * 256 + (dvc + 1) * 128], rhs=pT[bi][:, hh * 2 + mt, :], start=(mt == 0), stop=(mt == 1)),
                             reads=["mv", tag + "pT0", tag + "pT1"], writes=["ps%d" % (4 + c // 4)])
        else:
            scol = li * 128
            for b in range(16):
                sb_ = b % 2
                P.dma("sp", lambda h, b=b, sb_=sb_: h.dma_start(out=Kst[sb_], in_=I["memk"][L, b].rearrange("(a p) n -> p a n", p=128)), writes=["Kst%d" % sb_])
                for mt in range(2):
                    for c8 in range(8):
                        t_ = mt * 8 + c8
                        P.op("pe", lambda h, sb_=sb_, mt=mt, c8=c8: h.transpose(out=K.pp[2 + c8 // 4][:, (c8 % 4) * 256 + mt * 128:(c8 % 4) * 256 + (mt + 1) * 128], in_=Kst[sb_][:, mt, c8 * 128:(c8 + 1) * 128], identity=K.identf[:]),
                             reads=["Kst%d" % sb_, "identf"], writes=["ps%d" % (4 + (c8 // 4) * 2 + (c8 % 4) // 2)])
                evac(P, "act", KTb[:, 0:4, :], K.pp[2][:, :].rearrange("p (a b) -> p a b", b=256), ["ps4", "ps5"], ["KTb0"])
                evac(P, "dve", KTb[:, 4:8, :], K.pp[3][:, :].rearrange("p (a b) -> p a b", b=256), ["ps6", "ps7"], ["KTb1"])
                for hh in range(4):
                    for mt in range(2):
                        c = hh * 2 + mt
                        for dc in range(2):
                            P.op("pe", lambda h, hh=hh, mt=mt, c=c, dc=dc, b=b: h.matmul(K.pp[0][:, c * 128 + b * 8:c * 128 + b * 8 + 8], lhsT=KTb[:, hh * 2 + dc, mt * 128:(mt + 1) * 128], rhs=qT[:, hh * 2 + dc, scol + b * 8:scol + b * 8 + 8], start=(dc == 0), stop=(dc == 1)),
                                 reads=["KTb0", "KTb1", "qT"], writes=["ps%d" % (c // 4)])
            evac(P, "act", sTs[:, 0:4, :], K.pp[0][:, 0:512].rearrange("p (a b) -> p a b", b=128), ["ps0"], ["sTs"])
            evac(P, "dve", sTs[:, 4:8, :], K.pp[0][:, 512:1024].rearrange("p (a b) -> p a b", b=128), ["ps1"], ["sTs"])
            for hh in range(4):
                for mt in range(2):
                    P.op("pe", lambda h, hh=hh, mt=mt: h.transpose(out=K.pp[1][:, hh * 256 + mt * 128:hh * 256 + (mt + 1) * 128], in_=sTs[:, hh * 2 + mt, :], identity=K.identf[:]),
                         reads=["sTs", "identf"], writes=["ps%d" % (2 + hh // 2)])
            softmax_pt(K, None, P, 1, 0, MEM_SCALE, p[bi], Dm[bi], pT[bi], sm[bi], tag)
            for b in range(16):
                sb_ = b % 2
                P.dma("sp", lambda h, b=b, sb_=sb_: h.dma_start(out=Kst[sb_], in_=I["memv"][L, b].rearrange("(a p) n -> p a n", p=128)), writes=["Kst%d" % sb_])
                P.op("pool", lambda h, sb_=sb_: h.tensor_copy(out=Vb, in_=Kst[sb_]), reads=["Kst%d" % sb_], writes=["Vb"])
                for hh in range(4):
                    for dvc in range(2):
                        c = hh * 2 + dvc
                        for mt in range(2):
                            P.op("pe", lambda h, hh=hh, dvc=dvc, c=c, mt=mt, bi=bi, b=b: h.matmul(K.pp[2][:, c * 128 + b * 8:c * 128 + b * 8 + 8], lhsT=Vb[:, mt, hh * 256 + dvc * 128:hh * 256 + (dvc + 1) * 128], rhs=pT[bi][:, hh * 2 + mt, b * 8:b * 8 + 8], start=(mt == 0), stop=(mt == 1)),
                                 reads=["Vb", tag + "pT0", tag + "pT1"], writes=["ps%d" % (4 + c // 4)])
        evac(P, "act", oT[bi][:, 0:4, :], K.pp[2][:, 0:512].rearrange("p (a b) -> p a b", b=128), ["ps4"], [tag + "oT"])
        evac(P, "dve", oT[bi][:, 4:8, :], K.pp[2][:, 512:1024].rearrange("p (a b) -> p a b", b=128), ["ps5"], [tag + "oT"])
        for nb in range(2):
            for c8 in range(8):
                P.op("pe", lambda h, nb=nb, c8=c8, bi=bi: h.matmul(K.pp[3][:, nb * 512:(nb + 1) * 512], lhsT=oT[bi][:, c8, :], rhs=wo[:, c8, nb * 512:(nb + 1) * 512], start=(c8 == 0), stop=(c8 == 7)),
                     reads=[tag + "oT", wok], writes=["ps%d" % (6 + nb)])
        K.post_norm_add(li, K.pp[3][:, :], ["ps6", "ps7"], g_post, gpok)
    P.barrier()


def from_mlp(K, L):
    P, I, O = K.P, K.I, K.O
    C = Carver(K.arena)
    TOK = K.tok
    NT = K.ntile
    acc = C(NT * 1024, F32, [NT, 1024])
    hdnT = C(TOK * 4, BF16, [8, TOK])
    r = [C(512) for _ in range(2)]
    g_pre, gpk = K.load_gain(I["norm_mlp_pre"][L])
    for li in range(NT):
        K.norm_to_xnT(li, g_pre, gpk, li * 128)
    xk = ["xnT.%d" % k for k in range(8)]
    nblk = 2
    bs = TOK // nblk
    u = 0
    for c in range(4):
        wu, wuk = K.load_w(I["w_mlp_up"][L][:, c * 1024:(c + 1) * 1024], 1024)
        wd, wdk = K.load_w(I["w_mlp_down"][L][c * 1024:(c + 1) * 1024, :], 1024)
        for blk in range(nblk):
            cs = slice(blk * bs, (blk + 1) * bs)
            for f in range(8):
                bank = u % 4
                ri = u % 2
                for k in range(8):
                    P.op("pe", lambda h, bank=bank, f=f, k=k, cs=cs, wu=wu: h.matmul(K.ps[bank][:, 0:bs], lhsT=wu[:, k, f * 128:(f + 1) * 128], rhs=K.xnT[:, k, cs], start=(k == 0), stop=(k == 7)),
                         reads=xk + [wuk], writes=["ps%d" % bank])
                P.op("act", lambda h, bank=bank, ri=ri: h.activation(out=r[ri][:, 0:bs], in_=K.ps[bank][:, 0:bs], func=AF.Relu), reads=["ps%d" % bank], writes=["r%d" % ri])
                P.op("pool", lambda h, ri=ri, f=f, cs=cs: h.tensor_tensor(out=hdnT[:, f, cs], in0=r[ri][:, 0:bs], in1=r[ri][:, 0:bs], op=ALU.mult), reads=["r%d" % ri], writes=["hdnT.%d.%d" % (f, blk)])
                u += 1
        for li in range(NT):
            blk = (li * 128) // bs
            pair = 2 + li % 2
            for nb in range(2):
                for f in range(8):
                    P.op("pe", lambda h, pair=pair, nb=nb, f=f, li=li, wd=wd: h.matmul(K.pp[pair][:, nb * 512:(nb + 1) * 512], lhsT=hdnT[:, f, li * 128:(li + 1) * 128], rhs=wd[:, f, nb * 512:(nb + 1) * 512], start=(f == 0), stop=(f == 7)),
                         reads=["hdnT.%d.%d" % (f, b_) for b_ in range(nblk)] + [wdk], writes=["ps%d" % (2 * pair + nb)])
            pk = ["ps%d" % (2 * pair), "ps%d" % (2 * pair + 1)]
            if c == 0:
                evac(P, "act", acc[:, li, :], K.pp[pair][:, :], pk, ["acc%d" % li])
            else:
                P.op("dve", lambda h, li=li, pair=pair: h.tensor_tensor(out=acc[:, li, :], in0=acc[:, li, :], in1=K.pp[pair][:, :], op=ALU.add), reads=pk + ["acc%d" % li], writes=["acc%d" % li])
    g_post, gpok = K.load_gain(I["norm_mlp_post"][L])
    for li in range(NT):
        K.post_norm_add(li, acc[:, li, :], ["acc%d" % li], g_post, gpok)
    P.barrier()


def from_mla(K):
    P, I, O = K.P, K.I, K.O
    C = Carver(K.arena)
    TOK, NT, grp = K.tok, K.ntile, K.grp
    ptiles = [idx for (kind, idx) in grp if kind == "P"]
    has_s = len(ptiles) < len(grp)
    tp = 128 * len(ptiles)
    invf = K.invf[:, 0:1]
    cqT = C(3 * TOK // 2, BF16, [3, TOK])
    TQ = C(TOK)
    CS = C(NT * 128, F32, [NT, 128])
    glat = C(256)
    gq = C(384)
    off_wkr2 = C.off
    wkr2 = C(512, BF16, [8, 128])
    off_wqr = C.off
    wqr = C(1536, BF16, [3, 8, 128])
    off_wukT = C.off
    wukT = C(1024, BF16, [8, 256])
    oT = C(4 * TOK, BF16, [8, TOK])
    ckvf = C(256)
    ab = C(128)
    kr2 = C(128)
    if has_s:
        qSl = C(1024, BF16, [2, 128, 8])
        qSr = C(512, BF16, [128, 8])
        KTn = C(192, BF16, [3, 128])
        Vn = C(128, BF16)
        olS = C(1024, BF16, [2, 128, 8])
    base = C.off
    xI = C(TOK).bitcast(I32)
    rr = C(TOK)
    stg = C(2048, F32, [2, 1024])
    cqf = C(384)
    C.off = base
    qn = [C(TOK // 2, BF16) for _ in range(2)]
    qlat = [C(TOK, BF16, [2, TOK]) for _ in range(2)]
    qs = [C(TOK // 2, BF16) for _ in range(2)]
    pb = [C(1024, BF16) for _ in range(2)]
    Dm = [C(64, BF16) for _ in range(2)]
    pT = [C(1024, BF16, [16, 128]) for _ in range(2)]
    olT = [C(128, BF16, [2, 128]) for _ in range(2)]
    sm = [C(16) for _ in range(2)]

    g_in, gk = K.load_gain(I["kv_in_norm"])
    for li in range(NT):
        K.norm_to_xnT(li, g_in, gk, li * 128)
    wdkv, wdkvk = K.load_w(I["w_dkv"], 256)
    wkr, wkrk = K.load_w(I["w_kr"], 64)
    P.op("act", lambda h: h.copy(out=wkr2[:, :, 0:64], in_=wkr), reads=[wkrk], writes=["wkr2"])
    P.op("act", lambda h: h.mul(out=wkr2[:, :, 64:96], in_=wkr[:, :, 32:64], mul=-1.0), reads=[wkrk, "wkr2"], writes=["wkr2"])
    P.op("act", lambda h: h.copy(out=wkr2[:, :, 96:128], in_=wkr[:, :, 0:32]), reads=[wkrk, "wkr2"], writes=["wkr2"])
    P.dma("sp", lambda h: h.dma_start(out=glat, in_=I["kv_latent_norm"].rearrange("(o n) -> o n", o=1).to_broadcast([128, 256])), writes=["glat"])
    P.dma("sp", lambda h: h.dma_start(out=gq, in_=I["q_norm"][0].rearrange("(o n) -> o n", o=1).to_broadcast([128, 384])), writes=["gq"])
    P.op("dve", lambda h: h.tensor_scalar(out=xI[:, 0:tp], in0=K.iota_g[:, 0:tp], scalar1=invf, scalar2=None, op0=ALU.mult), reads=["iota_g", "invf"], writes=["xI"])
    P.op("dve", lambda h: h.scalar_tensor_tensor(out=rr[:, 0:tp], in0=K.iota_g[:, 0:tp], scalar=invf, in1=xI[:, 0:tp], op0=ALU.mult, op1=ALU.subtract), reads=["iota_g", "invf", "xI"], writes=["rr"])
    if has_s:
        P.op("dve", lambda h: h.tensor_scalar(out=xI[:, tp:TOK], in0=K.iota_g[:, tp:TOK], scalar1=float(PAST), scalar2=invf, op0=ALU.add, op1=ALU.mult), reads=["iota_g", "invf", "xI"], writes=["xI"])
        P.op("dve", lambda h: h.tensor_scalar(out=rr[:, tp:TOK], in0=K.iota_g[:, tp:TOK], scalar1=float(PAST), scalar2=invf, op0=ALU.add, op1=ALU.mult), reads=["iota_g", "invf", "rr"], writes=["rr"])
        P.op("dve", lambda h: h.tensor_tensor(out=rr[:, tp:TOK], in0=rr[:, tp:TOK], in1=xI[:, tp:TOK], op=ALU.subtract), reads=["xI", "rr"], writes=["rr"])
    P.op("act", lambda h: h.activation(out=TQ[64:128, 0:TOK], in_=rr[64:128, 0:TOK], func=AF.Sin, scale=TWO_PI), reads=["rr"], writes=["TQs"])
    P.op("act", lambda h: h.activation(out=TQ[0:64, 0:TOK], in_=rr[0:64, 0:TOK], func=AF.Abs), reads=["rr"], writes=["TQc"])
    P.op("act", lambda h: h.activation(out=TQ[0:64, 0:TOK], in_=TQ[0:64, 0:TOK], func=AF.Sin, scale=-TWO_PI, bias=K.halfpi[0:64]), reads=["TQc"], writes=["TQc"])
    for li in range(NT):
        P.op("pe", lambda h, li=li: h.transpose(out=K.ps[3][:, 0:128], in_=TQ[:, li * 128:(li + 1) * 128], identity=K.identf[:]), reads=["TQs", "TQc", "identf"], writes=["ps3"])
        evac(P, "act", CS[:, li, :], K.ps[3][:, 0:128], ["ps3"], ["CS"])
    xk = ["xnT.%d" % k for k in range(8)]
    for li, (kind, idx) in enumerate(grp):
        tcols = slice(li * 128, (li + 1) * 128)
        for k in range(8):
            P.op("pe", lambda h, k=k, tcols=tcols: h.matmul(K.ps[0][:, 0:256], lhsT=K.xnT[:, k, tcols], rhs=wdkv[:, k, :], start=(k == 0), stop=(k == 7)), reads=xk + [wdkvk], writes=["ps0"])
        for k in range(8):
            P.op("pe", lambda h, k=k, tcols=tcols: h.matmul(K.ps[1][:, 0:128], lhsT=K.xnT[:, k, tcols], rhs=wkr2[:, k, :], start=(k == 0), stop=(k == 7)), reads=xk + ["wkr2"], writes=["ps1"])
        rs, rk = K.rms_stats(K.ps[0][:, 0:256], ["ps0"], 256)
        P.op("dve", lambda h, rs=rs: h.scalar_tensor_tensor(out=ckvf, in0=K.ps[0][:, 0:256], scalar=rs, in1=glat, op0=ALU.mult, op1=ALU.mult), reads=["ps0", rk, "glat"], writes=["ckvf"])
        if kind == "P":
            P.dma("sp", lambda h, idx=idx: h.dma_start(out=O["kvl_p"][idx * 128:(idx + 1) * 128, :], in_=ckvf), reads=["ckvf"], is_output=True)
            P.op("pool", lambda h, idx=idx: h.tensor_copy(out=K.Vp[:, idx, :], in_=ckvf), reads=["ckvf"], writes=["Vp"])
            K.transpose_to(ckvf, "ckvf", K.KT, "KT", idx * 128, nk=2, banks=(2,))
        else:
            P.dma("sp", lambda h: h.dma_start(out=O["kvl_s"][:, :], in_=ckvf), reads=["ckvf"], is_output=True)
            P.op("pool", lambda h: h.tensor_copy(out=Vn, in_=ckvf), reads=["ckvf"], writes=["Vn"])
            K.transpose_to(ckvf, "ckvf", KTn, "KTn", 0, nk=2, banks=(2,))
        P.op("dve", lambda h, li=li: h.tensor_tensor(out=ab, in0=K.ps[1][:, 0:128], in1=CS[:, li, :], op=ALU.mult), reads=["ps1", "CS"], writes=["ab"])
        P.op("pool", lambda h: h.tensor_tensor(out=kr2[:, 0:64], in0=ab[:, 0:64], in1=ab[:, 64:128], op=ALU.add), reads=["ab"], writes=["kr2"])
        P.op("pool", lambda h: h.tensor_copy(out=kr2[:, 64:128], in_=kr2[:, 0:64]), reads=["kr2"], writes=["kr2"])
        if kind == "P":
            P.dma("sp", lambda h, idx=idx: h.dma_start(out=O["kr_p"][idx * 128:(idx + 1) * 128, :], in_=kr2[:, 0:64]), reads=["kr2"], is_output=True)
            K.transpose_to(kr2, "kr2", K.KT[:, 2:3, :], "KTr", idx * 128, nk=1, banks=(3,))
        else:
            P.dma("sp", lambda h: h.dma_start(out=O["kr_s"][:, :], in_=kr2[:, 0:64]), reads=["kr2"], is_output=True)
            K.transpose_to(kr2, "kr2", KTn[:, 2:3, :], "KTnr", 0, nk=1, banks=(3,))

    g_pre, gpk = K.load_gain(I["norm_mix_pre"][1])
    for li in range(NT):
        K.norm_to_xnT(li, g_pre, gpk, li * 128)
    wdq, wdqk = K.load_w(I["w_dq"][0], 384)
    for li in range(NT):
        tcols = slice(li * 128, (li + 1) * 128)
        bank = 4 + li % 2
        for k in range(8):
            P.op("pe", lambda h, k=k, tcols=tcols, bank=bank: h.matmul(K.ps[bank][:, 0:384], lhsT=K.xnT[:, k, tcols], rhs=wdq[:, k, :], start=(k == 0), stop=(k == 7)), reads=xk + [wdqk], writes=["ps%d" % bank])
        rs, rk = K.rms_stats(K.ps[bank][:, 0:384], ["ps%d" % bank], 384)
        P.op("dve", lambda h, rs=rs, bank=bank: h.scalar_tensor_tensor(out=cqf, in0=K.ps[bank][:, 0:384], scalar=rs, in1=gq, op0=ALU.mult, op1=ALU.mult), reads=["ps%d" % bank, rk, "gq"], writes=["cqf"])
        K.transpose_to(cqf, "cqf", cqT, "cqT", li * 128, nk=3, banks=(6,))
    cqk = ["cqT.%d" % k for k in range(3)]
    wuq, wuqk = K.load_w(I["w_uq"][0].rearrange("r h d -> r (h d)"), 1536, nk=3)
    wuq4 = wuq.rearrange("p k (h d) -> p k h d", d=192)
    P.op("act", lambda h: h.copy(out=wqr[:, :, :, 0:64], in_=wuq4[:, :, :, 128:192]), reads=[wuqk], writes=["wqr"])
    P.op("act", lambda h: h.mul(out=wqr[:, :, :, 64:96], in_=wuq4[:, :, :, 160:192], mul=-1.0), reads=[wuqk, "wqr"], writes=["wqr"])
    P.op("act", lambda h: h.copy(out=wqr[:, :, :, 96:128], in_=wuq4[:, :, :, 128:160]), reads=[wuqk, "wqr"], writes=["wqr"])
    P.dma("sp", lambda h: h.dma_start(out=stg, in_=I["w_uk"].rearrange("(c p) h d -> p c (h d)", p=128)), writes=["stg"])
    for hh in range(8):
        for rc in range(2):
            t_ = hh * 2 + rc
            P.op("pe", lambda h, hh=hh, rc=rc, t_=t_: h.transpose(out=K.pp[t_ // 8][:, (t_ % 8) * 128:(t_ % 8 + 1) * 128], in_=stg[:, rc, hh * 128:(hh + 1) * 128], identity=K.identf[:]),
                 reads=["stg", "identf"], writes=["ps%d" % (2 * (t_ // 8) + (t_ % 8) // 4)])
    evac(P, "act", wukT[:, 0:4, :], K.pp[0][:, :].rearrange("p (a b) -> p a b", b=256), ["ps0", "ps1"], ["wukT"])
    evac(P, "dve", wukT[:, 4:8, :], K.pp[1][:, :].rearrange("p (a b) -> p a b", b=256), ["ps2", "ps3"], ["wukT"])
    wuv, wuvk = K.load_w(I["w_uv"].rearrange("r h v -> r (h v)"), 1024, nk=2)
    wo, wok = K.load_w(I["w_o"][0], 1024)
    P.barrier()

    nblk = 2
    bs = TOK // nblk
    cnt = 0
    for hh in range(8):
        hb = hh % 2
        qk = "q%d" % hb
        for blk in range(nblk):
            cs = slice(blk * bs, (blk + 1) * bs)
            for kc in range(3):
                P.op("pe", lambda h, hh=hh, kc=kc, cs=cs: h.matmul(K.ps[4][:, 0:bs], lhsT=wuq[:, kc, hh * 192:hh * 192 + 128], rhs=cqT[:, kc, cs], start=(kc == 0), stop=(kc == 2)), reads=cqk + [wuqk], writes=["ps4"])
            evac(P, "act", qn[hb][:, cs], K.ps[4][:, 0:bs], ["ps4"], [qk + "n"])
            for rc in range(2):
                P.op("pe", lambda h, hh=hh, rc=rc, cs=cs, hb=hb: h.matmul(K.ps[5 + rc][:, 0:bs], lhsT=wukT[:, hh, rc * 128:(rc + 1) * 128], rhs=qn[hb][:, cs], start=True, stop=True), reads=["wukT", qk + "n"], writes=["ps%d" % (5 + rc)])
                evac(P, "dve" if rc == 0 else "act", qlat[hb][:, rc, cs], K.ps[5 + rc][:, 0:bs], ["ps%d" % (5 + rc)], [qk + "l"])
            for kc in range(3):
                P.op("pe", lambda h, hh=hh, kc=kc, cs=cs: h.matmul(K.ps[7][:, 0:bs], lhsT=wqr[:, kc, hh, :], rhs=cqT[:, kc, cs], start=(kc == 0), stop=(kc == 2)), reads=cqk + ["wqr"], writes=["ps7"])
            P.op("dve", lambda h, cs=cs, hb=hb: h.tensor_tensor(out=qs[hb][:, cs], in0=K.ps[7][:, 0:bs], in1=TQ[:, cs], op=ALU.mult), reads=["ps7", "TQs", "TQc"], writes=[qk + "s"])
        if has_s:
            P.op("pool", lambda h, hh=hh, hb=hb: h.tensor_copy(out=qSl[:, :, :, hh], in_=qlat[hb][:, :, tp:TOK]), reads=[qk + "l"], writes=["qSl"])
            P.op("pool", lambda h, hh=hh, hb=hb: h.tensor_copy(out=qSr[:, :, hh], in_=qs[hb][:, tp:TOK]), reads=[qk + "s"], writes=["qSr"])
        for li, (kind, idx) in enumerate(grp):
            if kind != "P":
                continue
            bi = cnt % 2
            cnt += 1
            tag = "a%d" % bi
            tcols = slice(li * 128, (li + 1) * 128)
            nk = idx + 1
            nkeys = 128 * nk
            nkb = (nkeys + 511) // 512
            for kb in range(nkb):
                w = min(512, nkeys - kb * 512)
                kcols = slice(kb * 512, kb * 512 + w)
                P.op("pe", lambda h, kb=kb, w=w, kcols=kcols, tcols=tcols, hb=hb: h.matmul(K.ps[kb][:, 0:w], lhsT=qlat[hb][:, 0, tcols], rhs=K.KT[:, 0, kcols], start=True, stop=False), reads=[qk + "l", "KT.0"], writes=["ps%d" % kb])
                P.op("pe", lambda h, kb=kb, w=w, kcols=kcols, tcols=tcols, hb=hb: h.matmul(K.ps[kb][:, 0:w], lhsT=qlat[hb][:, 1, tcols], rhs=K.KT[:, 1, kcols], start=False, stop=False), reads=[qk + "l", "KT.1"], writes=["ps%d" % kb])
                P.op("pe", lambda h, kb=kb, w=w, kcols=kcols, tcols=tcols, hb=hb: h.matmul(K.ps[kb][:, 0:w], lhsT=qs[hb][:, tcols], rhs=K.KT[:, 2, kcols], start=False, stop=True), reads=[qk + "s", "KTr.0"], writes=["ps%d" % kb])
            db = (idx * 128) // 512
            do = (idx * 128) % 512
            P.op("dve", lambda h, db=db, do=do: h.tensor_tensor(out=K.ps[db][:, do:do + 128], in0=K.ps[db][:, do:do + 128], in1=K.cm[:], op=ALU.add), reads=["ps%d" % db, "cm"], writes=["ps%d" % db])
            n0 = min(nkeys, 1024)
            n1 = nkeys - n0
            smt = sm[bi]
            mx, nb_, l0, l1, rl = smt[:, 0:1], smt[:, 2:3], smt[:, 3:4], smt[:, 4:5], smt[:, 5:6]
            P.op("dve", lambda h, n0=n0, mx=mx: h.tensor_reduce(out=mx, in_=K.pp[0][:, 0:n0], axis=AX.X, op=ALU.max), reads=["ps0", "ps1"], writes=[tag + "sm"])
            if n1 > 0:
                m1 = smt[:, 1:2]
                P.op("dve", lambda h, n1=n1, m1=m1: h.tensor_reduce(out=m1, in_=K.pp[1][:, 0:n1], axis=AX.X, op=ALU.max), reads=["ps2", "ps3", tag + "sm"], writes=[tag + "sm"])
                P.op("dve", lambda h, mx=mx, m1=m1: h.tensor_tensor(out=mx, in0=mx, in1=m1, op=ALU.max), reads=[tag + "sm"], writes=[tag + "sm"])
            P.op("dve", lambda h, mx=mx, nb_=nb_: h.tensor_scalar(out=nb_, in0=mx, scalar1=-MLA_SCALE, scalar2=None, op0=ALU.mult), reads=[tag + "sm"], writes=[tag + "sm"])
            P.op("act", lambda h, n0=n0, bi=bi, nb_=nb_, l0=l0: h.activation(out=pb[bi][:, 0:n0], in_=K.pp[0][:, 0:n0], func=AF.Exp, scale=MLA_SCALE, bias=nb_, accum_out=l0), reads=["ps0", "ps1", tag + "sm"], writes=[tag + "p", tag + "l"])
            if n1 > 0:
                P.op("act", lambda h, n0=n0, n1=n1, bi=bi, nb_=nb_, l1=l1: h.activation(out=pb[bi][:, n0:n0 + n1], in_=K.pp[1][:, 0:n1], func=AF.Exp, scale=MLA_SCALE, bias=nb_, accum_out=l1), reads=["ps2", "ps3", tag + "sm", tag + "l"], writes=[tag + "p", tag + "l"])
                P.op("dve", lambda h, l0=l0, l1=l1: h.tensor_tensor(out=l0, in0=l0, in1=l1, op=ALU.add), reads=[tag + "l", tag + "sm"], writes=[tag + "l"])
            P.op("dve", lambda h, l0=l0, rl=rl: h.reciprocal(out=rl, in_=l0), reads=[tag + "l", tag + "sm"], writes=[tag + "sm"])
            P.op("dve", lambda h, bi=bi, rl=rl: h.tensor_scalar(out=Dm[bi], in0=K.ident[:], scalar1=rl, scalar2=None, op0=ALU.mult), reads=["ident", tag + "sm"], writes=[tag + "D"])
            for kt in range(nk):
                P.op("pe", lambda h, kt=kt, bi=bi: h.matmul(K.pp[2 + kt // 8][:, (kt % 8) * 128:(kt % 8 + 1) * 128], lhsT=pb[bi][:, kt * 128:(kt + 1) * 128], rhs=Dm[bi], start=True, stop=True),
                     reads=[tag + "p", tag + "D"], writes=["ps%d" % (4 + 2 * (kt // 8) + (kt % 8) // 4)])
            na = min(nk, 8)
            evac(P, "act", pT[bi][:, 0:na, :], K.pp[2][:, 0:na * 128].rearrange("p (a b) -> p a b", b=128), ["ps4", "ps5"], [tag + "pTa"])
            if nk > 8:
                evac(P, "dve", pT[bi][:, 8:nk, :], K.pp[3][:, 0:(nk - 8) * 128].rearrange("p (a b) -> p a b", b=128), ["ps6", "ps7"], [tag + "pTb"])
            for rc in range(2):
                for kt in range(nk):
                    P.op("pe", lambda h, rc=rc, kt=kt, bi=bi: h.matmul(K.ps[0][:, rc * 128:(rc + 1) * 128], lhsT=K.Vp[:, kt, rc * 128:(rc + 1) * 128], rhs=pT[bi][:, kt, :], start=(kt == 0), stop=(kt == nk - 1)),
                         reads=["Vp", tag + "pTa", tag + "pTb"], writes=["ps0"])
            evac(P, "act", olT[bi], K.ps[0][:, 0:256].rearrange("p (a b) -> p a b", b=128), ["ps0"], [tag + "ol"])
            for rc in range(2):
                P.op("pe", lambda h, rc=rc, bi=bi, hh=hh: h.matmul(K.ps[1][:, 0:128], lhsT=wuv[:, rc, hh * 128:(hh + 1) * 128], rhs=olT[bi][:, rc, :], start=(rc == 0), stop=(rc == 1)), reads=[wuvk, tag + "ol"], writes=["ps1"])
            evac(P, "dve", oT[:, hh, tcols], K.ps[1][:, 0:128], ["ps1"], ["oT.%d" % li])

    if has_s:
        mla_sample(K, C, locals())

    g_post, gpok = K.load_gain(I["norm_mix_post"][1])
    for li in range(NT):
        tcols = slice(li * 128, (li + 1) * 128)
        pair = 2 + li % 2
        for nb in range(2):
            for hh in range(8):
                P.op("pe", lambda h, nb=nb, hh=hh, tcols=tcols, pair=pair: h.matmul(K.pp[pair][:, nb * 512:(nb + 1) * 512], lhsT=oT[:, hh, tcols], rhs=wo[:, hh, nb * 512:(nb + 1) * 512], start=(hh == 0), stop=(hh == 7)),
                     reads=["oT.%d" % li, wok], writes=["ps%d" % (2 * pair + nb)])
        K.post_norm_add(li, K.pp[pair][:, :], ["ps%d" % (2 * pair), "ps%d" % (2 * pair + 1)], g_post, gpok)
    P.barrier()


def mla_sample(K, C, env):
    P, I, O = K.P, K.I, K.O
    qSl, qSr, KTn, Vn, olS, oT, wuv, wuvk = (env[k] for k in ("qSl", "qSr", "KTn", "Vn", "olS", "oT", "wuv", "wuvk"))
    tp, TOK = env["tp"], env["TOK"]
    P.barrier()
    C.off = env["base"]
    stgK = [C(2048), K.arena[:, 0:2048]]
    stgR = [C(512), K.arena[:, env["off_wkr2"]:env["off_wkr2"] + 512]]
    krd = K.arena[:, env["off_wqr"]:env["off_wqr"] + 1024].rearrange("p (a b) -> p a b", b=128)
    Vb = [K.arena[:, env["off_wukT"]:env["off_wukT"] + 1024].bitcast(BF16).rearrange("p (a b) -> p a b", b=256),
          K.xn[:].bitcast(BF16).rearrange("p (a b) -> p a b", b=256)]
    KTb = C(1536, BF16, [3, 1024])
    pS = C(512, BF16)
    pTs = C(256, BF16, [8, 64])
    Oacc = C(256)
    On = C(256)
    mS = C(16)
    maskb = C(128)
    ptf = stgK[0][:, 0:1024]
    tmpx = stgK[0][:, 1024:2048]
    idx = C(128).bitcast(I32)
    Msel = C(8)
    pm = C(2)
    pti = tmpx.bitcast(I32)

    P.dma("sp", lambda h: h.dma_start(out=pti, in_=I["pt"].rearrange("b (o n) -> o (b n)", o=1).to_broadcast([128, 1024])), writes=["tmpx"])
    P.op("dve", lambda h: h.tensor_copy(out=ptf, in_=pti), reads=["tmpx"], writes=["ptf"])
    P.op("pool", lambda h: h.memset(Msel, 1.0), writes=["Msel"])
    P.op("pool", lambda h: h.affine_select(out=Msel, in_=Msel, pattern=[[-16, 8]], compare_op=ALU.is_ge, fill=0.0, base=0, channel_multiplier=1), reads=["Msel"], writes=["Msel"])
    P.op("pool", lambda h: h.affine_select(out=Msel, in_=Msel, pattern=[[16, 8]], compare_op=ALU.is_ge, fill=0.0, base=15, channel_multiplier=-1), reads=["Msel"], writes=["Msel"])
    P.op("dve", lambda h: h.tensor_tensor(out=tmpx.rearrange("p (a j) -> p a j", j=8), in0=ptf.rearrange("p (a j) -> p a j", j=8),
                                          in1=Msel.unsqueeze(1).to_broadcast([128, 128, 8]), op=ALU.mult), reads=["ptf", "Msel", "tmpx"], writes=["tmpx"])
    P.op("dve", lambda h: h.tensor_reduce(out=ptf[:, 0:128], in_=tmpx.rearrange("p (a j) -> p a j", j=8), axis=AX.X, op=ALU.add), reads=["tmpx", "ptf"], writes=["ptf"])
    pmi = pm.bitcast(I32)
    P.op("pool", lambda h: h.iota(pmi[:, 0:1], pattern=[[0, 1]], base=0, channel_multiplier=1), writes=["pm"])
    P.op("dve", lambda h: h.tensor_single_scalar(out=pmi[:, 1:2], in_=pmi[:, 0:1], scalar=15, op=ALU.bitwise_and), reads=["pm"], writes=["pm"])
    P.op("dve", lambda h: h.tensor_copy(out=pm[:, 0:1], in_=pmi[:, 1:2]), reads=["pm"], writes=["pm"])
    P.op("dve", lambda h: h.tensor_scalar(out=idx, in0=ptf[:, 0:128], scalar1=16.0, scalar2=pm[:, 0:1], op0=ALU.mult, op1=ALU.add), reads=["ptf", "pm"], writes=["idx"])

    P.barrier()
    ident64 = K.ident[0:64, 0:64]
    blocks = [(b, d) for b in range(16) for d in range(8)]

    def issue_gather(n):
        b, d = blocks[n]
        sb = n % 2
        col = b * 8 + d
        P.dma("pool", lambda h: h.indirect_dma_start(out=stgK[sb], out_offset=None, in_=I["ckvc"], in_offset=bass.IndirectOffsetOnAxis(ap=idx[:, col:col + 1], axis=0)), reads=["idx"], writes=["stgK%d" % sb])
        P.dma("pool", lambda h: h.indirect_dma_start(out=stgR[sb], out_offset=None, in_=I["krc"], in_offset=bass.IndirectOffsetOnAxis(ap=idx[:, col:col + 1], axis=0)), reads=["idx"], writes=["stgR%d" % sb])

    m_, l_, bm, mn, corr, nb_, ls = (mS[0:64, i:i + 1] for i in range(7))

    def softmax_update(nkeys, nslot, vsrc, vkey):
        S = K.pp[0][0:64, 0:nkeys]
        sk = ["ps0", "ps1"] if nkeys > 512 else ["ps0"]
        P.op("dve", lambda h: h.tensor_reduce(out=bm, in_=S, axis=AX.X, op=ALU.max), reads=sk + ["mS"], writes=["mS"])
        P.op("dve", lambda h: h.tensor_tensor(out=mn, in0=m_, in1=bm, op=ALU.max), reads=["mS"], writes=["mS"])
        P.op("dve", lambda h: h.tensor_tensor(out=corr, in0=m_, in1=mn, op=ALU.subtract), reads=["mS"], writes=["mS"])
        P.op("act", lambda h: h.activation(out=corr, in_=corr, func=AF.Exp, scale=MLA_SCALE), reads=["mS"], writes=["mS"])
        P.op("dve", lambda h: h.tensor_scalar(out=nb_, in0=mn, scalar1=-MLA_SCALE, scalar2=None, op0=ALU.mult), reads=["mS"], writes=["mS"])
        P.op("act", lambda h: h.activation(out=pS[0:64, 0:nkeys], in_=S, func=AF.Exp, scale=MLA_SCALE, bias=nb_, accum_out=ls), reads=sk + ["mS"], writes=["pS", "mS"])
        P.op("dve", lambda h: h.scalar_tensor_tensor(out=l_, in0=l_, scalar=corr, in1=ls, op0=ALU.mult, op1=ALU.add), reads=["mS"], writes=["mS"])
        P.op("dve", lambda h: h.tensor_copy(out=m_, in_=mn), reads=["mS"], writes=["mS"])
        for s_ in range(nslot):
            P.op("pe", lambda h, s_=s_: h.matmul(K.ps[5][:, s_ * 64:(s_ + 1) * 64], lhsT=pS[0:64, s_ * 128:(s_ + 1) * 128], rhs=ident64, start=True, stop=True), reads=["pS", "ident"], writes=["ps5"])
        evac(P, "act", pTs[:, 0:nslot, :], K.ps[5][:, 0:nslot * 64].rearrange("p (a b) -> p a b", b=64), ["ps5"], ["pTs"])
        for s_ in range(nslot):
            P.op("pe", lambda h, s_=s_: h.matmul(K.ps[6][0:64, 0:256], lhsT=pTs[:, s_, :], rhs=vsrc(s_), start=(s_ == 0), stop=(s_ == nslot - 1)), reads=["pTs", vkey], writes=["ps6"])
        P.op("dve", lambda h: h.scalar_tensor_tensor(out=Oacc[0:64, :], in0=Oacc[0:64, :], scalar=corr, in1=K.ps[6][0:64, 0:256], op0=ALU.mult, op1=ALU.add), reads=["Oacc", "ps6", "mS"], writes=["Oacc"])

    issue_gather(0)
    for n, (b, d) in enumerate(blocks):
        sb = n % 2
        if n + 1 < len(blocks):
            issue_gather(n + 1)
        qcols = slice(b * 8, (b + 1) * 8)
        lq = [qSl[:, rc, qcols, :].rearrange("p t h -> p (t h)") for rc in range(2)]
        rq = qSr[:, qcols, :].rearrange("p t h -> p (t h)")
        if d == 0:
            P.op("dve", lambda h: h.memset(m_, NEG), reads=["mS"], writes=["mS"])
            P.op("dve", lambda h: h.memset(l_, 0.0), reads=["mS"], writes=["mS"])
            P.op("pool", lambda h: h.memset(Oacc[0:64, :], 0.0), reads=["Oacc"], writes=["Oacc"])
            P.op("pool", lambda h: h.memset(maskb[0:64, :], 0.0), reads=["maskb"], writes=["maskb"])
            P.op("pool", lambda h, b=b: h.affine_select(out=maskb[0:64, :], in_=maskb[0:64, :], pattern=[[1, 128]], compare_op=ALU.is_ge, fill=NEG, base=-8 * b, channel_multiplier=0), reads=["maskb"], writes=["maskb"])
            P.op("pool", lambda h, b=b: h.affine_select(out=maskb[0:64, :], in_=maskb[0:64, :], pattern=[[-8, 128]], compare_op=ALU.is_ge, fill=NEG, base=64 * b, channel_multiplier=1), reads=["maskb"], writes=["maskb"])
        sK, sR, vB = stgK[sb], stgR[sb], Vb[sb]
        kK, kR, kV = "stgK%d" % sb, "stgR%d" % sb, "Vb%d" % sb
        P.op("act", lambda h, sK=sK, vB=vB: h.copy(out=vB, in_=sK.rearrange("p (s r) -> p s r", r=256)), reads=[kK], writes=[kV])
        P.op("dve", lambda h, sR=sR: h.tensor_copy(out=krd[:, :, 0:64], in_=sR.rearrange("p (s r) -> p s r", r=64)), reads=[kR], writes=["krd"])
        P.op("dve", lambda h, sR=sR: h.tensor_copy(out=krd[:, :, 64:128], in_=sR.rearrange("p (s r) -> p s r", r=64)), reads=[kR, "krd"], writes=["krd"])
        for half in range(2):
            for sl in range(4):
                s_ = half * 4 + sl
                for c in range(3):
                    src = sK[:, s_ * 256 + c * 128:s_ * 256 + (c + 1) * 128] if c < 2 else krd[:, s_, :]
                    P.op("pe", lambda h, src=src, c=c, sl=sl: h.transpose(out=K.ps[2 + c][:, sl * 128:(sl + 1) * 128], in_=src, identity=K.identf[:]),
                         reads=[kK if c < 2 else "krd", "identf"], writes=["ps%d" % (2 + c)])
            for c in range(3):
                evac(P, ("act", "dve", "act")[c] if half == 0 else ("dve", "act", "dve")[c], KTb[:, c, half * 512:(half + 1) * 512], K.ps[2 + c][:, :], ["ps%d" % (2 + c)], ["KTb%d.%d" % (c, half)])
            P.op("pe", lambda h, half=half, lq=lq: h.matmul(K.ps[half][0:64, :], lhsT=lq[0], rhs=KTb[:, 0, half * 512:(half + 1) * 512], start=True, stop=False), reads=["qSl", "KTb0.%d" % half], writes=["ps%d" % half])
            P.op("pe", lambda h, half=half, lq=lq: h.matmul(K.ps[half][0:64, :], lhsT=lq[1], rhs=KTb[:, 1, half * 512:(half + 1) * 512], start=False, stop=False), reads=["qSl", "KTb1.%d" % half], writes=["ps%d" % half])
            P.op("pe", lambda h, half=half, rq=rq: h.matmul(K.ps[half][0:64, :], lhsT=rq, rhs=KTb[:, 2, half * 512:(half + 1) * 512], start=False, stop=True), reads=["qSr", "KTb2.%d" % half], writes=["ps%d" % half])
        softmax_update(1024, 8, (lambda s_, vB=vB: vB[:, s_, :]), kV)
        if d == 7:
            P.op("pe", lambda h, lq=lq: h.matmul(K.ps[0][0:64, 0:128], lhsT=lq[0], rhs=KTn[:, 0, :], start=True, stop=False), reads=["qSl", "KTn.0"], writes=["ps0"])
            P.op("pe", lambda h, lq=lq: h.matmul(K.ps[0][0:64, 0:128], lhsT=lq[1], rhs=KTn[:, 1, :], start=False, stop=False), reads=["qSl", "KTn.1"], writes=["ps0"])
            P.op("pe", lambda h, rq=rq: h.matmul(K.ps[0][0:64, 0:128], lhsT=rq, rhs=KTn[:, 2, :], start=False, stop=True), reads=["qSr", "KTnr.0"], writes=["ps0"])
            P.op("dve", lambda h: h.tensor_tensor(out=K.ps[0][0:64, 0:128], in0=K.ps[0][0:64, 0:128], in1=maskb[0:64, :], op=ALU.add), reads=["ps0", "maskb"], writes=["ps0"])
            softmax_update(128, 1, (lambda s_: Vn), "Vn")
            rl = mS[0:64, 8:9]
            P.op("dve", lambda h: h.reciprocal(out=rl, in_=l_), reads=["mS"], writes=["mS"])
            P.op("dve", lambda h: h.tensor_scalar(out=On[0:64, :], in0=Oacc[0:64, :], scalar1=rl, scalar2=None, op0=ALU.mult), reads=["Oacc", "mS"], writes=["On"])
            for rc in range(2):
                P.op("pe", lambda h, rc=rc: h.transpose(out=K.ps[7][:, rc * 64:(rc + 1) * 64], in_=On[0:64, rc * 128:(rc + 1) * 128], identity=K.identf[0:64, 0:64]), reads=["On", "identf"], writes=["ps7"])
            evac(P, "act", olS[:, :, qcols, :].rearrange("p c t h -> p c (t h)"), K.ps[7][:, 0:128].rearrange("p (c x) -> p c x", x=64), ["ps7"], ["olS"])
    for hh in range(8):
        for rc in range(2):
            P.op("pe", lambda h, hh=hh, rc=rc: h.matmul(K.ps[1][:, 0:128], lhsT=wuv[:, rc, hh * 128:(hh + 1) * 128], rhs=olS[:, rc, :, hh], start=(rc == 0), stop=(rc == 1)), reads=[wuvk, "olS"], writes=["ps1"])
        evac(P, "dve", oT[:, hh, tp:TOK], K.ps[1][:, 0:128], ["ps1"], ["oT.%d" % (K.ntile - 1)])
    P.barrier()


def _prep_inputs(inputs, c):
    m = {}
    m["xp"] = np.ascontiguousarray(inputs["x_prompt"][c])
    m["xs"] = np.ascontiguousarray(inputs["x_sample"][16 * c:16 * c + 16].reshape(128, D))
    m["ssr"] = np.ascontiguousarray(inputs["cache_ssm_re"][0, 16 * c:16 * c + 16].reshape(16, 4096))
    m["ssi"] = np.ascontiguousarray(inputs["cache_ssm_im"][0, 16 * c:16 * c + 16].reshape(16, 4096))
    m["ckvc"] = inputs["cache_kv_latent"].reshape(N_PHYS * 16, 2048)
    m["krc"] = inputs["cache_k_rope"].reshape(N_PHYS * 16, 512)
    m["memk"] = np.ascontiguousarray(inputs["cache_mem_k"][:, 16 * c:16 * c + 16].reshape(2, 16, 256, D))
    m["memv"] = np.ascontiguousarray(inputs["cache_mem_v"][:, 16 * c:16 * c + 16].reshape(2, 16, 256, D))
    m["pt"] = np.ascontiguousarray(inputs["page_table"][16 * c:16 * c + 16]).astype(np.int32)
    m["memp"] = np.ascontiguousarray(inputs["mem_prompt"][c])
    for name, shape in WEIGHT_SPECS:
        m[name] = np.ascontiguousarray(inputs[name]).reshape(shape)
    return m


_NC_CACHE = {}


def kernel(**inputs):
    inputs = {k: np.asarray(v) for k, v in inputs.items()}
    if "nc" not in _NC_CACHE:
        _NC_CACHE["nc"] = build()
    nc = _NC_CACHE["nc"]
    in_maps = [_prep_inputs(inputs, c) for c in range(NCORES)]
    res = run_bass_kernel_spmd(nc, in_maps, core_ids=list(range(NCORES)))
    R = res.results
    f = np.float32
    y_p = np.stack([R[c]["y_p"] for c in range(NCORES)]).astype(f)
    y_s = np.concatenate([R[c]["y_s"].reshape(16, 8, D) for c in range(NCORES)]).astype(f)
    ssm_re_p = np.stack([R[c]["ssm_re_p"].reshape(64, 64) for c in range(NCORES)])[None].astype(f)
    ssm_im_p = np.stack([R[c]["ssm_im_p"].reshape(64, 64) for c in range(NCORES)])[None].astype(f)
    ssm_re_s = np.concatenate([R[c]["ssm_re_s"].reshape(16, 64, 64) for c in range(NCORES)])[None].astype(f)
    ssm_im_s = np.concatenate([R[c]["ssm_im_s"].reshape(16, 64, 64) for c in range(NCORES)])[None].astype(f)
    kvl_p = np.stack([R[c]["kvl_p"] for c in range(NCORES)]).astype(f)
    kr_p = np.stack([R[c]["kr_p"] for c in range(NCORES)]).astype(f)
    kvl_s = np.concatenate([R[c]["kvl_s"].reshape(16, 8, 256) for c in range(NCORES)]).astype(f)
    kr_s = np.concatenate([R[c]["kr_s"].reshape(16, 8, 64) for c in range(NCORES)]).astype(f)
    memk_p = np.stack([R[c]["memk_p"].reshape(2, 256, 4, 256) for c in range(NCORES)], axis=1).astype(f)
    memv_p = np.stack([R[c]["memv_p"].reshape(2, 256, 4, 256) for c in range(NCORES)], axis=1).astype(f)
    return (y_p, y_s, ssm_re_p, ssm_im_p, ssm_re_s, ssm_im_s, kvl_p, kr_p, kvl_s, kr_s, memk_p, memv_p)
```

```python
import math
from contextlib import ExitStack
import numpy as np
import concourse.bass as bass
import concourse.mybir as mybir
from concourse.bass_utils import run_bass_kernel_spmd

F32 = mybir.dt.float32
BF16 = mybir.dt.bfloat16
I32 = mybir.dt.int32
AF = mybir.ActivationFunctionType
ALU = mybir.AluOpType
AX = mybir.AxisListType

NCORES = 8
D = 1024
SEQ = 2048
NPT = 16
NG = 64
NS = 64
EPS = 1e-6
TWO_PI = 2.0 * math.pi
NEG = -30000.0
MEM_SCALE = 256 ** -0.5
MLA_SCALE = 192 ** -0.5
PAST = 8192
N_PHYS = 10240

GROUPS = [[("P", i) for i in range(0, 6)],
          [("P", i) for i in range(6, 12)],
          [("P", i) for i in range(12, 16)] + [("S", 0)]]
MAXT = 6
ARENA = 19712


class Prog:
    ENG = ("pe", "act", "dve", "pool", "sp")

    def __init__(self, nc, stack, n_dma_sems=48, same_engine_sync=True):
        self.nc = nc
        self.h = {"pe": nc.tensor, "act": nc.scalar, "dve": nc.vector,
                  "pool": nc.gpsimd, "sp": nc.sync}
        self.stream = {e: [] for e in self.ENG}
        self.sem = {e: stack.enter_context(nc.semaphore("s_" + e)) for e in self.ENG}
        self.cnt = {e: 0 for e in self.ENG}
        self.dsem = [stack.enter_context(nc.semaphore("d%d" % i)) for i in range(n_dma_sems)]
        self.dcnt = [0] * n_dma_sems
        self.dnext = 0
        self.waited = {e: {} for e in self.ENG}
        self.buf = {}
        self.same = same_engine_sync
        self.out_tokens = []
        self.nops = 0

    def _deps(self, eng, reads, writes):
        toks = []
        for k in reads:
            st = self.buf.get(k)
            if st and st[0] is not None:
                toks.append(st[0])
        for k in writes:
            st = self.buf.get(k)
            if st:
                if st[0] is not None:
                    toks.append(st[0])
                toks.extend(st[1])
        need = {}
        for (sem, val, src) in toks:
            if src == eng and (eng == "pe" or not self.same):
                continue
            key = id(sem)
            if self.waited[eng].get(key, 0) >= val:
                continue
            if key not in need or need[key][1] < val:
                need[key] = (sem, val)
        for key, (sem, val) in need.items():
            self.waited[eng][key] = val
        return list(need.values())

    def _commit(self, tok, reads, writes):
        for k in reads:
            st = self.buf.setdefault(k, [None, []])
            st[1].append(tok)
        for k in writes:
            self.buf[k] = [tok, []]

    def op(self, eng, fn, reads=(), writes=()):
        waits = self._deps(eng, reads, writes)
        sem = self.sem[eng]
        self.cnt[eng] += 1
        val = self.cnt[eng]
        self.nops += 1

        def emit(h, waits=waits, fn=fn, sem=sem):
            for (s, v) in waits:
                h.wait_ge(s, v)
            fn(h).then_inc(sem, 1)
        self.stream[eng].append(emit)
        tok = (sem, val, eng)
        self._commit(tok, reads, writes)
        return tok

    def dma(self, q, fn, reads=(), writes=(), is_output=False):
        waits = self._deps(q, reads, writes)
        i = self.dnext
        self.dnext = (self.dnext + 1) % len(self.dsem)
        sem = self.dsem[i]
        prev = self.dcnt[i]
        if prev > 0 and self.waited[q].get(id(sem), 0) < prev:
            waits.append((sem, prev))
            self.waited[q][id(sem)] = prev
        self.dcnt[i] += 16
        val = self.dcnt[i]
        self.nops += 1

        def emit(h, waits=waits, fn=fn, sem=sem):
            for (s, v) in waits:
                h.wait_ge(s, v)
            fn(h).then_inc(sem, 16)
        self.stream[q].append(emit)
        tok = (sem, val, "dma")
        self._commit(tok, reads, writes)
        if is_output:
            self.out_tokens.append(tok)
        return tok

    def barrier(self):
        targets = [(self.sem[e], self.cnt[e]) for e in self.ENG if self.cnt[e] > 0]
        targets += [(self.dsem[i], self.dcnt[i]) for i in range(len(self.dsem)) if self.dcnt[i] > 0]
        for e in self.ENG:
            ws = []
            for (s, v) in targets:
                if s is self.sem[e]:
                    continue
                if self.waited[e].get(id(s), 0) >= v:
                    continue
                self.waited[e][id(s)] = v
                ws.append((s, v))

            def emit(h, ws=ws):
                for (s, v) in ws:
                    h.wait_ge(s, v)
            self.stream[e].append(emit)
        self.buf = {}

    def finish(self):
        need = {}
        for (sem, val, _) in self.out_tokens:
            k = id(sem)
            if k not in need or need[k][1] < val:
                need[k] = (sem, val)
        waits = list(need.values())

        def emit(h, waits=waits):
            for (s, v) in waits:
                h.wait_ge(s, v)
        self.stream["sp"].append(emit)
        streams = self.stream
        with self.nc.Block() as block:
            @block.tensor
            def _(e):
                for f in streams["pe"]:
                    f(e)

            @block.scalar
            def _(e):
                for f in streams["act"]:
                    f(e)

            @block.vector
            def _(e):
                for f in streams["dve"]:
                    f(e)

            @block.gpsimd
            def _(e):
                for f in streams["pool"]:
                    f(e)

            @block.sync
            def _(e):
                for f in streams["sp"]:
                    f(e)


WEIGHT_SPECS = [
    ("norm_mix_pre", [2, D]), ("norm_mix_post", [2, D]), ("norm_mem_pre", [2, D]),
    ("norm_mem_post", [2, D]), ("norm_mlp_pre", [2, D]), ("norm_mlp_post", [2, D]),
    ("mem_in_norm", [2, D]), ("w_mem_q", [2, D, D]), ("w_mem_k", [2, D, D]),
    ("w_mem_v", [2, D, D]), ("w_mem_o", [2, D, D]), ("w_mlp_up", [2, D, 4 * D]),
    ("w_mlp_down", [2, 4 * D, D]), ("ssm_a_re", [1, 64, 64]), ("ssm_a_im", [1, 64, 64]),
    ("ssm_log_dt", [1, 64]), ("ssm_b_re", [1, 64, 64, 16]), ("ssm_b_im", [1, 64, 64, 16]),
    ("ssm_c_re", [1, 64, 16, 64]), ("ssm_c_im", [1, 64, 16, 64]), ("ssm_d", [1, D]),
    ("w_glu", [1, D, 2 * D]), ("kv_in_norm", [D]), ("w_dkv", [D, 256]),
    ("kv_latent_norm", [256]), ("w_kr", [D, 64]), ("w_uk", [256, 8, 128]),
    ("w_uv", [256, 8, 128]), ("w_dq", [1, D, 384]), ("q_norm", [1, 384]),
    ("w_uq", [1, 384, 8, 192]), ("w_o", [1, D, D]),
]

IN_SPECS = [
    ("xp", [SEQ, D], F32), ("xs", [128, D], F32), ("ssr", [16, 4096], F32), ("ssi", [16, 4096], F32),
    ("ckvc", [N_PHYS * 16, 2048], F32), ("krc", [N_PHYS * 16, 512], F32),
    ("memk", [2, 16, 256, D], F32), ("memv", [2, 16, 256, D], F32), ("pt", [16, 64], I32),
    ("memp", [256, D], F32),
]

OUT_SPECS = [
    ("y_p", [SEQ, D]), ("y_s", [128, D]), ("ssm_re_p", [1, 4096]), ("ssm_im_p", [1, 4096]),
    ("ssm_re_s", [16, 4096]), ("ssm_im_s", [16, 4096]), ("kvl_p", [SEQ, 256]), ("kr_p", [SEQ, 64]),
    ("kvl_s", [128, 256]), ("kr_s", [128, 64]), ("memk_p", [2, 256, D]), ("memv_p", [2, 256, D]),
]


def build(stage=99, dbg_shape=None):
    nc = bass.Bass("TRN2", target_bir_lowering=False)
    I = {}
    for name, shape, dt in IN_SPECS:
        I[name] = nc.dram_tensor(name, shape, dt, kind="ExternalInput").ap()
    for name, shape in WEIGHT_SPECS:
        I[name] = nc.dram_tensor(name, shape, F32, kind="ExternalInput").ap()
    O = {}
    for name, shape in OUT_SPECS:
        O[name] = nc.dram_tensor(name, shape, F32, kind="ExternalOutput").ap()
    if dbg_shape is not None:
        O["dbg"] = nc.dram_tensor("dbg", dbg_shape, F32, kind="ExternalOutput").ap()

    with ExitStack() as st:
        import os
        P = Prog(nc, st, same_engine_sync=(os.environ.get('KSAME', '1') == '1'))
        K = Kern(nc, st, P, I, O, stage)
        K.run()
        P.finish()
    return nc


class Kern:
    def __init__(self, nc, st, P, I, O, stage):
        self.nc, self.st, self.P, self.I, self.O, self.stage = nc, st, P, I, O, stage
        self.uid = 0
        sb = self.sb
        self.h = sb("h", [128, MAXT, D], F32)
        self.xnT = sb("xnT", [128, 8, MAXT * 128], BF16)
        self.ident = sb("ident", [128, 128], BF16)
        self.identf = sb("identf", [128, 128], F32)
        self.gbc = [sb("gbc%d" % i, [128, D], F32) for i in range(2)]
        self.gbc_i = 0
        self.wslot = [sb("wslot%d" % i, [128, 8, 1024], BF16) for i in range(3)]
        self.ws_i = 0
        self.small = sb("small", [128, 64], F32)
        self.small_i = 0
        self.junk = sb("junk", [128, D], BF16)
        self.xn = sb("xn", [128, D], F32)
        self.iota_g = sb("iota_g", [128, MAXT * 128], F32)
        self.halfpi = sb("halfpi", [128, 1], F32)
        self.s5_carry = sb("s5_carry", [128, 32, 2], F32)
        self.arena = sb("arena", [128, ARENA], F32)
        self.KT = sb("KT", [128, 3, SEQ], BF16)
        self.Vp = sb("Vp", [128, NPT, 256], BF16)
        self.cm = sb("cm", [128, 128], F32)
        self.invf = sb("invf", [128, 2], F32)
        self.pp = [st.enter_context(nc.psum_tensor("pp%d" % i, [128, 1024], F32)) for i in range(4)]
        self.ps = [self.pp[i // 2][:, (i % 2) * 512:(i % 2 + 1) * 512] for i in range(8)]

    def sb(self, name, shape, dt):
        return self.st.enter_context(self.nc.sbuf_tensor(name, shape, dt))

    def key(self, base):
        self.uid += 1
        return "%s#%d" % (base, self.uid)

    def scal(self):
        i = self.small_i
        self.small_i = (self.small_i + 1) % 64
        return self.small[:, i:i + 1], "small%d" % i

    def load_gain(self, vec_ap):
        i = self.gbc_i
        self.gbc_i = (self.gbc_i + 1) % len(self.gbc)
        t = self.gbc[i]
        k = "gbc%d" % i
        src = vec_ap.rearrange("(o n) -> o n", o=1).to_broadcast([128, D])
        self.P.dma("sp", lambda h: h.dma_start(out=t[:], in_=src), writes=[k])
        return t, k

    def load_w(self, src_ap, ncols, nk=8):
        i = self.ws_i
        self.ws_i = (self.ws_i + 1) % len(self.wslot)
        t = self.wslot[i]
        k = "wslot%d" % i
        view = t[:].rearrange("p a b -> p (a b)")[:, 0:nk * ncols].rearrange("p (a b) -> p a b", b=ncols)
        src = src_ap.rearrange("(k p) n -> p k n", p=128)
        self.P.dma("pool", lambda h: h.dma_start(out=view, in_=src), writes=[k])
        return view, k

    def rms_stats(self, src, skey, n):
        P = self.P
        ssq, k1 = self.scal()
        rs, k2 = self.scal()
        P.op("act", lambda h: h.activation(out=self.junk[:, 0:n], in_=src, func=AF.Square, accum_out=ssq),
             reads=skey, writes=["junk", k1])
        P.op("dve", lambda h: h.tensor_scalar(out=rs, in0=ssq, scalar1=1.0 / n, scalar2=EPS, op0=ALU.mult, op1=ALU.add),
             reads=[k1], writes=[k2])
        P.op("act", lambda h: h.activation(out=rs, in_=rs, func=AF.Sqrt), reads=[k2], writes=[k2])
        P.op("dve", lambda h: h.reciprocal(out=rs, in_=rs), reads=[k2], writes=[k2])
        return rs, k2

    def norm_to_xnT(self, li, gain, gkey, col0):
        P = self.P
        hk = "h%d" % li
        rs, rk = self.rms_stats(self.h[:, li, :], [hk], D)
        P.op("dve", lambda h: h.scalar_tensor_tensor(out=self.xn[:], in0=self.h[:, li, :], scalar=rs, in1=gain[:],
                                                      op0=ALU.mult, op1=ALU.mult),
             reads=[hk, rk, gkey], writes=["xn"])
        self.transpose_to(self.xn, "xn", self.xnT, "xnT", col0)

    def transpose_to(self, src, skey, dstT, dkey, col0, nk=8, banks=(0, 1)):
        P = self.P
        for half in range((nk + 3) // 4):
            b = self.ps[banks[half % len(banks)]]
            bk = "ps%d" % banks[half % len(banks)]
            kk = range(half * 4, min(nk, half * 4 + 4))
            for k in kk:
                P.op("pe", lambda h, k=k, b=b: h.transpose(out=b[:, (k % 4) * 128:(k % 4 + 1) * 128],
                                                           in_=src[:, k * 128:(k + 1) * 128], identity=self.identf[:]),
                     reads=[skey, "identf"], writes=[bk])
            n = len(kk)
            eng = "act" if half % 2 == 0 else "dve"
            outv = dstT[:, half * 4:half * 4 + n, col0:col0 + 128]
            inv = b[:, 0:n * 128].rearrange("p (a b) -> p a b", b=128)
            if eng == "act":
                P.op("act", lambda h, outv=outv, inv=inv: h.copy(out=outv, in_=inv), reads=[bk],
                     writes=["%s.%d" % (dkey, k) for k in kk])
            else:
                P.op("dve", lambda h, outv=outv, inv=inv: h.tensor_copy(out=outv, in_=inv), reads=[bk],
                     writes=["%s.%d" % (dkey, k) for k in kk])

    def post_norm_add(self, li, src, skey, gain, gkey):
        P = self.P
        hk = "h%d" % li
        rs, rk = self.rms_stats(src, skey, D)
        P.op("dve", lambda h: h.scalar_tensor_tensor(out=self.xn[:], in0=src, scalar=rs, in1=gain[:],
                                                      op0=ALU.mult, op1=ALU.mult),
             reads=list(skey) + [rk, gkey], writes=["xn"])
        P.op("pool", lambda h: h.tensor_tensor(out=self.h[:, li, :], in0=self.h[:, li, :], in1=self.xn[:], op=ALU.add),
             reads=["xn", hk], writes=[hk])

    def setup_consts(self):
        P = self.P
        P.op("pool", lambda h: h.memset(self.identf[:], 0.0), writes=["identf"])
        P.op("pool", lambda h: h.affine_select(out=self.identf[:], in_=self.identf[:], pattern=[[-1, 128]],
                                               compare_op=ALU.not_equal, fill=1.0, base=0, channel_multiplier=1),
             reads=["identf"], writes=["identf"])
        P.op("dve", lambda h: h.tensor_copy(out=self.ident[:], in_=self.identf[:]), reads=["identf"], writes=["ident"])
        P.op("pool", lambda h: h.memset(self.halfpi[:], math.pi / 2), writes=["halfpi"])
        P.op("pool", lambda h: h.memset(self.cm[:], 0.0), writes=["cm"])
        P.op("pool", lambda h: h.affine_select(out=self.cm[:], in_=self.cm[:], pattern=[[-1, 128]], compare_op=ALU.is_ge, fill=NEG,
                                               base=0, channel_multiplier=1), reads=["cm"], writes=["cm"])
        iv = self.invf[:, 0:2].bitcast(I32)
        P.op("pool", lambda h: h.iota(iv[:, 0:1], pattern=[[0, 1]], base=0, channel_multiplier=1), writes=["invf"])
        P.op("dve", lambda h: h.tensor_single_scalar(out=iv[:, 1:2], in_=iv[:, 0:1], scalar=31, op=ALU.bitwise_and), reads=["invf"], writes=["invf"])
        P.op("dve", lambda h: h.tensor_copy(out=self.invf[:, 0:1], in_=iv[:, 1:2]), reads=["invf"], writes=["invf"])
        P.op("act", lambda h: h.activation(out=self.invf[:, 0:1], in_=self.invf[:, 0:1], func=AF.Exp, scale=-math.log(10000.0) / 32.0), reads=["invf"], writes=["invf"])
        P.op("dve", lambda h: h.tensor_scalar(out=self.invf[:, 0:1], in0=self.invf[:, 0:1], scalar1=1.0 / TWO_PI, scalar2=None, op0=ALU.mult), reads=["invf"], writes=["invf"])

    def run(self):
        P = self.P
        self.setup_consts()
        import os
        only = os.environ.get('KGROUPS')
        for gi, grp in enumerate(GROUPS):
            if only is not None and str(gi) not in only:
                continue
            self.grp = grp
            self.gi = gi
            self.ntile = len(grp)
            self.tok = 128 * len(grp)
            self.load_group()
            self.layer(0)
            if self.stage >= 4:
                self.layer(1)
            self.store_group()
        if "dbg" in self.O:
            pass

    def load_group(self):
        P = self.P
        ptiles = [idx for (kind, idx) in self.grp if kind == "P"]
        tp = 128 * len(ptiles)
        P.op("pool", lambda h: h.iota(self.iota_g[:, 0:tp], pattern=[[1, tp]], base=ptiles[0] * 128, channel_multiplier=0,
                                      allow_small_or_imprecise_dtypes=True), reads=["iota_g"], writes=["iota_g"])
        if len(ptiles) < len(self.grp):
            P.op("pool", lambda h: h.iota(self.iota_g[:, tp:tp + 128], pattern=[[0, 16], [1, 8]], base=0, channel_multiplier=0,
                                          allow_small_or_imprecise_dtypes=True), reads=["iota_g"], writes=["iota_g"])
        for li, (kind, idx) in enumerate(self.grp):
            src = self.I["xp"][idx * 128:(idx + 1) * 128, :] if kind == "P" else self.I["xs"][:, :]
            P.dma("sp", lambda h, li=li, src=src: h.dma_start(out=self.h[:, li, :], in_=src), writes=["h%d" % li])

    def store_group(self):
        P = self.P
        for li, (kind, idx) in enumerate(self.grp):
            dst = self.O["y_p"][idx * 128:(idx + 1) * 128, :] if kind == "P" else self.O["y_s"][:, :]
            P.dma("sp", lambda h, li=li, dst=dst: h.dma_start(out=dst, in_=self.h[:, li, :]), reads=["h%d" % li],
                  is_output=True)

    def layer(self, L):
        if L == 0:
            self.s5_mixer()
        else:
            self.mla_mixer()
        if self.stage >= 2:
            self.mem_attn(L)
        if self.stage >= 3:
            self.mlp(L)

    def s5_mixer(self):
        from_s5(self)

    def mem_attn(self, L):
        from_mem(self, L)

    def mlp(self, L):
        from_mlp(self, L)

    def mla_mixer(self):
        from_mla(self)


def from_s5(K):
    P, nc, I, O = K.P, K.nc, K.I, K.O
    A = K.arena
    off = [0]

    def carve(ncols, dt=F32, shape=None):
        a = off[0]
        off[0] += ncols
        v = A[:, a:a + ncols]
        if dt == BF16:
            v = v.bitcast(BF16)
        if shape is not None:
            names = "abc"[:len(shape)]
            kw = {names[i]: shape[i] for i in range(1, len(shape))}
            v = v.rearrange("p (%s) -> p %s" % (" ".join(names), " ".join(names)), **kw)
        return v

    BW = [carve(2048, BF16, [4, 8, 128]) for _ in range(2)]
    CW = [carve(2048, BF16, [4, 8, 128]) for _ in range(2)]
    sc = carve(32 * 16, F32, [16, 32])
    rows = carve(128 * 4, F32, [4, 128])
    msk = carve(8, F32)
    mski = carve(2, F32)
    dcol = carve(8)
    rmask = carve(128)
    rho_s = carve(128)
    ah0 = [carve(512, F32, [32, 16]) for _ in range(2)]
    fs = [carve(512, F32, [32, 16]) for _ in range(2)]
    fin = carve(64, F32, [32, 2])
    zt = carve(1024)
    r1 = off[0]
    Xre = carve(1024, F32, [32, 32])
    Xim = carve(1024, F32, [32, 32])
    T1 = carve(1024, F32, [32, 32])
    T2 = carve(1024, F32, [32, 32])
    raw = carve(1024, F32, [8, 128])
    Cl = carve(512, F32, [8, 64])
    endA = off[0]
    off[0] = r1
    h0stage = carve(4096)
    h0 = [carve(512, F32, [32, 16]) for _ in range(2)]
    endB = off[0]
    off[0] = r1
    CH = 256
    tabS = [carve(CH) for _ in range(2)]
    tabC = [carve(CH) for _ in range(2)]
    xi = [carve(CH).bitcast(I32) for _ in range(2)]
    rr = [carve(CH) for _ in range(2)]
    t1 = carve(CH)
    t2 = carve(CH)
    wre = [carve(CH) for _ in range(2)]
    wim = [carve(CH) for _ in range(2)]
    gre = [carve(CH) for _ in range(2)]
    gim = [carve(CH) for _ in range(2)]
    u1 = carve(CH)
    u2 = carve(CH)
    hre = [carve(CH // 2, BF16) for _ in range(2)]
    him = [carve(CH // 2, BF16) for _ in range(2)]
    yv = carve(MAXT * 128)
    v1 = carve(CH)
    v2 = carve(CH)
    endC = off[0]
    carry = K.s5_carry
    assert max(endA, endB, endC) <= ARENA, (endA, endB, endC)

    SC_AR, SC_AI, SC_LDT, SC_DT, SC_RHO, SC_FR, SC_ABR, SC_ABI, SC_CRE, SC_CIM, SC_T0, SC_T1, SC_T2, SC_T3 = range(14)

    def s(k):
        return sc[:, k, :]

    P.dma("sp", lambda h: h.dma_start(out=rows[0:32, 0, :], in_=I["ssm_a_re"][0].rearrange("(j g) n -> j (g n)", g=2)), writes=["rows"])
    P.dma("sp", lambda h: h.dma_start(out=rows[0:32, 1, :], in_=I["ssm_a_im"][0].rearrange("(j g) n -> j (g n)", g=2)), writes=["rows"])
    P.dma("sp", lambda h: h.dma_start(out=rows[0:32, 3, 0:2], in_=I["ssm_log_dt"][0].rearrange("(j g) -> j g", g=2)), writes=["rows"])
    P.op("dve", lambda h: h.tensor_copy(out=rows[0:32, 2, :].rearrange("p (g n) -> p g n", g=2),
                                        in_=rows[0:32, 3, 0:2].unsqueeze(2).to_broadcast([32, 2, 64])),
         reads=["rows"], writes=["rows"])
    for k in range(3):
        P.op("pe", lambda h, k=k: h.transpose(out=K.ps[0][:, k * 32:(k + 1) * 32], in_=rows[0:32, k, :], identity=K.identf[0:32, 0:32]),
             reads=["rows", "identf"], writes=["ps0"])
    P.op("dve", lambda h: h.tensor_copy(out=sc[:, 0:3, :], in_=K.ps[0][:, 0:96].rearrange("p (a b) -> p a b", b=32)),
         reads=["ps0"], writes=["sc"])

    def ew(eng, fn):
        P.op(eng, fn, reads=["sc"], writes=["sc"])
    ew("act", lambda h: h.activation(out=s(SC_DT), in_=s(SC_LDT), func=AF.Exp))
    ew("dve", lambda h: h.tensor_tensor(out=s(SC_T0), in0=s(SC_AR), in1=s(SC_DT), op=ALU.mult))
    ew("act", lambda h: h.activation(out=s(SC_RHO), in_=s(SC_T0), func=AF.Exp))
    ew("dve", lambda h: h.scalar_tensor_tensor(out=s(SC_T1), in0=s(SC_AI), scalar=1.0 / TWO_PI, in1=s(SC_DT), op0=ALU.mult, op1=ALU.mult))
    ew("dve", lambda h: h.tensor_copy(out=s(SC_T2).bitcast(I32), in_=s(SC_T1)))
    ew("dve", lambda h: h.tensor_tensor(out=s(SC_FR), in0=s(SC_T1), in1=s(SC_T2).bitcast(I32), op=ALU.subtract))
    ew("act", lambda h: h.activation(out=s(SC_T0), in_=s(SC_FR), func=AF.Sin, scale=TWO_PI))
    ew("act", lambda h: h.activation(out=s(SC_T1), in_=s(SC_FR), func=AF.Abs))
    ew("act", lambda h: h.activation(out=s(SC_T1), in_=s(SC_T1), func=AF.Sin, scale=-TWO_PI, bias=K.halfpi[:]))
    ew("dve", lambda h: h.tensor_tensor(out=s(SC_ABI), in0=s(SC_RHO), in1=s(SC_T0), op=ALU.mult))
    ew("dve", lambda h: h.tensor_tensor(out=s(SC_ABR), in0=s(SC_RHO), in1=s(SC_T1), op=ALU.mult))
    ew("dve", lambda h: h.tensor_tensor(out=s(SC_T0), in0=s(SC_AR), in1=s(SC_AR), op=ALU.mult))
    ew("dve", lambda h: h.tensor_tensor(out=s(SC_T1), in0=s(SC_AI), in1=s(SC_AI), op=ALU.mult))
    ew("dve", lambda h: h.tensor_tensor(out=s(SC_T0), in0=s(SC_T0), in1=s(SC_T1), op=ALU.add))
    ew("dve", lambda h: h.reciprocal(out=s(SC_T0), in_=s(SC_T0)))
    ew("dve", lambda h: h.tensor_scalar(out=s(SC_T1), in0=s(SC_ABR), scalar1=-1.0, scalar2=None, op0=ALU.add))
    ew("dve", lambda h: h.tensor_tensor(out=s(SC_T2), in0=s(SC_T1), in1=s(SC_AR), op=ALU.mult))
    ew("dve", lambda h: h.tensor_tensor(out=s(SC_T3), in0=s(SC_ABI), in1=s(SC_AI), op=ALU.mult))
    ew("dve", lambda h: h.tensor_tensor(out=s(SC_T2), in0=s(SC_T2), in1=s(SC_T3), op=ALU.add))
    ew("dve", lambda h: h.tensor_tensor(out=s(SC_CRE), in0=s(SC_T2), in1=s(SC_T0), op=ALU.mult))
    ew("dve", lambda h: h.tensor_tensor(out=s(SC_T2), in0=s(SC_ABI), in1=s(SC_AR), op=ALU.mult))
    ew("dve", lambda h: h.tensor_tensor(out=s(SC_T3), in0=s(SC_T1), in1=s(SC_AI), op=ALU.mult))
    ew("dve", lambda h: h.tensor_tensor(out=s(SC_T2), in0=s(SC_T2), in1=s(SC_T3), op=ALU.subtract))
    ew("dve", lambda h: h.tensor_tensor(out=s(SC_CIM), in0=s(SC_T2), in1=s(SC_T0), op=ALU.mult))

    P.op("pool", lambda h: h.memset(Xre, 0.0), writes=["Xre"])
    P.op("pool", lambda h: h.memset(Xim, 0.0), writes=["Xim"])
    for g2 in range(2):
        for nm, X, xk in (("ssm_b_re", Xre, "Xre"), ("ssm_b_im", Xim, "Xim")):
            src = I[nm][0].rearrange("(j g) n k -> g n j k", g=2)[g2]
            P.dma("sp", lambda h, X=X, src=src, g2=g2: h.dma_start(out=X[g2 * 64:(g2 + 1) * 64, :, g2 * 16:(g2 + 1) * 16], in_=src),
                  reads=[xk], writes=[xk])
    cre_b = s(SC_CRE).unsqueeze(2).to_broadcast([128, 32, 32])
    cim_b = s(SC_CIM).unsqueeze(2).to_broadcast([128, 32, 32])
    P.op("pool", lambda h: h.memset(msk, 0.0), writes=["msk"])
    for jj in range(4):
        P.op("pool", lambda h, jj=jj: h.memset(msk[32 * jj:32 * jj + 32, jj:jj + 1], 1.0), reads=["msk"], writes=["msk"])
    for ri in range(2):
        if ri == 0:
            P.op("dve", lambda h: h.tensor_tensor(out=T1, in0=Xre, in1=cre_b, op=ALU.mult), reads=["Xre", "sc"], writes=["T1"])
            P.op("dve", lambda h: h.tensor_tensor(out=T2, in0=Xim, in1=cim_b, op=ALU.mult), reads=["Xim", "sc"], writes=["T2"])
            P.op("dve", lambda h: h.tensor_tensor(out=T1, in0=T1, in1=T2, op=ALU.subtract), reads=["T1", "T2"], writes=["T1"])
        else:
            P.op("dve", lambda h: h.tensor_tensor(out=T1, in0=Xim, in1=cre_b, op=ALU.mult), reads=["Xim", "sc"], writes=["T1"])
            P.op("dve", lambda h: h.tensor_tensor(out=T2, in0=Xre, in1=cim_b, op=ALU.mult), reads=["Xre", "sc"], writes=["T2"])
            P.op("dve", lambda h: h.tensor_tensor(out=T1, in0=T1, in1=T2, op=ALU.add), reads=["T1", "T2"], writes=["T1"])
        for half in range(2):
            bk = "ps%d" % half
            for qq in range(4):
                q = half * 4 + qq
                P.op("pe", lambda h, q=q, qq=qq, half=half: h.transpose(
                    out=K.ps[half][:, qq * 128:(qq + 1) * 128],
                    in_=T1[:, 4 * q:4 * q + 4, :].rearrange("p a b -> p (a b)"), identity=K.identf[:]),
                    reads=["T1", "identf"], writes=[bk])
            P.op("act", lambda h, half=half: h.copy(out=raw[:, half * 4:half * 4 + 4, :],
                                                     in_=K.ps[half][:, :].rearrange("p (a b) -> p a b", b=128)),
                 reads=[bk], writes=["raw"])
        for jj in range(4):
            P.op("dve", lambda h, jj=jj, ri=ri: h.tensor_scalar(out=BW[ri][:, jj, :, :], in0=raw, scalar1=msk[:, jj:jj + 1],
                                                                 scalar2=None, op0=ALU.mult),
                 reads=["raw", "msk"], writes=["BW%d" % ri])

    m1 = msk[:, 4:5]
    m0 = msk[:, 5:6]
    P.op("pool", lambda h: h.iota(mski.bitcast(I32)[:, 0:1], pattern=[[0, 1]], base=0, channel_multiplier=1), writes=["mski"])
    P.op("dve", lambda h: h.tensor_single_scalar(out=mski.bitcast(I32)[:, 1:2], in_=mski.bitcast(I32)[:, 0:1], scalar=16, op=ALU.bitwise_and),
         reads=["mski"], writes=["mski"])
    P.op("dve", lambda h: h.tensor_copy(out=m1, in_=mski.bitcast(I32)[:, 1:2]), reads=["mski", "msk"], writes=["msk"])
    P.op("dve", lambda h: h.tensor_scalar(out=m1, in0=m1, scalar1=1.0 / 16, scalar2=None, op0=ALU.mult), reads=["msk"], writes=["msk"])
    P.op("dve", lambda h: h.tensor_scalar(out=m0, in0=m1, scalar1=-1.0, scalar2=1.0, op0=ALU.mult, op1=ALU.add), reads=["msk"], writes=["msk"])
    XC = T1.rearrange("p a b -> p (a b)").rearrange("p (q c) -> p q c", c=128)
    for ri, nm in enumerate(("ssm_c_re", "ssm_c_im")):
        src = I[nm][0].rearrange("g k n -> (g k) n").rearrange("(q p) n -> p q n", p=128)
        P.dma("sp", lambda h, src=src: h.dma_start(out=Cl, in_=src), writes=["Cl"])
        sgn = 1.0 if ri == 0 else -1.0
        P.op("dve", lambda h, sgn=sgn: h.tensor_scalar(out=XC[:, :, 0:64], in0=Cl, scalar1=m0, scalar2=sgn, op0=ALU.mult, op1=ALU.mult),
             reads=["Cl", "msk"], writes=["T1"])
        P.op("dve", lambda h, sgn=sgn: h.tensor_scalar(out=XC[:, :, 64:128], in0=Cl, scalar1=m1, scalar2=sgn, op0=ALU.mult, op1=ALU.mult),
             reads=["Cl", "msk"], writes=["T1"])
        for half in range(2):
            bk = "ps%d" % half
            for qq in range(4):
                q = half * 4 + qq
                P.op("pe", lambda h, q=q, qq=qq, half=half: h.transpose(
                    out=K.ps[half][:, qq * 128:(qq + 1) * 128], in_=XC[:, q, :], identity=K.identf[:]),
                    reads=["T1", "identf"], writes=[bk])
            P.op("act", lambda h, half=half: h.copy(out=raw[:, half * 4:half * 4 + 4, :],
                                                     in_=K.ps[half][:, :].rearrange("p (a b) -> p a b", b=128)),
                 reads=[bk], writes=["raw"])
        P.op("pool", lambda h, ri=ri: h.memset(CW[ri].rearrange("p a b c -> p (a b c)"), 0.0), writes=["CW%d" % ri])
        for jj in range(4):
            P.op("dve", lambda h, jj=jj, ri=ri: h.tensor_copy(out=CW[ri][:, jj, :, 32 * jj:32 * jj + 32], in_=raw[:, :, 32 * jj:32 * jj + 32]),
                 reads=["raw", "CW%d" % ri], writes=["CW%d" % ri])

    P.dma("sp", lambda h: h.dma_start(out=rows[0:8, 0, :], in_=I["ssm_d"][0].rearrange("(q p) -> q p", p=128)), reads=["rows"], writes=["rows"])
    P.op("pe", lambda h: h.transpose(out=K.ps[2][:, 0:8], in_=rows[0:8, 0, :], identity=K.identf[0:8, 0:8]), reads=["rows", "identf"], writes=["ps2"])
    P.op("dve", lambda h: h.tensor_copy(out=dcol, in_=K.ps[2][:, 0:8]), reads=["ps2"], writes=["dcol"])

    P.barrier()
    g_pre, gk = K.load_gain(I["norm_mix_pre"][0])
    for li in range(K.ntile):
        K.norm_to_xnT(li, g_pre, gk, li * 128)

    wglu = [K.load_w(I["w_glu"][0][:, hf * 1024:(hf + 1) * 1024], 1024) for hf in range(2)]

    ptiles = [idx for (kind, idx) in K.grp if kind == "P"]
    has_s = any(kind == "S" for (kind, _) in K.grp)
    tp = 128 * len(ptiles)
    chunks = []
    c0 = 0
    while c0 < tp:
        n = min(256, tp - c0)
        chunks.append(("P", c0, n, ptiles[0] * 128 + c0))
        c0 += n
    if has_s:
        chunks.append(("S", tp, 128, 0))
        P.op("pool", lambda h: h.memset(rmask, 1.0), writes=["rmask"])
        P.op("pool", lambda h: h.memset(rmask.rearrange("p (b t) -> p b t", t=8)[:, :, 0:1], 0.0), reads=["rmask"], writes=["rmask"])
        for ri, nm in enumerate(("ssr", "ssi")):
            P.dma("sp", lambda h, nm=nm: h.dma_start(out=h0stage[0:16, :], in_=I[nm][:, :]), writes=["h0stage"])
            for j in range(32):
                P.op("pe", lambda h, j=j: h.transpose(out=K.ps[3][:, j * 16:(j + 1) * 16], in_=h0stage[0:16, j * 128:(j + 1) * 128],
                                                       identity=K.identf[0:16, 0:16]), reads=["h0stage", "identf"], writes=["ps3"])
            P.op("act", lambda h, ri=ri: h.copy(out=h0[ri], in_=K.ps[3][:, :].rearrange("p (j b) -> p j b", b=16)), reads=["ps3"], writes=["h0_%d" % ri])
        abr_b = s(SC_ABR).unsqueeze(2).to_broadcast([128, 32, 16])
        abi_b = s(SC_ABI).unsqueeze(2).to_broadcast([128, 32, 16])
        Ta = zt[:, 0:512].rearrange("p (j b) -> p j b", b=16)
        Tb = zt[:, 512:1024].rearrange("p (j b) -> p j b", b=16)
        P.op("dve", lambda h: h.tensor_tensor(out=Ta, in0=h0[0], in1=abr_b, op=ALU.mult), reads=["h0_0", "sc"], writes=["zt"])
        P.op("dve", lambda h: h.tensor_tensor(out=Tb, in0=h0[1], in1=abi_b, op=ALU.mult), reads=["h0_1", "sc"], writes=["zt"])
        P.op("dve", lambda h: h.tensor_tensor(out=ah0[0], in0=Ta, in1=Tb, op=ALU.subtract), reads=["zt"], writes=["ah0_0"])
        P.op("dve", lambda h: h.tensor_tensor(out=Ta, in0=h0[1], in1=abr_b, op=ALU.mult), reads=["h0_1", "sc"], writes=["zt"])
        P.op("dve", lambda h: h.tensor_tensor(out=Tb, in0=h0[0], in1=abi_b, op=ALU.mult), reads=["h0_0", "sc"], writes=["zt"])
        P.op("dve", lambda h: h.tensor_tensor(out=ah0[1], in0=Ta, in1=Tb, op=ALU.add), reads=["zt"], writes=["ah0_1"])
        P.barrier()
    if K.gi == 0:
        P.op("pool", lambda h: h.memset(carry.rearrange("p a b -> p (a b)"), 0.0), writes=["carry%d" % j_ for j_ in range(32)])

    units = []
    for q in range(8):
        for ci, (kind, c0, n, tglob) in enumerate(chunks):
            for jj in range(4):
                units.append(dict(q=q, ci=ci, kind=kind, c0=c0, n=n, tglob=tglob, jj=jj, u=len(units), last_chunk=(ci == len(chunks) - 1)))

    def stage1(U):
        q, kind, c0, n, jj, u = U["q"], U["kind"], U["c0"], U["n"], U["jj"], U["u"]
        j = 4 * q + jj
        bi = u % 2
        cols = slice(c0, c0 + n)
        bre, bim = K.ps[bi * 2], K.ps[bi * 2 + 1]
        brk, bik = "ps%d" % (bi * 2), "ps%d" % (bi * 2 + 1)
        ukey = "xnT.%d" % q
        P.op("pe", lambda h: h.matmul(bre[:, 0:n], lhsT=BW[0][:, jj, q, :], rhs=K.xnT[:, q, cols], start=True, stop=True), reads=["BW0", ukey], writes=[brk])
        P.op("pe", lambda h: h.matmul(bim[:, 0:n], lhsT=BW[1][:, jj, q, :], rhs=K.xnT[:, q, cols], start=True, stop=True), reads=["BW1", ukey], writes=[bik])
        tS, tC, xI, rR = tabS[bi], tabC[bi], xi[bi], rr[bi]
        tk = "tab%d" % bi
        fr = sc[:, SC_FR, j:j + 1]
        io = K.iota_g[:, cols]
        P.op("dve", lambda h: h.tensor_scalar(out=xI[:, 0:n], in0=io, scalar1=fr, scalar2=None, op0=ALU.mult), reads=["iota_g", "sc"], writes=[tk + "x"])
        P.op("dve", lambda h: h.scalar_tensor_tensor(out=rR[:, 0:n], in0=io, scalar=fr, in1=xI[:, 0:n], op0=ALU.mult, op1=ALU.subtract), reads=["iota_g", "sc", tk + "x"], writes=[tk + "r"])
        P.op("act", lambda h: h.activation(out=tS[:, 0:n], in_=rR[:, 0:n], func=AF.Sin, scale=TWO_PI), reads=[tk + "r"], writes=[tk + "S"])
        P.op("dve", lambda h: h.scalar_tensor_tensor(out=tC[:, 0:n], in0=rR[:, 0:n], scalar=-1.0, in1=rR[:, 0:n], op0=ALU.mult, op1=ALU.max), reads=[tk + "r"], writes=[tk + "C"])
        P.op("act", lambda h: h.activation(out=tC[:, 0:n], in_=tC[:, 0:n], func=AF.Sin, scale=-TWO_PI, bias=K.halfpi[:]), reads=[tk + "C"], writes=[tk + "C"])
        if kind == "S":
            for ri, bb, bk_ in ((0, bre, brk), (1, bim, bik)):
                v = bb[:, 0:128].rearrange("p (b t) -> p b t", t=8)[:, :, 0]
                P.op("dve", lambda h, v=v, ri=ri: h.tensor_tensor(out=v, in0=v, in1=ah0[ri][:, j, :], op=ALU.add), reads=[bk_, "ah0_%d" % ri], writes=[bk_])

    def stage2(U):
        q, kind, c0, n, jj, u, tglob = U["q"], U["kind"], U["c0"], U["n"], U["jj"], U["u"], U["tglob"]
        j = 4 * q + jj
        bi = u % 2
        cols = slice(c0, c0 + n)
        bre, bim = K.ps[bi * 2], K.ps[bi * 2 + 1]
        brk, bik = "ps%d" % (bi * 2), "ps%d" % (bi * 2 + 1)
        tS, tC = tabS[bi], tabC[bi]
        tk = "tab%d" % bi
        ybank = 4 + ((u // 4) % 2)
        ybk = "ps%d" % ybank
        yps = K.ps[ybank]
        wr, wi = wre[bi], wim[bi]
        wk = "w%d" % bi
        P.op("dve", lambda h: h.tensor_tensor(out=t1[:, 0:n], in0=bre[:, 0:n], in1=tC[:, 0:n], op=ALU.mult), reads=[brk, tk + "C"], writes=["t1"])
        P.op("dve", lambda h: h.tensor_tensor(out=t2[:, 0:n], in0=bim[:, 0:n], in1=tS[:, 0:n], op=ALU.mult), reads=[bik, tk + "S"], writes=["t2"])
        P.op("dve", lambda h: h.tensor_tensor(out=wr[:, 0:n], in0=t1[:, 0:n], in1=t2[:, 0:n], op=ALU.add), reads=["t1", "t2"], writes=[wk + "r"])
        P.op("dve", lambda h: h.tensor_tensor(out=t1[:, 0:n], in0=bim[:, 0:n], in1=tC[:, 0:n], op=ALU.mult), reads=[bik, tk + "C"], writes=["t1"])
        P.op("dve", lambda h: h.tensor_tensor(out=t2[:, 0:n], in0=bre[:, 0:n], in1=tS[:, 0:n], op=ALU.mult), reads=[brk, tk + "S"], writes=["t2"])
        P.op("dve", lambda h: h.tensor_tensor(out=wi[:, 0:n], in0=t1[:, 0:n], in1=t2[:, 0:n], op=ALU.subtract), reads=["t1", "t2"], writes=[wk + "i"])
        gr, gi_ = gre[bi], gim[bi]
        gk_ = "g%d" % bi
        if kind == "P":
            d0 = sc[:, SC_RHO, j:j + 1].to_broadcast([128, n])
            d0k = "sc"
            ini = [carry[:, j, 0:1], carry[:, j, 1:2]]
            inik = ["carry%d" % j]
        else:
            P.op("pool", lambda h: h.tensor_scalar(out=rho_s, in0=rmask, scalar1=sc[:, SC_RHO, j:j + 1], scalar2=None, op0=ALU.mult), reads=["rmask", "sc"], writes=["rho_s"])
            d0 = rho_s[:, 0:n]
            d0k = "rho_s"
            ini = [0.0, 0.0]
            inik = []
        P.op("dve", lambda h: h.tensor_tensor_scan(out=gr[:, 0:n], data0=d0, data1=wr[:, 0:n], initial=ini[0], op0=ALU.mult, op1=ALU.add), reads=[wk + "r", d0k] + inik, writes=[gk_ + "r"])
        P.op("dve", lambda h: h.tensor_tensor_scan(out=gi_[:, 0:n], data0=d0, data1=wi[:, 0:n], initial=ini[1], op0=ALU.mult, op1=ALU.add), reads=[wk + "i", d0k] + inik, writes=[gk_ + "i"])
        if kind == "P":
            P.op("pool", lambda h: h.tensor_copy(out=carry[:, j, 0:1], in_=gr[:, n - 1:n]), reads=[gk_ + "r"], writes=["carry%d" % j])
            P.op("pool", lambda h: h.tensor_copy(out=carry[:, j, 1:2], in_=gi_[:, n - 1:n]), reads=[gk_ + "i", "carry%d" % j], writes=["carry%d" % j])
            if tglob + n == SEQ:
                P.op("pool", lambda h: h.tensor_copy(out=fin[:, j, 0:1], in_=tC[:, n - 1:n]), reads=[tk + "C"], writes=["fin"])
                P.op("pool", lambda h: h.tensor_copy(out=fin[:, j, 1:2], in_=tS[:, n - 1:n]), reads=[tk + "S", "fin"], writes=["fin"])
        else:
            g7r = gr[:, 0:128].rearrange("p (b t) -> p b t", t=8)[:, :, 7]
            g7i = gi_[:, 0:128].rearrange("p (b t) -> p b t", t=8)[:, :, 7]
            c7 = tC[:, 7:8]
            s7 = tS[:, 7:8]
            o_r = fs[0][:, j, :]
            o_i = fs[1][:, j, :]
            P.op("pool", lambda h: h.tensor_scalar(out=u1[:, 0:16], in0=g7r, scalar1=c7, scalar2=None, op0=ALU.mult), reads=[gk_ + "r", tk + "C"], writes=["u1"])
            P.op("pool", lambda h: h.tensor_scalar(out=u2[:, 0:16], in0=g7i, scalar1=s7, scalar2=None, op0=ALU.mult), reads=[gk_ + "i", tk + "S"], writes=["u2"])
            P.op("pool", lambda h: h.tensor_tensor(out=o_r, in0=u1[:, 0:16], in1=u2[:, 0:16], op=ALU.subtract), reads=["u1", "u2"], writes=["fs0"])
            P.op("pool", lambda h: h.tensor_scalar(out=u1[:, 0:16], in0=g7i, scalar1=c7, scalar2=None, op0=ALU.mult), reads=[gk_ + "i", tk + "C"], writes=["u1"])
            P.op("pool", lambda h: h.tensor_scalar(out=u2[:, 0:16], in0=g7r, scalar1=s7, scalar2=None, op0=ALU.mult), reads=[gk_ + "r", tk + "S"], writes=["u2"])
            P.op("pool", lambda h: h.tensor_tensor(out=o_i, in0=u1[:, 0:16], in1=u2[:, 0:16], op=ALU.add), reads=["u1", "u2"], writes=["fs1"])
        hr, hi = hre[bi], him[bi]
        hk_ = "hh%d" % bi
        P.op("pool", lambda h: h.tensor_tensor(out=u1[:, 0:n], in0=gr[:, 0:n], in1=tC[:, 0:n], op=ALU.mult), reads=[gk_ + "r", tk + "C"], writes=["u1"])
        P.op("pool", lambda h: h.tensor_tensor(out=u2[:, 0:n], in0=gi_[:, 0:n], in1=tS[:, 0:n], op=ALU.mult), reads=[gk_ + "i", tk + "S"], writes=["u2"])
        P.op("pool", lambda h: h.tensor_tensor(out=hr[:, 0:n], in0=u1[:, 0:n], in1=u2[:, 0:n], op=ALU.subtract), reads=["u1", "u2"], writes=[hk_ + "r"])
        P.op("dve", lambda h: h.tensor_tensor(out=v1[:, 0:n], in0=gi_[:, 0:n], in1=tC[:, 0:n], op=ALU.mult), reads=[gk_ + "i", tk + "C"], writes=["v1"])
        P.op("dve", lambda h: h.tensor_tensor(out=v2[:, 0:n], in0=gr[:, 0:n], in1=tS[:, 0:n], op=ALU.mult), reads=[gk_ + "r", tk + "S"], writes=["v2"])
        P.op("pool", lambda h: h.tensor_tensor(out=hi[:, 0:n], in0=v1[:, 0:n], in1=v2[:, 0:n], op=ALU.add), reads=["v1", "v2"], writes=[hk_ + "i"])
        P.op("pe", lambda h: h.matmul(yps[:, 0:n], lhsT=CW[0][:, jj, q, :], rhs=hr[:, 0:n], start=(jj == 0), stop=False), reads=["CW0", hk_ + "r"], writes=[ybk])
        P.op("pe", lambda h: h.matmul(yps[:, 0:n], lhsT=CW[1][:, jj, q, :], rhs=hi[:, 0:n], start=False, stop=(jj == 3)), reads=["CW1", hk_ + "i"], writes=[ybk])
        if jj == 3:
            P.op("dve", lambda h: h.scalar_tensor_tensor(out=yv[:, cols], in0=K.xnT[:, q, cols], scalar=dcol[:, q:q + 1], in1=yps[:, 0:n], op0=ALU.mult, op1=ALU.add),
                 reads=[ybk, "xnT.%d" % q, "dcol"], writes=["yv"])
            if U["last_chunk"]:
                tok = K.tok
                P.op("act", lambda h: h.activation(out=K.xnT[:, q, 0:tok], in_=yv[:, 0:tok], func=AF.Gelu), reads=["yv"], writes=["xnT.%d" % q])

    stage1(units[0])
    for i, U in enumerate(units):
        if i + 1 < len(units):
            stage1(units[i + 1])
        stage2(U)

    P.barrier()
    if any(kind == "P" and idx == NPT - 1 for (kind, idx) in K.grp):
        cc, ss = fin[:, :, 0], fin[:, :, 1]
        gr_, gi2 = carry[:, :, 0], carry[:, :, 1]
        Fa = zt[:, 0:32]
        Fb = zt[:, 32:64]
        Fc = zt[:, 64:96]
        Fd = zt[:, 96:128]
        P.op("dve", lambda h: h.tensor_tensor(out=Fa, in0=cc, in1=gr_, op=ALU.mult), reads=["fin"] + ["carry%d" % j_ for j_ in range(32)], writes=["zt"])
        P.op("dve", lambda h: h.tensor_tensor(out=Fb, in0=ss, in1=gi2, op=ALU.mult), reads=["fin"] + ["carry%d" % j_ for j_ in range(32)], writes=["zt"])
        P.op("dve", lambda h: h.tensor_tensor(out=Fc, in0=Fa, in1=Fb, op=ALU.subtract), reads=["zt"], writes=["zt"])
        P.op("dve", lambda h: h.tensor_tensor(out=Fa, in0=cc, in1=gi2, op=ALU.mult), reads=["fin"] + ["carry%d" % j_ for j_ in range(32)], writes=["zt"])
        P.op("dve", lambda h: h.tensor_tensor(out=Fb, in0=ss, in1=gr_, op=ALU.mult), reads=["fin"] + ["carry%d" % j_ for j_ in range(32)], writes=["zt"])
        P.op("dve", lambda h: h.tensor_tensor(out=Fd, in0=Fa, in1=Fb, op=ALU.add), reads=["zt"], writes=["zt"])
        for ri, (T_, oname) in enumerate(((Fc, "ssm_re_p"), (Fd, "ssm_im_p"))):
            P.op("pe", lambda h, T_=T_: h.transpose(out=K.ps[6][0:32, 0:128], in_=T_, identity=K.identf[:]), reads=["zt", "identf"], writes=["ps6"])
            P.op("act", lambda h: h.copy(out=rows[0:32, 0, :], in_=K.ps[6][0:32, 0:128]), reads=["ps6"], writes=["rows"])
            P.dma("sp", lambda h, oname=oname: h.dma_start(out=O[oname].rearrange("o (j c) -> (o j) c", c=128), in_=rows[0:32, 0, :]), reads=["rows"], is_output=True)
    if has_s:
        for ri, oname in enumerate(("ssm_re_s", "ssm_im_s")):
            for j in range(32):
                bank = 6 + (j // 16) % 2
                P.op("pe", lambda h, j=j, ri=ri, bank=bank: h.transpose(out=K.ps[bank][0:16, (j % 4) * 128:(j % 4 + 1) * 128], in_=fs[ri][:, j, :], identity=K.identf[:]),
                     reads=["fs%d" % ri, "identf"], writes=["ps%d" % bank])
                if j % 4 == 3:
                    P.op("act", lambda h, j=j, bank=bank: h.copy(out=h0stage[0:16, (j - 3) * 128:(j + 1) * 128], in_=K.ps[bank][0:16, :]),
                         reads=["ps%d" % bank], writes=["h0stage"])
            P.dma("sp", lambda h, oname=oname: h.dma_start(out=O[oname][:, :], in_=h0stage[0:16, :]), reads=["h0stage"], is_output=True)

    g_post, gpk = K.load_gain(I["norm_mix_post"][0])
    for li in range(K.ntile):
        tc = slice(li * 128, (li + 1) * 128)
        for hf in range(2):
            wv, wk_ = wglu[hf]
            for nb in range(2):
                bank = hf * 2 + nb
                for k in range(8):
                    P.op("pe", lambda h, bank=bank, k=k, tc=tc, wv=wv, nb=nb: h.matmul(K.ps[bank][:, :], lhsT=K.xnT[:, k, tc], rhs=wv[:, k, nb * 512:(nb + 1) * 512], start=(k == 0), stop=(k == 7)),
                         reads=["xnT.%d" % k, wk_], writes=["ps%d" % bank])
        for nb in range(2):
            P.op("act", lambda h, nb=nb: h.activation(out=zt[:, nb * 512:(nb + 1) * 512], in_=K.ps[2 + nb][:, :], func=AF.Sigmoid), reads=["ps%d" % (2 + nb)], writes=["zt%d" % nb])
            P.op("dve", lambda h, nb=nb: h.tensor_tensor(out=zt[:, nb * 512:(nb + 1) * 512], in0=zt[:, nb * 512:(nb + 1) * 512], in1=K.ps[nb][:, :], op=ALU.mult),
                 reads=["zt%d" % nb, "ps%d" % nb], writes=["zt%d" % nb])
        K.post_norm_add(li, zt, ["zt0", "zt1"], g_post, gpk)
    P.barrier()


class Carver:
    def __init__(self, arena):
        self.A = arena
        self.off = 0
        self.hi = 0

    def __call__(self, ncols, dt=F32, shape=None):
        a = self.off
        self.off += ncols
        self.hi = max(self.hi, self.off)
        assert self.hi <= ARENA, self.hi
        v = self.A[:, a:a + ncols]
        if dt != F32:
            v = v.bitcast(dt)
        if shape is not None:
            names = "abc"[:len(shape)]
            kw = {names[i]: shape[i] for i in range(1, len(shape))}
            v = v.rearrange("p (%s) -> p %s" % (" ".join(names), " ".join(names)), **kw)
        return v


def evac(P, eng, out, in_, reads, writes):
    if eng == "act":
        P.op("act", lambda h: h.copy(out=out, in_=in_), reads=reads, writes=writes)
    else:
        P.op(eng, lambda h: h.tensor_copy(out=out, in_=in_), reads=reads, writes=writes)


def softmax_pt(K, S, P, spair, tpair, scale, p, Dm, pT, sm, tag):
    sp = K.pp[spair]
    tp_ = K.pp[tpair]
    sk = ["ps%d" % (2 * spair), "ps%d" % (2 * spair + 1)]
    tk = ["ps%d" % (2 * tpair), "ps%d" % (2 * tpair + 1)]
    mx, nb, l, rl = sm[:, 0:4], sm[:, 4:8], sm[:, 8:12], sm[:, 12:16]
    P.op("dve", lambda h: h.tensor_reduce(out=mx, in_=sp[:, :].rearrange("p (a b) -> p a b", b=256), axis=AX.X, op=ALU.max),
         reads=sk, writes=[tag + "sm"])
    P.op("dve", lambda h: h.tensor_scalar(out=nb, in0=mx, scalar1=-scale, scalar2=None, op0=ALU.mult), reads=[tag + "sm"], writes=[tag + "sm"])
    for hh in range(4):
        P.op("act", lambda h, hh=hh: h.activation(out=p[:, hh, :], in_=sp[:, hh * 256:(hh + 1) * 256], func=AF.Exp, scale=scale,
                                                   bias=nb[:, hh:hh + 1], accum_out=l[:, hh:hh + 1]),
             reads=sk + [tag + "sm"], writes=[tag + "p", tag + "l"])
    P.op("dve", lambda h: h.reciprocal(out=rl, in_=l), reads=[tag + "l", tag + "sm"], writes=[tag + "sm"])
    P.op("dve", lambda h: h.tensor_tensor(out=Dm, in0=K.ident[:].unsqueeze(1).to_broadcast([128, 4, 128]),
                                          in1=rl.unsqueeze(2).to_broadcast([128, 4, 128]), op=ALU.mult),
         reads=["ident", tag + "sm"], writes=[tag + "D"])
    for hh in range(4):
        for mt in range(2):
            c = hh * 2 + mt
            P.op("pe", lambda h, hh=hh, mt=mt, c=c: h.matmul(tp_[:, c * 128:(c + 1) * 128], lhsT=p[:, hh, mt * 128:(mt + 1) * 128], rhs=Dm[:, hh, :], start=True, stop=True),
                 reads=[tag + "p", tag + "D"], writes=[tk[c // 4]])
    evac(P, "act", pT[:, 0:4, :], tp_[:, 0:512].rearrange("p (a b) -> p a b", b=128), [tk[0]], [tag + "pT0"])
    evac(P, "dve", pT[:, 4:8, :], tp_[:, 512:1024].rearrange("p (a b) -> p a b", b=128), [tk[1]], [tag + "pT1"])


def from_mem(K, L):
    P, I, O = K.P, K.I, K.O
    C = Carver(K.arena)
    TOK = K.tok
    qT = C(TOK * 4, BF16, [8, TOK])
    mnT = C(1024, BF16, [8, 256])
    mkT = C(1024, BF16, [8, 256])
    mv = C(1024, BF16, [2, 1024])
    mst = C(1024)
    p = [C(512, BF16, [4, 256]) for _ in range(2)]
    Dm = [C(256, BF16, [4, 128]) for _ in range(2)]
    pT = [C(512, BF16, [8, 128]) for _ in range(2)]
    oT = [C(512, BF16, [8, 128]) for _ in range(2)]
    sm = [C(16) for _ in range(2)]
    has_s = any(kind == "S" for (kind, _) in K.grp)
    if has_s:
        Kst = [C(2048, F32, [2, 1024]) for _ in range(2)]
        KTb = C(1024, BF16, [8, 256])
        Vb = C(1024, BF16, [2, 1024])
        sTs = C(1024, F32, [8, 128])

    wk, wkk = K.load_w(I["w_mem_k"][L], 1024)
    wv, wvk = K.load_w(I["w_mem_v"][L], 1024)
    wq, wqk = K.load_w(I["w_mem_q"][L], 1024)

    g_in, gik = K.load_gain(I["mem_in_norm"][L])
    for mt in range(2):
        P.dma("sp", lambda h, mt=mt: h.dma_start(out=mst, in_=I["memp"][mt * 128:(mt + 1) * 128, :]), writes=["mst"])
        rs, rk = K.rms_stats(mst, ["mst"], D)
        P.op("dve", lambda h, rs=rs: h.scalar_tensor_tensor(out=K.xn[:], in0=mst, scalar=rs, in1=g_in[:], op0=ALU.mult, op1=ALU.mult),
             reads=["mst", rk, gik], writes=["xn"])
        K.transpose_to(K.xn, "xn", mnT, "mnT", mt * 128)
    mnk = ["mnT.%d" % k for k in range(8)]
    for mt in range(2):
        for which, (w_, wkey, oname) in enumerate(((wk, wkk, "memk_p"), (wv, wvk, "memv_p"))):
            if which == 0 and K.gi != 0:
                continue
            pair = 2 + which
            for nb in range(2):
                for k in range(8):
                    P.op("pe", lambda h, pair=pair, nb=nb, k=k, mt=mt, w_=w_: h.matmul(K.pp[pair][:, nb * 512:(nb + 1) * 512], lhsT=mnT[:, k, mt * 128:(mt + 1) * 128], rhs=w_[:, k, nb * 512:(nb + 1) * 512], start=(k == 0), stop=(k == 7)),
                         reads=mnk + [wkey], writes=["ps%d" % (2 * pair + nb)])
            pk = ["ps%d" % (2 * pair), "ps%d" % (2 * pair + 1)]
            if which == 1:
                evac(P, "act", mv[:, mt, :], K.pp[pair][:, :], pk, ["mv"])
            if K.gi == 0:
                evac(P, "dve", mst, K.pp[pair][:, :], pk, ["mst"])
                P.dma("sp", lambda h, oname=oname, mt=mt: h.dma_start(out=O[oname][L, mt * 128:(mt + 1) * 128, :], in_=mst), reads=["mst"], is_output=True)
    for c8 in range(8):
        bank = c8 % 2
        for k in range(8):
            P.op("pe", lambda h, bank=bank, c8=c8, k=k: h.matmul(K.ps[bank][:, 0:256], lhsT=wk[:, k, c8 * 128:(c8 + 1) * 128], rhs=mnT[:, k, :], start=(k == 0), stop=(k == 7)),
                 reads=mnk + [wkk], writes=["ps%d" % bank])
        evac(P, "act" if c8 % 2 == 0 else "dve", mkT[:, c8, :], K.ps[bank][:, 0:256], ["ps%d" % bank], ["mkT"])

    g_pre, gpk = K.load_gain(I["norm_mem_pre"][L])
    for li in range(K.ntile):
        K.norm_to_xnT(li, g_pre, gpk, li * 128)
    wo, wok = K.load_w(I["w_mem_o"][L], 1024)
    xk = ["xnT.%d" % k for k in range(8)]
    nblk = 2
    bs = TOK // nblk
    u = 0
    for blk in range(nblk):
        cs = slice(blk * bs, (blk + 1) * bs)
        for c8 in range(8):
            bank = 4 + (u % 4)
            for k in range(8):
                P.op("pe", lambda h, bank=bank, c8=c8, k=k, cs=cs: h.matmul(K.ps[bank][:, 0:bs], lhsT=wq[:, k, c8 * 128:(c8 + 1) * 128], rhs=K.xnT[:, k, cs], start=(k == 0), stop=(k == 7)),
                     reads=xk + [wqk], writes=["ps%d" % bank])
            evac(P, "act" if u % 2 == 0 else "dve", qT[:, c8, cs], K.ps[bank][:, 0:bs], ["ps%d" % bank], ["qT"])
            u += 1
    g_post, gpok = K.load_gain(I["norm_mem_post"][L])

    def tail(li, bi, tpair_pT):
        pass

    for li, (kind, idx) in enumerate(K.grp):
        bi = li % 2
        tag = "m%d" % bi
        tcols = slice(li * 128, (li + 1) * 128)
        if kind == "P":
            for hh in range(4):
                for dc in range(2):
                    c8 = hh * 2 + dc
                    P.op("pe", lambda h, hh=hh, dc=dc, c8=c8, tcols=tcols: h.matmul(K.pp[0][:, hh * 256:(hh + 1) * 256], lhsT=qT[:, c8, tcols], rhs=mkT[:, c8, :], start=(dc == 0), stop=(dc == 1)),
                         reads=["qT", "mkT"], writes=["ps%d" % (hh // 2)])
            softmax_pt(K, None, P, 0, 1, MEM_SCALE, p[bi], Dm[bi], pT[bi], sm[bi], tag)
            for hh in range(4):
                for dvc in range(2):
                    c = hh * 2 + dvc
                    for mt in range(2):
                        P.op("pe", lambda h, hh=hh, dvc=dvc, c=c, mt=mt, bi=bi: h.matmul(K.pp[2][:, c * 128:(c + 1) * 128], lhsT=mv[:, mt, hh * 256 + dvc * 128:hh * 256 + (dvc + 1) * 128], rhs=pT[bi][:, hh * 2 + mt, :], start=(mt == 0), stop=(mt == 1)),
                             reads=["mv", tag + "pT0", tag + "pT1"], writes=["ps%d" % (4 + c // 4)])
        else:
            scol = li * 128
            for b in range(16):
                sb_ = b % 2
                P.dma("sp", lambda h, b=b, sb_=sb_: h.dma_start(out=Kst[sb_], in_=I["memk"][L, b].rearrange("(a p) n -> p a n", p=128)), writes=["Kst%d" % sb_])
                for mt in range(2):
                    for c8 in range(8):
                        t_ = mt * 8 + c8
                        P.op("pe", lambda h, sb_=sb_, mt=mt, c8=c8: h.transpose(out=K.pp[2 + c8 // 4][:, (c8 % 4) * 256 + mt * 128:(c8 % 4) * 256 + (mt + 1) * 128], in_=Kst[sb_][:, mt, c8 * 128:(c8 + 1) * 128], identity=K.identf[:]),
                             reads=["Kst%d" % sb_, "identf"], writes=["ps%d" % (4 + (c8 // 4) * 2 + (c8 % 4) // 2)])
                evac(P, "act", KTb[:, 0:4, :], K.pp[2][:, :].rearrange("p (a b) -> p a b", b=256), ["ps4", "ps5"], ["KTb0"])
                evac(P, "dve", KTb[:, 4:8, :], K.pp[3][:, :].rearrange("p (a b) -> p a b", b=256), ["ps6", "ps7"], ["KTb1"])
                for hh in range(4):
                    for mt in range(2):
                        c = hh * 2 + mt
                        for dc in range(2):
                            P.op("pe", lambda h, hh=hh, mt=mt, c=c, dc=dc, b=b: h.matmul(K.pp[0][:, c * 128 + b * 8:c * 128 + b * 8 + 8], lhsT=KTb[:, hh * 2 + dc, mt * 128:(mt + 1) * 128], rhs=qT[:, hh * 2 + dc, scol + b * 8:scol + b * 8 + 8], start=(dc == 0), stop=(dc == 1)),
                                 reads=["KTb0", "KTb1", "qT"], writes=["ps%d" % (c // 4)])
            evac(P, "act", sTs[:, 0:4, :], K.pp[0][:, 0:512].rearrange("p (a b) -> p a b", b=128), ["ps0"], ["sTs"])
            evac(P, "dve", sTs[:, 4:8, :], K.pp[0][:, 512:1024].rearrange("p (a b) -> p a b", b=128), ["ps1"], ["sTs"])
            for hh in range(4):
                for mt in range(2):
                    P.op("pe", lambda h, hh=hh, mt=mt: h.transpose(out=K.pp[1][:, hh * 256 + mt * 128:hh * 256 + (mt + 1) * 128], in_=sTs[:, hh * 2 + mt, :], identity=K.identf[:]),
                         reads=["sTs", "identf"], writes=["ps%d" % (2 + hh // 2)])
            softmax_pt(K, None, P, 1, 0, MEM_SCALE, p[bi], Dm[bi], pT[bi], sm[bi], tag)
            for b in range(16):
                sb_ = b % 2
                P.dma("sp", lambda h, b=b, sb_=sb_: h.dma_start(out=Kst[sb_], in_=I["memv"][L, b].rearrange("(a p) n -> p a n", p=128)), writes=["Kst%d" % sb_])
                P.op("pool", lambda h, sb_=sb_: h.tensor_copy(out=Vb, in_=Kst[sb_]), reads=["Kst%d" % sb_], writes=["Vb"])
                for hh in range(4):
                    for dvc in range(2):
                        c = hh * 2 + dvc
                        for mt in range(2):
                            P.op("pe", lambda h, hh=hh, dvc=dvc, c=c, mt=mt, bi=bi, b=b: h.matmul(K.pp[2][:, c * 128 + b * 8:c * 128 + b * 8 + 8], lhsT=Vb[:, mt, hh * 256 + dvc * 128:hh * 256 + (dvc + 1) * 128], rhs=pT[bi][:, hh * 2 + mt, b * 8:b * 8 + 8], start=(mt == 0), stop=(mt == 1)),
                                 reads=["Vb", tag + "pT0", tag + "pT1"], writes=["ps%d" % (4 + c // 4)])
        evac(P, "act", oT[bi][:, 0:4, :], K.pp[2][:, 0:512].rearrange("p (a b) -> p a b", b=128), ["ps4"], [tag + "oT"])
        evac(P, "dve", oT[bi][:, 4:8, :], K.pp[2][:, 512:1024].rearrange("p (a b) -> p a b", b=128), ["ps5"], [tag + "oT"])
        for nb in range(2):
            for c8 in range(8):
                P.op("pe", lambda h, nb=nb, c8=c8, bi=bi: h.matmul(K.pp[3][:, nb * 512:(nb + 1) * 512], lhsT=oT[bi][:, c8, :], rhs=wo[:, c8, nb * 512:(nb + 1) * 512], start=(c8 == 0), stop=(c8 == 7)),
                     reads=[tag + "oT", wok], writes=["ps%d" % (6 + nb)])
        K.post_norm_add(li, K.pp[3][:, :], ["ps6", "ps7"], g_post, gpok)
    P.barrier()


def from_mlp(K, L):
    P, I, O = K.P, K.I, K.O
    C = Carver(K.arena)
    TOK = K.tok
    NT = K.ntile
    acc = C(NT * 1024, F32, [NT, 1024])
    hdnT = C(TOK * 4, BF16, [8, TOK])
    r = [C(512) for _ in range(2)]
    g_pre, gpk = K.load_gain(I["norm_mlp_pre"][L])
    for li in range(NT):
        K.norm_to_xnT(li, g_pre, gpk, li * 128)
    xk = ["xnT.%d" % k for k in range(8)]
    nblk = 2
    bs = TOK // nblk
    u = 0
    for c in range(4):
        wu, wuk = K.load_w(I["w_mlp_up"][L][:, c * 1024:(c + 1) * 1024], 1024)
        wd, wdk = K.load_w(I["w_mlp_down"][L][c * 1024:(c + 1) * 1024, :], 1024)
        for blk in range(nblk):
            cs = slice(blk * bs, (blk + 1) * bs)
            for f in range(8):
                bank = u % 4
                ri = u % 2
                for k in range(8):
                    P.op("pe", lambda h, bank=bank, f=f, k=k, cs=cs, wu=wu: h.matmul(K.ps[bank][:, 0:bs], lhsT=wu[:, k, f * 128:(f + 1) * 128], rhs=K.xnT[:, k, cs], start=(k == 0), stop=(k == 7)),
                         reads=xk + [wuk], writes=["ps%d" % bank])
                P.op("act", lambda h, bank=bank, ri=ri: h.activation(out=r[ri][:, 0:bs], in_=K.ps[bank][:, 0:bs], func=AF.Relu), reads=["ps%d" % bank], writes=["r%d" % ri])
                P.op("pool", lambda h, ri=ri, f=f, cs=cs: h.tensor_tensor(out=hdnT[:, f, cs], in0=r[ri][:, 0:bs], in1=r[ri][:, 0:bs], op=ALU.mult), reads=["r%d" % ri], writes=["hdnT.%d.%d" % (f, blk)])
                u += 1
        for li in range(NT):
            blk = (li * 128) // bs
            pair = 2 + li % 2
            for nb in range(2):
                for f in range(8):
                    P.op("pe", lambda h, pair=pair, nb=nb, f=f, li=li, wd=wd: h.matmul(K.pp[pair][:, nb * 512:(nb + 1) * 512], lhsT=hdnT[:, f, li * 128:(li + 1) * 128], rhs=wd[:, f, nb * 512:(nb + 1) * 512], start=(f == 0), stop=(f == 7)),
                         reads=["hdnT.%d.%d" % (f, b_) for b_ in range(nblk)] + [wdk], writes=["ps%d" % (2 * pair + nb)])
            pk = ["ps%d" % (2 * pair), "ps%d" % (2 * pair + 1)]
            if c == 0:
                evac(P, "act", acc[:, li, :], K.pp[pair][:, :], pk, ["acc%d" % li])
            else:
                P.op("dve", lambda h, li=li, pair=pair: h.tensor_tensor(out=acc[:, li, :], in0=acc[:, li, :], in1=K.pp[pair][:, :], op=ALU.add), reads=pk + ["acc%d" % li], writes=["acc%d" % li])
    g_post, gpok = K.load_gain(I["norm_mlp_post"][L])
    for li in range(NT):
        K.post_norm_add(li, acc[:, li, :], ["acc%d" % li], g_post, gpok)
    P.barrier()


def from_mla(K):
    P, I, O = K.P, K.I, K.O
    C = Carver(K.arena)
    TOK, NT, grp = K.tok, K.ntile, K.grp
    ptiles = [idx for (kind, idx) in grp if kind == "P"]
    has_s = len(ptiles) < len(grp)
    tp = 128 * len(ptiles)
    invf = K.invf[:, 0:1]
    cqT = C(3 * TOK // 2, BF16, [3, TOK])
    TQ = C(TOK)
    CS = C(NT * 128, F32, [NT, 128])
    glat = C(256)
    gq = C(384)
    off_wkr2 = C.off
    wkr2 = C(512, BF16, [8, 128])
    off_wqr = C.off
    wqr = C(1536, BF16, [3, 8, 128])
    off_wukT = C.off
    wukT = C(1024, BF16, [8, 256])
    oT = C(4 * TOK, BF16, [8, TOK])
    ckvf = C(256)
    ab = C(128)
    kr2 = C(128)
    if has_s:
        qSl = C(1024, BF16, [2, 128, 8])
        qSr = C(512, BF16, [128, 8])
        KTn = C(192, BF16, [3, 128])
        Vn = C(128, BF16)
        olS = C(1024, BF16, [2, 128, 8])
    base = C.off
    xI = C(TOK).bitcast(I32)
    rr = C(TOK)
    stg = C(2048, F32, [2, 1024])
    cqf = C(384)
    C.off = base
    qn = [C(TOK // 2, BF16) for _ in range(2)]
    qlat = [C(TOK, BF16, [2, TOK]) for _ in range(2)]
    qs = [C(TOK // 2, BF16) for _ in range(2)]
    pb = [C(1024, BF16) for _ in range(2)]
    Dm = [C(64, BF16) for _ in range(2)]
    pT = [C(1024, BF16, [16, 128]) for _ in range(2)]
    olT = [C(128, BF16, [2, 128]) for _ in range(2)]
    sm = [C(16) for _ in range(2)]

    g_in, gk = K.load_gain(I["kv_in_norm"])
    for li in range(NT):
        K.norm_to_xnT(li, g_in, gk, li * 128)
    wdkv, wdkvk = K.load_w(I["w_dkv"], 256)
    wkr, wkrk = K.load_w(I["w_kr"], 64)
    P.op("act", lambda h: h.copy(out=wkr2[:, :, 0:64], in_=wkr), reads=[wkrk], writes=["wkr2"])
    P.op("act", lambda h: h.mul(out=wkr2[:, :, 64:96], in_=wkr[:, :, 32:64], mul=-1.0), reads=[wkrk, "wkr2"], writes=["wkr2"])
    P.op("act", lambda h: h.copy(out=wkr2[:, :, 96:128], in_=wkr[:, :, 0:32]), reads=[wkrk, "wkr2"], writes=["wkr2"])
    P.dma("sp", lambda h: h.dma_start(out=glat, in_=I["kv_latent_norm"].rearrange("(o n) -> o n", o=1).to_broadcast([128, 256])), writes=["glat"])
    P.dma("sp", lambda h: h.dma_start(out=gq, in_=I["q_norm"][0].rearrange("(o n) -> o n", o=1).to_broadcast([128, 384])), writes=["gq"])
    P.op("dve", lambda h: h.tensor_scalar(out=xI[:, 0:tp], in0=K.iota_g[:, 0:tp], scalar1=invf, scalar2=None, op0=ALU.mult), reads=["iota_g", "invf"], writes=["xI"])
    P.op("dve", lambda h: h.scalar_tensor_tensor(out=rr[:, 0:tp], in0=K.iota_g[:, 0:tp], scalar=invf, in1=xI[:, 0:tp], op0=ALU.mult, op1=ALU.subtract), reads=["iota_g", "invf", "xI"], writes=["rr"])
    if has_s:
        P.op("dve", lambda h: h.tensor_scalar(out=xI[:, tp:TOK], in0=K.iota_g[:, tp:TOK], scalar1=float(PAST), scalar2=invf, op0=ALU.add, op1=ALU.mult), reads=["iota_g", "invf", "xI"], writes=["xI"])
        P.op("dve", lambda h: h.tensor_scalar(out=rr[:, tp:TOK], in0=K.iota_g[:, tp:TOK], scalar1=float(PAST), scalar2=invf, op0=ALU.add, op1=ALU.mult), reads=["iota_g", "invf", "rr"], writes=["rr"])
        P.op("dve", lambda h: h.tensor_tensor(out=rr[:, tp:TOK], in0=rr[:, tp:TOK], in1=xI[:, tp:TOK], op=ALU.subtract), reads=["xI", "rr"], writes=["rr"])
    P.op("act", lambda h: h.activation(out=TQ[64:128, 0:TOK], in_=rr[64:128, 0:TOK], func=AF.Sin, scale=TWO_PI), reads=["rr"], writes=["TQs"])
    P.op("act", lambda h: h.activation(out=TQ[0:64, 0:TOK], in_=rr[0:64, 0:TOK], func=AF.Abs), reads=["rr"], writes=["TQc"])
    P.op("act", lambda h: h.activation(out=TQ[0:64, 0:TOK], in_=TQ[0:64, 0:TOK], func=AF.Sin, scale=-TWO_PI, bias=K.halfpi[0:64]), reads=["TQc"], writes=["TQc"])
    for li in range(NT):
        P.op("pe", lambda h, li=li: h.transpose(out=K.ps[3][:, 0:128], in_=TQ[:, li * 128:(li + 1) * 128], identity=K.identf[:]), reads=["TQs", "TQc", "identf"], writes=["ps3"])
        evac(P, "act", CS[:, li, :], K.ps[3][:, 0:128], ["ps3"], ["CS"])
    xk = ["xnT.%d" % k for k in range(8)]
    for li, (kind, idx) in enumerate(grp):
        tcols = slice(li * 128, (li + 1) * 128)
        for k in range(8):
            P.op("pe", lambda h, k=k, tcols=tcols: h.matmul(K.ps[0][:, 0:256], lhsT=K.xnT[:, k, tcols], rhs=wdkv[:, k, :], start=(k == 0), stop=(k == 7)), reads=xk + [wdkvk], writes=["ps0"])
        for k in range(8):
            P.op("pe", lambda h, k=k, tcols=tcols: h.matmul(K.ps[1][:, 0:128], lhsT=K.xnT[:, k, tcols], rhs=wkr2[:, k, :], start=(k == 0), stop=(k == 7)), reads=xk + ["wkr2"], writes=["ps1"])
        rs, rk = K.rms_stats(K.ps[0][:, 0:256], ["ps0"], 256)
        P.op("dve", lambda h, rs=rs: h.scalar_tensor_tensor(out=ckvf, in0=K.ps[0][:, 0:256], scalar=rs, in1=glat, op0=ALU.mult, op1=ALU.mult), reads=["ps0", rk, "glat"], writes=["ckvf"])
        if kind == "P":
            P.dma("sp", lambda h, idx=idx: h.dma_start(out=O["kvl_p"][idx * 128:(idx + 1) * 128, :], in_=ckvf), reads=["ckvf"], is_output=True)
            P.op("pool", lambda h, idx=idx: h.tensor_copy(out=K.Vp[:, idx, :], in_=ckvf), reads=["ckvf"], writes=["Vp"])
            K.transpose_to(ckvf, "ckvf", K.KT, "KT", idx * 128, nk=2, banks=(2,))
        else:
            P.dma("sp", lambda h: h.dma_start(out=O["kvl_s"][:, :], in_=ckvf), reads=["ckvf"], is_output=True)
            P.op("pool", lambda h: h.tensor_copy(out=Vn, in_=ckvf), reads=["ckvf"], writes=["Vn"])
            K.transpose_to(ckvf, "ckvf", KTn, "KTn", 0, nk=2, banks=(2,))
        P.op("dve", lambda h, li=li: h.tensor_tensor(out=ab, in0=K.ps[1][:, 0:128], in1=CS[:, li, :], op=ALU.mult), reads=["ps1", "CS"], writes=["ab"])
        P.op("pool", lambda h: h.tensor_tensor(out=kr2[:, 0:64], in0=ab[:, 0:64], in1=ab[:, 64:128], op=ALU.add), reads=["ab"], writes=["kr2"])
        P.op("pool", lambda h: h.tensor_copy(out=kr2[:, 64:128], in_=kr2[:, 0:64]), reads=["kr2"], writes=["kr2"])
        if kind == "P":
            P.dma("sp", lambda h, idx=idx: h.dma_start(out=O["kr_p"][idx * 128:(idx + 1) * 128, :], in_=kr2[:, 0:64]), reads=["kr2"], is_output=True)
            K.transpose_to(kr2, "kr2", K.KT[:, 2:3, :], "KTr", idx * 128, nk=1, banks=(3,))
        else:
            P.dma("sp", lambda h: h.dma_start(out=O["kr_s"][:, :], in_=kr2[:, 0:64]), reads=["kr2"], is_output=True)
            K.transpose_to(kr2, "kr2", KTn[:, 2:3, :], "KTnr", 0, nk=1, banks=(3,))

    g_pre, gpk = K.load_gain(I["norm_mix_pre"][1])
    for li in range(NT):
        K.norm_to_xnT(li, g_pre, gpk, li * 128)
    wdq, wdqk = K.load_w(I["w_dq"][0], 384)
    for li in range(NT):
        tcols = slice(li * 128, (li + 1) * 128)
        bank = 4 + li % 2
        for k in range(8):
            P.op("pe", lambda h, k=k, tcols=tcols, bank=bank: h.matmul(K.ps[bank][:, 0:384], lhsT=K.xnT[:, k, tcols], rhs=wdq[:, k, :], start=(k == 0), stop=(k == 7)), reads=xk + [wdqk], writes=["ps%d" % bank])
        rs, rk = K.rms_stats(K.ps[bank][:, 0:384], ["ps%d" % bank], 384)
        P.op("dve", lambda h, rs=rs, bank=bank: h.scalar_tensor_tensor(out=cqf, in0=K.ps[bank][:, 0:384], scalar=rs, in1=gq, op0=ALU.mult, op1=ALU.mult), reads=["ps%d" % bank, rk, "gq"], writes=["cqf"])
        K.transpose_to(cqf, "cqf", cqT, "cqT", li * 128, nk=3, banks=(6,))
    cqk = ["cqT.%d" % k for k in range(3)]
    wuq, wuqk = K.load_w(I["w_uq"][0].rearrange("r h d -> r (h d)"), 1536, nk=3)
    wuq4 = wuq.rearrange("p k (h d) -> p k h d", d=192)
    P.op("act", lambda h: h.copy(out=wqr[:, :, :, 0:64], in_=wuq4[:, :, :, 128:192]), reads=[wuqk], writes=["wqr"])
    P.op("act", lambda h: h.mul(out=wqr[:, :, :, 64:96], in_=wuq4[:, :, :, 160:192], mul=-1.0), reads=[wuqk, "wqr"], writes=["wqr"])
    P.op("act", lambda h: h.copy(out=wqr[:, :, :, 96:128], in_=wuq4[:, :, :, 128:160]), reads=[wuqk, "wqr"], writes=["wqr"])
    P.dma("sp", lambda h: h.dma_start(out=stg, in_=I["w_uk"].rearrange("(c p) h d -> p c (h d)", p=128)), writes=["stg"])
    for hh in range(8):
        for rc in range(2):
            t_ = hh * 2 + rc
            P.op("pe", lambda h, hh=hh, rc=rc, t_=t_: h.transpose(out=K.pp[t_ // 8][:, (t_ % 8) * 128:(t_ % 8 + 1) * 128], in_=stg[:, rc, hh * 128:(hh + 1) * 128], identity=K.identf[:]),
                 reads=["stg", "identf"], writes=["ps%d" % (2 * (t_ // 8) + (t_ % 8) // 4)])
    evac(P, "act", wukT[:, 0:4, :], K.pp[0][:, :].rearrange("p (a b) -> p a b", b=256), ["ps0", "ps1"], ["wukT"])
    evac(P, "dve", wukT[:, 4:8, :], K.pp[1][:, :].rearrange("p (a b) -> p a b", b=256), ["ps2", "ps3"], ["wukT"])
    wuv, wuvk = K.load_w(I["w_uv"].rearrange("r h v -> r (h v)"), 1024, nk=2)
    wo, wok = K.load_w(I["w_o"][0], 1024)
    P.barrier()

    nblk = 2
    bs = TOK // nblk
    cnt = 0
    for hh in range(8):
        hb = hh % 2
        qk = "q%d" % hb
        for blk in range(nblk):
            cs = slice(blk * bs, (blk + 1) * bs)
            for kc in range(3):
                P.op("pe", lambda h, hh=hh, kc=kc, cs=cs: h.matmul(K.ps[4][:, 0:bs], lhsT=wuq[:, kc, hh * 192:hh * 192 + 128], rhs=cqT[:, kc, cs], start=(kc == 0), stop=(kc == 2)), reads=cqk + [wuqk], writes=["ps4"])
            evac(P, "act", qn[hb][:, cs], K.ps[4][:, 0:bs], ["ps4"], [qk + "n"])
            for rc in range(2):
                P.op("pe", lambda h, hh=hh, rc=rc, cs=cs, hb=hb: h.matmul(K.ps[5 + rc][:, 0:bs], lhsT=wukT[:, hh, rc * 128:(rc + 1) * 128], rhs=qn[hb][:, cs], start=True, stop=True), reads=["wukT", qk + "n"], writes=["ps%d" % (5 + rc)])
                evac(P, "dve" if rc == 0 else "act", qlat[hb][:, rc, cs], K.ps[5 + rc][:, 0:bs], ["ps%d" % (5 + rc)], [qk + "l"])
            for kc in range(3):
                P.op("pe", lambda h, hh=hh, kc=kc, cs=cs: h.matmul(K.ps[7][:, 0:bs], lhsT=wqr[:, kc, hh, :], rhs=cqT[:, kc, cs], start=(kc == 0), stop=(kc == 2)), reads=cqk + ["wqr"], writes=["ps7"])
            P.op("dve", lambda h, cs=cs, hb=hb: h.tensor_tensor(out=qs[hb][:, cs], in0=K.ps[7][:, 0:bs], in1=TQ[:, cs], op=ALU.mult), reads=["ps7", "TQs", "TQc"], writes=[qk + "s"])
        if has_s:
            P.op("pool", lambda h, hh=hh, hb=hb: h.tensor_copy(out=qSl[:, :, :, hh], in_=qlat[hb][:, :, tp:TOK]), reads=[qk + "l"], writes=["qSl"])
            P.op("pool", lambda h, hh=hh, hb=hb: h.tensor_copy(out=qSr[:, :, hh], in_=qs[hb][:, tp:TOK]), reads=[qk + "s"], writes=["qSr"])
        for li, (kind, idx) in enumerate(grp):
            if kind != "P":
                continue
            bi = cnt % 2
            cnt += 1
            tag = "a%d" % bi
            tcols = slice(li * 128, (li + 1) * 128)
            nk = idx + 1
            nkeys = 128 * nk
            nkb = (nkeys + 511) // 512
            for kb in range(nkb):
                w = min(512, nkeys - kb * 512)
                kcols = slice(kb * 512, kb * 512 + w)
                P.op("pe", lambda h, kb=kb, w=w, kcols=kcols, tcols=tcols, hb=hb: h.matmul(K.ps[kb][:, 0:w], lhsT=qlat[hb][:, 0, tcols], rhs=K.KT[:, 0, kcols], start=True, stop=False), reads=[qk + "l", "KT.0"], writes=["ps%d" % kb])
                P.op("pe", lambda h, kb=kb, w=w, kcols=kcols, tcols=tcols, hb=hb: h.matmul(K.ps[kb][:, 0:w], lhsT=qlat[hb][:, 1, tcols], rhs=K.KT[:, 1, kcols], start=False, stop=False), reads=[qk + "l", "KT.1"], writes=["ps%d" % kb])
                P.op("pe", lambda h, kb=kb, w=w, kcols=kcols, tcols=tcols, hb=hb: h.matmul(K.ps[kb][:, 0:w], lhsT=qs[hb][:, tcols], rhs=K.KT[:, 2, kcols], start=False, stop=True), reads=[qk + "s", "KTr.0"], writes=["ps%d" % kb])
            db = (idx * 128) // 512
            do = (idx * 128) % 512
            P.op("dve", lambda h, db=db, do=do: h.tensor_tensor(out=K.ps[db][:, do:do + 128], in0=K.ps[db][:, do:do + 128], in1=K.cm[:], op=ALU.add), reads=["ps%d" % db, "cm"], writes=["ps%d" % db])
            n0 = min(nkeys, 1024)
            n1 = nkeys - n0
            smt = sm[bi]
            mx, nb_, l0, l1, rl = smt[:, 0:1], smt[:, 2:3], smt[:, 3:4], smt[:, 4:5], smt[:, 5:6]
            P.op("dve", lambda h, n0=n0, mx=mx: h.tensor_reduce(out=mx, in_=K.pp[0][:, 0:n0], axis=AX.X, op=ALU.max), reads=["ps0", "ps1"], writes=[tag + "sm"])
            if n1 > 0:
                m1 = smt[:, 1:2]
                P.op("dve", lambda h, n1=n1, m1=m1: h.tensor_reduce(out=m1, in_=K.pp[1][:, 0:n1], axis=AX.X, op=ALU.max), reads=["ps2", "ps3", tag + "sm"], writes=[tag + "sm"])
                P.op("dve", lambda h, mx=mx, m1=m1: h.tensor_tensor(out=mx, in0=mx, in1=m1, op=ALU.max), reads=[tag + "sm"], writes=[tag + "sm"])
            P.op("dve", lambda h, mx=mx, nb_=nb_: h.tensor_scalar(out=nb_, in0=mx, scalar1=-MLA_SCALE, scalar2=None, op0=ALU.mult), reads=[tag + "sm"], writes=[tag + "sm"])
            P.op("act", lambda h, n0=n0, bi=bi, nb_=nb_, l0=l0: h.activation(out=pb[bi][:, 0:n0], in_=K.pp[0][:, 0:n0], func=AF.Exp, scale=MLA_SCALE, bias=nb_, accum_out=l0), reads=["ps0", "ps1", tag + "sm"], writes=[tag + "p", tag + "l"])
            if n1 > 0:
                P.op("act", lambda h, n0=n0, n1=n1, bi=bi, nb_=nb_, l1=l1: h.activation(out=pb[bi][:, n0:n0 + n1], in_=K.pp[1][:, 0:n1], func=AF.Exp, scale=MLA_SCALE, bias=nb_, accum_out=l1), reads=["ps2", "ps3", tag + "sm", tag + "l"], writes=[tag + "p", tag + "l"])
                P.op("dve", lambda h, l0=l0, l1=l1: h.tensor_tensor(out=l0, in0=l0, in1=l1, op=ALU.add), reads=[tag + "l", tag + "sm"], writes=[tag + "l"])
            P.op("dve", lambda h, l0=l0, rl=rl: h.reciprocal(out=rl, in_=l0), reads=[tag + "l", tag + "sm"], writes=[tag + "sm"])
            P.op("dve", lambda h, bi=bi, rl=rl: h.tensor_scalar(out=Dm[bi], in0=K.ident[:], scalar1=rl, scalar2=None, op0=ALU.mult), reads=["ident", tag + "sm"], writes=[tag + "D"])
            for kt in range(nk):
                P.op("pe", lambda h, kt=kt, bi=bi: h.matmul(K.pp[2 + kt // 8][:, (kt % 8) * 128:(kt % 8 + 1) * 128], lhsT=pb[bi][:, kt * 128:(kt + 1) * 128], rhs=Dm[bi], start=True, stop=True),
                     reads=[tag + "p", tag + "D"], writes=["ps%d" % (4 + 2 * (kt // 8) + (kt % 8) // 4)])
            na = min(nk, 8)
            evac(P, "act", pT[bi][:, 0:na, :], K.pp[2][:, 0:na * 128].rearrange("p (a b) -> p a b", b=128), ["ps4", "ps5"], [tag + "pTa"])
            if nk > 8:
                evac(P, "dve", pT[bi][:, 8:nk, :], K.pp[3][:, 0:(nk - 8) * 128].rearrange("p (a b) -> p a b", b=128), ["ps6", "ps7"], [tag + "pTb"])
            for rc in range(2):
                for kt in range(nk):
                    P.op("pe", lambda h, rc=rc, kt=kt, bi=bi: h.matmul(K.ps[0][:, rc * 128:(rc + 1) * 128], lhsT=K.Vp[:, kt, rc * 128:(rc + 1) * 128], rhs=pT[bi][:, kt, :], start=(kt == 0), stop=(kt == nk - 1)),
                         reads=["Vp", tag + "pTa", tag + "pTb"], writes=["ps0"])
            evac(P, "act", olT[bi], K.ps[0][:, 0:256].rearrange("p (a b) -> p a b", b=128), ["ps0"], [tag + "ol"])
            for rc in range(2):
                P.op("pe", lambda h, rc=rc, bi=bi, hh=hh: h.matmul(K.ps[1][:, 0:128], lhsT=wuv[:, rc, hh * 128:(hh + 1) * 128], rhs=olT[bi][:, rc, :], start=(rc == 0), stop=(rc == 1)), reads=[wuvk, tag + "ol"], writes=["ps1"])
            evac(P, "dve", oT[:, hh, tcols], K.ps[1][:, 0:128], ["ps1"], ["oT.%d" % li])

    if has_s:
        mla_sample(K, C, locals())

    g_post, gpok = K.load_gain(I["norm_mix_post"][1])
    for li in range(NT):
        tcols = slice(li * 128, (li + 1) * 128)
        pair = 2 + li % 2
        for nb in range(2):
            for hh in range(8):
                P.op("pe", lambda h, nb=nb, hh=hh, tcols=tcols, pair=pair: h.matmul(K.pp[pair][:, nb * 512:(nb + 1) * 512], lhsT=oT[:, hh, tcols], rhs=wo[:, hh, nb * 512:(nb + 1) * 512], start=(hh == 0), stop=(hh == 7)),
                     reads=["oT.%d" % li, wok], writes=["ps%d" % (2 * pair + nb)])
        K.post_norm_add(li, K.pp[pair][:, :], ["ps%d" % (2 * pair), "ps%d" % (2 * pair + 1)], g_post, gpok)
    P.barrier()


def mla_sample(K, C, env):
    P, I, O = K.P, K.I, K.O
    qSl, qSr, KTn, Vn, olS, oT, wuv, wuvk = (env[k] for k in ("qSl", "qSr", "KTn", "Vn", "olS", "oT", "wuv", "wuvk"))
    tp, TOK = env["tp"], env["TOK"]
    P.barrier()
    C.off = env["base"]
    stgK = [C(2048), K.arena[:, 0:2048]]
    stgR = [C(512), K.arena[:, env["off_wkr2"]:env["off_wkr2"] + 512]]
    krd = K.arena[:, env["off_wqr"]:env["off_wqr"] + 1024].rearrange("p (a b) -> p a b", b=128)
    Vb = [K.arena[:, env["off_wukT"]:env["off_wukT"] + 1024].bitcast(BF16).rearrange("p (a b) -> p a b", b=256),
          K.xn[:].bitcast(BF16).rearrange("p (a b) -> p a b", b=256)]
    KTb = C(1536, BF16, [3, 1024])
    pS = C(512, BF16)
    pTs = C(256, BF16, [8, 64])
    Oacc = C(256)
    On = C(256)
    mS = C(16)
    maskb = C(128)
    ptf = stgK[0][:, 0:1024]
    tmpx = stgK[0][:, 1024:2048]
    idx = C(128).bitcast(I32)
    Msel = C(8)
    pm = C(2)
    pti = tmpx.bitcast(I32)

    P.dma("sp", lambda h: h.dma_start(out=pti, in_=I["pt"].rearrange("b (o n) -> o (b n)", o=1).to_broadcast([128, 1024])), writes=["tmpx"])
    P.op("dve", lambda h: h.tensor_copy(out=ptf, in_=pti), reads=["tmpx"], writes=["ptf"])
    P.op("pool", lambda h: h.memset(Msel, 1.0), writes=["Msel"])
    P.op("pool", lambda h: h.affine_select(out=Msel, in_=Msel, pattern=[[-16, 8]], compare_op=ALU.is_ge, fill=0.0, base=0, channel_multiplier=1), reads=["Msel"], writes=["Msel"])
    P.op("pool", lambda h: h.affine_select(out=Msel, in_=Msel, pattern=[[16, 8]], compare_op=ALU.is_ge, fill=0.0, base=15, channel_multiplier=-1), reads=["Msel"], writes=["Msel"])
    P.op("dve", lambda h: h.tensor_tensor(out=tmpx.rearrange("p (a j) -> p a j", j=8), in0=ptf.rearrange("p (a j) -> p a j", j=8),
                                          in1=Msel.unsqueeze(1).to_broadcast([128, 128, 8]), op=ALU.mult), reads=["ptf", "Msel", "tmpx"], writes=["tmpx"])
    P.op("dve", lambda h: h.tensor_reduce(out=ptf[:, 0:128], in_=tmpx.rearrange("p (a j) -> p a j", j=8), axis=AX.X, op=ALU.add), reads=["tmpx", "ptf"], writes=["ptf"])
    pmi = pm.bitcast(I32)
    P.op("pool", lambda h: h.iota(pmi[:, 0:1], pattern=[[0, 1]], base=0, channel_multiplier=1), writes=["pm"])
    P.op("dve", lambda h: h.tensor_single_scalar(out=pmi[:, 1:2], in_=pmi[:, 0:1], scalar=15, op=ALU.bitwise_and), reads=["pm"], writes=["pm"])
    P.op("dve", lambda h: h.tensor_copy(out=pm[:, 0:1], in_=pmi[:, 1:2]), reads=["pm"], writes=["pm"])
    P.op("dve", lambda h: h.tensor_scalar(out=idx, in0=ptf[:, 0:128], scalar1=16.0, scalar2=pm[:, 0:1], op0=ALU.mult, op1=ALU.add), reads=["ptf", "pm"], writes=["idx"])

    P.barrier()
    ident64 = K.ident[0:64, 0:64]
    blocks = [(b, d) for b in range(16) for d in range(8)]

    def issue_gather(n):
        b, d = blocks[n]
        sb = n % 2
        col = b * 8 + d
        P.dma("pool", lambda h: h.indirect_dma_start(out=stgK[sb], out_offset=None, in_=I["ckvc"], in_offset=bass.IndirectOffsetOnAxis(ap=idx[:, col:col + 1], axis=0)), reads=["idx"], writes=["stgK%d" % sb])
        P.dma("pool", lambda h: h.indirect_dma_start(out=stgR[sb], out_offset=None, in_=I["krc"], in_offset=bass.IndirectOffsetOnAxis(ap=idx[:, col:col + 1], axis=0)), reads=["idx"], writes=["stgR%d" % sb])

    m_, l_, bm, mn, corr, nb_, ls = (mS[0:64, i:i + 1] for i in range(7))

    KT2 = [[KTb[:, c, :] for c in range(3)],
           [K.gbc[0][:, 0:512].bitcast(BF16), K.gbc[0][:, 512:1024].bitcast(BF16), K.gbc[1][:, 0:512].bitcast(BF16)]]

    def softmax_update(sp_, nkeys, nslot, vsrc, vkey):
        Sp = K.pp[sp_]
        S = Sp[0:64, 0:nkeys]
        sk = ["ps%d" % (2 * sp_), "ps%d" % (2 * sp_ + 1)] if nkeys > 512 else ["ps%d" % (2 * sp_)]
        P.op("dve", lambda h: h.tensor_reduce(out=bm, in_=S, axis=AX.X, op=ALU.max), reads=sk + ["mS"], writes=["mS"])
        P.op("dve", lambda h: h.tensor_tensor(out=mn, in0=m_, in1=bm, op=ALU.max), reads=["mS"], writes=["mS"])
        P.op("dve", lambda h: h.tensor_tensor(out=corr, in0=m_, in1=mn, op=ALU.subtract), reads=["mS"], writes=["mS"])
        P.op("act", lambda h: h.activation(out=corr, in_=corr, func=AF.Exp, scale=MLA_SCALE), reads=["mS"], writes=["mS"])
        P.op("dve", lambda h: h.tensor_scalar(out=nb_, in0=mn, scalar1=-MLA_SCALE, scalar2=None, op0=ALU.mult), reads=["mS"], writes=["mS"])
        P.op("act", lambda h: h.activation(out=pS[0:64, 0:nkeys], in_=S, func=AF.Exp, scale=MLA_SCALE, bias=nb_, accum_out=ls), reads=sk + ["mS"], writes=["pS", "mS"])
        P.op("dve", lambda h: h.scalar_tensor_tensor(out=l_, in0=l_, scalar=corr, in1=ls, op0=ALU.mult, op1=ALU.add), reads=["mS"], writes=["mS"])
        P.op("dve", lambda h: h.tensor_copy(out=m_, in_=mn), reads=["mS"], writes=["mS"])
        for s_ in range(nslot):
            P.op("pe", lambda h, s_=s_: h.matmul(K.ps[7][:, s_ * 64:(s_ + 1) * 64], lhsT=pS[0:64, s_ * 128:(s_ + 1) * 128], rhs=ident64, start=True, stop=True), reads=["pS", "ident"], writes=["ps7"])
        evac(P, "act", pTs[:, 0:nslot, :], K.ps[7][:, 0:nslot * 64].rearrange("p (a b) -> p a b", b=64), ["ps7"], ["pTs"])
        for s_ in range(nslot):
            P.op("pe", lambda h, s_=s_: h.matmul(Sp[0:64, 0:256], lhsT=pTs[:, s_, :], rhs=vsrc(s_), start=(s_ == 0), stop=(s_ == nslot - 1)), reads=["pTs", vkey], writes=["ps%d" % (2 * sp_)])
        P.op("dve", lambda h: h.scalar_tensor_tensor(out=Oacc[0:64, :], in0=Oacc[0:64, :], scalar=corr, in1=Sp[0:64, 0:256], op0=ALU.mult, op1=ALU.add), reads=["Oacc", "ps%d" % (2 * sp_), "mS"], writes=["Oacc"])

    def qviews(b):
        qcols = slice(b * 8, (b + 1) * 8)
        lq = [qSl[:, rc, qcols, :].rearrange("p t h -> p (t h)") for rc in range(2)]
        rq = qSr[:, qcols, :].rearrange("p t h -> p (t h)")
        return qcols, lq, rq

    def stageA(n):
        b, d = blocks[n]
        sb = n % 2
        _, lq, rq = qviews(b)
        sK, sR, vB = stgK[sb], stgR[sb], Vb[sb]
        kK, kR, kV = "stgK%d" % sb, "stgR%d" % sb, "Vb%d" % sb
        KTc = KT2[sb]
        Sp = K.pp[sb]
        P.op("act", lambda h: h.copy(out=vB, in_=sK.rearrange("p (s r) -> p s r", r=256)), reads=[kK], writes=[kV])
        P.op("dve", lambda h: h.tensor_copy(out=krd[:, :, 0:64], in_=sR.rearrange("p (s r) -> p s r", r=64)), reads=[kR], writes=["krd"])
        P.op("dve", lambda h: h.tensor_copy(out=krd[:, :, 64:128], in_=sR.rearrange("p (s r) -> p s r", r=64)), reads=[kR, "krd"], writes=["krd"])
        for half in range(2):
            for sl in range(4):
                s_ = half * 4 + sl
                for c in range(3):
                    src = sK[:, s_ * 256 + c * 128:s_ * 256 + (c + 1) * 128] if c < 2 else krd[:, s_, :]
                    P.op("pe", lambda h, src=src, c=c, sl=sl: h.transpose(out=K.ps[4 + c][:, sl * 128:(sl + 1) * 128], in_=src, identity=K.identf[:]),
                         reads=[kK if c < 2 else "krd", "identf"], writes=["ps%d" % (4 + c)])
            for c in range(3):
                evac(P, ("act", "dve", "act")[c] if half == 0 else ("dve", "act", "dve")[c], KTc[c][:, half * 512:(half + 1) * 512], K.ps[4 + c][:, :], ["ps%d" % (4 + c)], ["KT%d.%d.%d" % (sb, c, half)])
            ob = Sp[0:64, half * 512:(half + 1) * 512]
            obk = "ps%d" % (2 * sb + half)
            P.op("pe", lambda h, half=half, ob=ob: h.matmul(ob, lhsT=lq[0], rhs=KTc[0][:, half * 512:(half + 1) * 512], start=True, stop=False), reads=["qSl", "KT%d.0.%d" % (sb, half)], writes=[obk])
            P.op("pe", lambda h, half=half, ob=ob: h.matmul(ob, lhsT=lq[1], rhs=KTc[1][:, half * 512:(half + 1) * 512], start=False, stop=False), reads=["qSl", "KT%d.1.%d" % (sb, half)], writes=[obk])
            P.op("pe", lambda h, half=half, ob=ob: h.matmul(ob, lhsT=rq, rhs=KTc[2][:, half * 512:(half + 1) * 512], start=False, stop=True), reads=["qSr", "KT%d.2.%d" % (sb, half)], writes=[obk])

    def stageB(n):
        b, d = blocks[n]
        sb = n % 2
        qcols, lq, rq = qviews(b)
        vB = Vb[sb]
        if d == 0:
            P.op("dve", lambda h: h.memset(m_, NEG), reads=["mS"], writes=["mS"])
            P.op("dve", lambda h: h.memset(l_, 0.0), reads=["mS"], writes=["mS"])
            P.op("pool", lambda h: h.memset(Oacc[0:64, :], 0.0), reads=["Oacc"], writes=["Oacc"])
            P.op("pool", lambda h: h.memset(maskb[0:64, :], 0.0), reads=["maskb"], writes=["maskb"])
            P.op("pool", lambda h: h.affine_select(out=maskb[0:64, :], in_=maskb[0:64, :], pattern=[[1, 128]], compare_op=ALU.is_ge, fill=NEG, base=-8 * b, channel_multiplier=0), reads=["maskb"], writes=["maskb"])
            P.op("pool", lambda h: h.affine_select(out=maskb[0:64, :], in_=maskb[0:64, :], pattern=[[-8, 128]], compare_op=ALU.is_ge, fill=NEG, base=64 * b, channel_multiplier=1), reads=["maskb"], writes=["maskb"])
        softmax_update(sb, 1024, 8, (lambda s_: vB[:, s_, :]), "Vb%d" % sb)
        if d == 7:
            Sp = K.pp[sb]
            pk = "ps%d" % (2 * sb)
            P.op("pe", lambda h: h.matmul(Sp[0:64, 0:128], lhsT=lq[0], rhs=KTn[:, 0, :], start=True, stop=False), reads=["qSl", "KTn.0"], writes=[pk])
            P.op("pe", lambda h: h.matmul(Sp[0:64, 0:128], lhsT=lq[1], rhs=KTn[:, 1, :], start=False, stop=False), reads=["qSl", "KTn.1"], writes=[pk])
            P.op("pe", lambda h: h.matmul(Sp[0:64, 0:128], lhsT=rq, rhs=KTn[:, 2, :], start=False, stop=True), reads=["qSr", "KTnr.0"], writes=[pk])
            P.op("dve", lambda h: h.tensor_tensor(out=Sp[0:64, 0:128], in0=Sp[0:64, 0:128], in1=maskb[0:64, :], op=ALU.add), reads=[pk, "maskb"], writes=[pk])
            softmax_update(sb, 128, 1, (lambda s_: Vn), "Vn")
            rl = mS[0:64, 8:9]
            P.op("dve", lambda h: h.reciprocal(out=rl, in_=l_), reads=["mS"], writes=["mS"])
            P.op("dve", lambda h: h.tensor_scalar(out=On[0:64, :], in0=Oacc[0:64, :], scalar1=rl, scalar2=None, op0=ALU.mult), reads=["Oacc", "mS"], writes=["On"])
            for rc in range(2):
                P.op("pe", lambda h, rc=rc: h.transpose(out=K.ps[7][:, rc * 64:(rc + 1) * 64], in_=On[0:64, rc * 128:(rc + 1) * 128], identity=K.identf[0:64, 0:64]), reads=["On", "identf"], writes=["ps7"])
            evac(P, "act", olS[:, :, qcols, :].rearrange("p c t h -> p c (t h)"), K.ps[7][:, 0:128].rearrange("p (c x) -> p c x", x=64), ["ps7"], ["olS"])

    issue_gather(0)
    issue_gather(1)
    stageA(0)
    for n in range(len(blocks)):
        if n + 2 < len(blocks):
            issue_gather(n + 2)
        if n + 1 < len(blocks):
            stageA(n + 1)
        stageB(n)
    for hh in range(8):
        for rc in range(2):
            P.op("pe", lambda h, hh=hh, rc=rc: h.matmul(K.ps[1][:, 0:128], lhsT=wuv[:, rc, hh * 128:(hh + 1) * 128], rhs=olS[:, rc, :, hh], start=(rc == 0), stop=(rc == 1)), reads=[wuvk, "olS"], writes=["ps1"])
        evac(P, "dve", oT[:, hh, tp:TOK], K.ps[1][:, 0:128], ["ps1"], ["oT.%d" % (K.ntile - 1)])
    P.barrier()


def _prep_inputs(inputs, c):
    m = {}
    m["xp"] = np.ascontiguousarray(inputs["x_prompt"][c])
    m["xs"] = np.ascontiguousarray(inputs["x_sample"][16 * c:16 * c + 16].reshape(128, D))
    m["ssr"] = np.ascontiguousarray(inputs["cache_ssm_re"][0, 16 * c:16 * c + 16].reshape(16, 4096))
    m["ssi"] = np.ascontiguousarray(inputs["cache_ssm_im"][0, 16 * c:16 * c + 16].reshape(16, 4096))
    m["ckvc"] = inputs["cache_kv_latent"].reshape(N_PHYS * 16, 2048)
    m["krc"] = inputs["cache_k_rope"].reshape(N_PHYS * 16, 512)
    m["memk"] = np.ascontiguousarray(inputs["cache_mem_k"][:, 16 * c:16 * c + 16].reshape(2, 16, 256, D))
    m["memv"] = np.ascontiguousarray(inputs["cache_mem_v"][:, 16 * c:16 * c + 16].reshape(2, 16, 256, D))
    m["pt"] = np.ascontiguousarray(inputs["page_table"][16 * c:16 * c + 16]).astype(np.int32)
    m["memp"] = np.ascontiguousarray(inputs["mem_prompt"][c])
    for name, shape in WEIGHT_SPECS:
        m[name] = np.ascontiguousarray(inputs[name]).reshape(shape)
    return m


_NC_CACHE = {}


def kernel(**inputs):
    inputs = {k: np.asarray(v) for k, v in inputs.items()}
    if "nc" not in _NC_CACHE:
        _NC_CACHE["nc"] = build()
    nc = _NC_CACHE["nc"]
    in_maps = [_prep_inputs(inputs, c) for c in range(NCORES)]
    res = run_bass_kernel_spmd(nc, in_maps, core_ids=list(range(NCORES)))
    R = res.results
    f = np.float32
    y_p = np.stack([R[c]["y_p"] for c in range(NCORES)]).astype(f)
    y_s = np.concatenate([R[c]["y_s"].reshape(16, 8, D) for c in range(NCORES)]).astype(f)
    ssm_re_p = np.stack([R[c]["ssm_re_p"].reshape(64, 64) for c in range(NCORES)])[None].astype(f)
    ssm_im_p = np.stack([R[c]["ssm_im_p"].reshape(64, 64) for c in range(NCORES)])[None].astype(f)
    ssm_re_s = np.concatenate([R[c]["ssm_re_s"].reshape(16, 64, 64) for c in range(NCORES)])[None].astype(f)
    ssm_im_s = np.concatenate([R[c]["ssm_im_s"].reshape(16, 64, 64) for c in range(NCORES)])[None].astype(f)
    kvl_p = np.stack([R[c]["kvl_p"] for c in range(NCORES)]).astype(f)
    kr_p = np.stack([R[c]["kr_p"] for c in range(NCORES)]).astype(f)
    kvl_s = np.concatenate([R[c]["kvl_s"].reshape(16, 8, 256) for c in range(NCORES)]).astype(f)
    kr_s = np.concatenate([R[c]["kr_s"].reshape(16, 8, 64) for c in range(NCORES)]).astype(f)
    memk_p = np.stack([R[c]["memk_p"].reshape(2, 256, 4, 256) for c in range(NCORES)], axis=1).astype(f)
    memv_p = np.stack([R[c]["memv_p"].reshape(2, 256, 4, 256) for c in range(NCORES)], axis=1).astype(f)
    return (y_p, y_s, ssm_re_p, ssm_im_p, ssm_re_s, ssm_im_s, kvl_p, kr_p, kvl_s, kr_s, memk_p, memv_p)
```
